# Optimizing a Trainium2 kernel written in Bass

```python
import math
import jax, jax.numpy as jnp
from jax import lax
import numpy as np

D_MODEL = 1024
BATCH = 8
SEQ = 4096
DEPTH = 1

DN_HEADS = 4
DN_DK = 128
DN_DV = 128
CONV_K = 4
CHUNK = 64
MLA_HEADS = 4
QK_NOPE = 128
QK_ROPE = 64
V_HEAD = 128
Q_LORA = 512
KV_LORA = 256
ROPE_THETA = 10000.0
Q_BLOCK = 128
D_FF = -(-8 * D_MODEL // (3 * 256)) * 256

DN_QK = DN_HEADS * DN_DK
DN_VW = DN_HEADS * DN_DV
DN_CONV_CH = 2 * DN_QK + DN_VW
MLA_Q_DIM = QK_NOPE + QK_ROPE
MLA_VW = MLA_HEADS * V_HEAD
MIX_WIDTH = DN_VW + MLA_VW
SPLIT_Z = DN_CONV_CH
SPLIT_BETA = SPLIT_Z + DN_VW
SPLIT_A = SPLIT_BETA + DN_HEADS
SPLIT_CQ = SPLIT_A + DN_HEADS
SPLIT_CKV = SPLIT_CQ + Q_LORA
SPLIT_KR = SPLIT_CKV + KV_LORA
N_IN = SPLIT_KR + QK_ROPE

DEEPNORM_ALPHA = (2.0 * DEPTH) ** 0.25
DEEPNORM_BETA = (8.0 * DEPTH) ** -0.25

kernel_name = "hybrid_gdn_mla_deepnorm_adaln"


def _layernorm(x, g, b, eps=1e-5):
    xf = x.astype(jnp.float32)
    mu = jnp.mean(xf, axis=-1, keepdims=True)
    var = jnp.mean(jnp.square(xf - mu), axis=-1, keepdims=True)
    y = (xf - mu) * lax.rsqrt(var + eps)
    return (y * g.astype(jnp.float32) + b.astype(jnp.float32)).astype(x.dtype)


def _rmsnorm(x, g, eps=1e-6):
    xf = x.astype(jnp.float32)
    y = xf * lax.rsqrt(jnp.mean(jnp.square(xf), axis=-1, keepdims=True) + eps)
    return (y * g.astype(jnp.float32)).astype(x.dtype)


def _l2norm(x, eps=1e-6):
    xf = x.astype(jnp.float32)
    return xf * lax.rsqrt(jnp.sum(jnp.square(xf), axis=-1, keepdims=True) + eps)


def _rope(t, cos, sin):
    t1, t2 = jnp.split(t.astype(jnp.float32), 2, axis=-1)
    return jnp.concatenate([t1 * cos - t2 * sin, t2 * cos + t1 * sin], axis=-1).astype(t.dtype)


def _gated_delta_rule(q, k, v, g, beta):
    b_, s_, h_, dk = q.shape
    dv = v.shape[-1]
    nc = s_ // CHUNK

    def to_chunks(t):
        return t.reshape(b_, nc, CHUNK, h_, -1).transpose(0, 3, 1, 2, 4)

    q, k, v = to_chunks(q), to_chunks(k), to_chunks(v)
    g = g.reshape(b_, nc, CHUNK, h_).transpose(0, 3, 1, 2)
    beta = beta.reshape(b_, nc, CHUNK, h_).transpose(0, 3, 1, 2)
    gc = jnp.cumsum(g, axis=-1)
    idx = jnp.arange(CHUNK)
    incl = idx[:, None] >= idx[None, :]
    strict = idx[:, None] > idx[None, :]
    diff = gc[..., :, None] - gc[..., None, :]
    decay = jnp.where(incl, jnp.exp(jnp.where(incl, diff, 0.0)), 0.0)
    kb = k * beta[..., None]
    a_mat = jnp.where(strict, jnp.einsum('bhnid,bhnjd->bhnij', kb, k) * decay, 0.0)
    lhs = a_mat + jnp.eye(CHUNK, dtype=a_mat.dtype)
    rhs = jnp.concatenate([kb * jnp.exp(gc)[..., None], v * beta[..., None]], axis=-1)
    wu = lax.linalg.triangular_solve(lhs, rhs, left_side=True, lower=True)
    w_c, u_c = wu[..., :dk], wu[..., dk:]
    attn = jnp.where(incl, jnp.einsum('bhnid,bhnjd->bhnij', q, k) * decay, 0.0)
    qg = q * jnp.exp(gc)[..., None]
    g_last = gc[..., -1]
    kd = k * jnp.exp(g_last[..., None] - gc)[..., None]

    def step(state, inp):
        qg_n, w_n, u_n, attn_n, kd_n, gl_n = inp
        v_new = u_n - jnp.einsum('bhcd,bhde->bhce', w_n, state)
        o = jnp.einsum('bhcd,bhde->bhce', qg_n, state) + jnp.einsum('bhij,bhje->bhie', attn_n, v_new)
        state = state * jnp.exp(gl_n)[..., None, None] + jnp.einsum('bhcd,bhce->bhde', kd_n, v_new)
        return state, o

    xs = (jnp.moveaxis(qg, 2, 0), jnp.moveaxis(w_c, 2, 0), jnp.moveaxis(u_c, 2, 0),
          jnp.moveaxis(attn, 2, 0), jnp.moveaxis(kd, 2, 0), jnp.moveaxis(g_last, 2, 0))
    state0 = jnp.zeros((b_, h_, dk, dv), jnp.float32)
    _, o = lax.scan(step, state0, xs)
    return o.transpose(1, 0, 3, 2, 4).reshape(b_, s_, h_, dv)


def _mla_attention(q_nope, q_rope, k_nope, k_rope, v):
    b_, s_, h_, dn = q_nope.shape
    dv = v.shape[-1]
    nq = s_ // Q_BLOCK
    scale = 1.0 / math.sqrt(QK_NOPE + QK_ROPE)
    qn_b = q_nope.reshape(b_, nq, Q_BLOCK, h_, dn).transpose(1, 0, 2, 3, 4)
    qr_b = q_rope.reshape(b_, nq, Q_BLOCK, h_, -1).transpose(1, 0, 2, 3, 4)
    starts = jnp.arange(nq, dtype=jnp.int32) * Q_BLOCK
    key_idx = jnp.arange(s_, dtype=jnp.int32)

    def block(args):
        qn, qr, start = args
        sc = (jnp.einsum('bqhd,bkhd->bhqk', qn, k_nope)
              + jnp.einsum('bqhd,bkd->bhqk', qr, k_rope)).astype(jnp.float32) * scale
        q_idx = start + jnp.arange(Q_BLOCK, dtype=jnp.int32)
        mask = key_idx[None, :] <= q_idx[:, None]
        p = jax.nn.softmax(jnp.where(mask, sc, -jnp.inf), axis=-1).astype(v.dtype)
        return jnp.einsum('bhqk,bkhd->bqhd', p, v)

    out = lax.map(block, (qn_b, qr_b, starts))
    return out.transpose(1, 0, 2, 3, 4).reshape(b_, s_, h_ * dv)


def _hybrid_mixer(h, cos, sin, w_in, conv_w, a_log, dt_bias, dn_norm_g,
                  q_norm_g, w_uq, kv_norm_g, w_ukv, w_o):
    b_, s_, _ = h.shape
    proj = h @ w_in
    qkv, z, b_raw, a_raw, cq, ckv, kr = jnp.split(
        proj, [SPLIT_Z, SPLIT_BETA, SPLIT_A, SPLIT_CQ, SPLIT_CKV, SPLIT_KR], axis=-1)

    qkv = lax.conv_general_dilated(qkv, conv_w, window_strides=(1,), padding=[(CONV_K - 1, 0)],
                                   dimension_numbers=('NWC', 'WIO', 'NWC'),
                                   feature_group_count=DN_CONV_CH)
    qkv = jax.nn.silu(qkv)
    q_dn, k_dn, v_dn = jnp.split(qkv, [DN_QK, 2 * DN_QK], axis=-1)
    q_dn = _l2norm(q_dn.reshape(b_, s_, DN_HEADS, DN_DK)) * (DN_DK ** -0.5)
    k_dn = _l2norm(k_dn.reshape(b_, s_, DN_HEADS, DN_DK))
    v_dn = v_dn.reshape(b_, s_, DN_HEADS, DN_DV).astype(jnp.float32)
    beta = jax.nn.sigmoid(b_raw.astype(jnp.float32))
    g = -jnp.exp(a_log.astype(jnp.float32)) * jax.nn.softplus(
        a_raw.astype(jnp.float32) + dt_bias.astype(jnp.float32))
    o_dn = _gated_delta_rule(q_dn, k_dn, v_dn, g, beta).astype(h.dtype)
    z = z.reshape(b_, s_, DN_HEADS, DN_DV)
    o_dn = (_rmsnorm(o_dn, dn_norm_g) * jax.nn.silu(z)).reshape(b_, s_, DN_VW)

    q_m = (_rmsnorm(cq, q_norm_g) @ w_uq).reshape(b_, s_, MLA_HEADS, MLA_Q_DIM)
    q_nope, q_rope = q_m[..., :QK_NOPE], q_m[..., QK_NOPE:]
    q_rope = _rope(q_rope, cos[:, :, None, :], sin[:, :, None, :])
    kv = (_rmsnorm(ckv, kv_norm_g) @ w_ukv).reshape(b_, s_, MLA_HEADS, QK_NOPE + V_HEAD)
    k_nope, v_m = kv[..., :QK_NOPE], kv[..., QK_NOPE:]
    k_rope = _rope(kr, cos, sin)
    o_mla = _mla_attention(q_nope, q_rope, k_nope, k_rope, v_m)

    return jnp.concatenate([o_dn, o_mla], axis=-1) @ w_o


def setup_inputs(seed: int = 0) -> dict:
    key = jax.random.key(seed)
    ks = jax.random.split(key, 24)
    f32 = jnp.float32
    nrm = lambda k, shape, s: jax.random.normal(k, shape, f32) * s
    x = jax.random.normal(ks[0], (BATCH, SEQ, D_MODEL), f32)
    c = jax.random.normal(ks[1], (BATCH, D_MODEL), f32)
    positions = (jnp.arange(SEQ, dtype=jnp.int32)[None, :]
                 + jax.random.randint(ks[2], (BATCH, 1), 0, SEQ, dtype=jnp.int32))
    w_ada = nrm(ks[3], (DEPTH, D_MODEL, 6 * D_MODEL), D_MODEL ** -0.5)
    b_ada = nrm(ks[4], (DEPTH, 6 * D_MODEL), 0.02)
    w_in = nrm(ks[5], (DEPTH, D_MODEL, N_IN), D_MODEL ** -0.5)
    conv_w = nrm(ks[6], (DEPTH, CONV_K, 1, DN_CONV_CH), CONV_K ** -0.5)
    a_log = jnp.log(jax.random.uniform(ks[7], (DEPTH, DN_HEADS), f32, 1.0, 16.0))
    dt = jnp.exp(jax.random.uniform(ks[8], (DEPTH, DN_HEADS), f32, math.log(1e-3), math.log(1e-1)))
    dt_bias = dt + jnp.log(-jnp.expm1(-dt))
    dn_norm_g = 1.0 + nrm(ks[9], (DEPTH, DN_DV), 0.02)
    q_norm_g = 1.0 + nrm(ks[10], (DEPTH, Q_LORA), 0.02)
    w_uq = nrm(ks[11], (DEPTH, Q_LORA, MLA_HEADS * MLA_Q_DIM), Q_LORA ** -0.5)
    kv_norm_g = 1.0 + nrm(ks[12], (DEPTH, KV_LORA), 0.02)
    w_ukv = nrm(ks[13], (DEPTH, KV_LORA, MLA_HEADS * (QK_NOPE + V_HEAD)), KV_LORA ** -0.5)
    w_o = nrm(ks[14], (DEPTH, MIX_WIDTH, D_MODEL), MIX_WIDTH ** -0.5 * DEEPNORM_BETA)
    ln1_g = 1.0 + nrm(ks[15], (DEPTH, D_MODEL), 0.02)
    ln1_b = nrm(ks[16], (DEPTH, D_MODEL), 0.02)
    w_gate = nrm(ks[17], (DEPTH, D_MODEL, D_FF), D_MODEL ** -0.5)
    w_up = nrm(ks[18], (DEPTH, D_MODEL, D_FF), D_MODEL ** -0.5)
    w_down = nrm(ks[19], (DEPTH, D_FF, D_MODEL), D_FF ** -0.5 * DEEPNORM_BETA)
    ln2_g = 1.0 + nrm(ks[20], (DEPTH, D_MODEL), 0.02)
    ln2_b = nrm(ks[21], (DEPTH, D_MODEL), 0.02)
    return {"x": x, "c": c, "positions": positions, "w_ada": w_ada, "b_ada": b_ada,
            "w_in": w_in, "conv_w": conv_w, "a_log": a_log, "dt_bias": dt_bias,
            "dn_norm_g": dn_norm_g, "q_norm_g": q_norm_g, "w_uq": w_uq,
            "kv_norm_g": kv_norm_g, "w_ukv": w_ukv, "w_o": w_o,
            "ln1_g": ln1_g, "ln1_b": ln1_b, "w_gate": w_gate, "w_up": w_up,
            "w_down": w_down, "ln2_g": ln2_g, "ln2_b": ln2_b}


def reference(x, c, positions, w_ada, b_ada, w_in, conv_w, a_log, dt_bias, dn_norm_g,
              q_norm_g, w_uq, kv_norm_g, w_ukv, w_o, ln1_g, ln1_b, w_gate, w_up,
              w_down, ln2_g, ln2_b):
    inv_freq = 1.0 / (ROPE_THETA ** (jnp.arange(0, QK_ROPE, 2, dtype=jnp.float32) / QK_ROPE))
    ang = positions.astype(jnp.float32)[..., None] * inv_freq
    cos, sin = jnp.cos(ang), jnp.sin(ang)
    c_act = jax.nn.silu(c)
    for l in range(DEPTH):
        mod = (c_act @ w_ada[l] + b_ada[l])[:, None, :]
        sh_m, sc_m, gt_m, sh_f, sc_f, gt_f = jnp.split(mod, 6, axis=-1)
        h = x * (1.0 + sc_m) + sh_m
        mix = _hybrid_mixer(h, cos, sin, w_in[l], conv_w[l], a_log[l], dt_bias[l], dn_norm_g[l],
                            q_norm_g[l], w_uq[l], kv_norm_g[l], w_ukv[l], w_o[l])
        x = _layernorm(DEEPNORM_ALPHA * x + gt_m * mix, ln1_g[l], ln1_b[l])
        h = x * (1.0 + sc_f) + sh_f
        ff = (jax.nn.silu(h @ w_gate[l]) * (h @ w_up[l])) @ w_down[l]
        x = _layernorm(DEEPNORM_ALPHA * x + gt_f * ff, ln2_g[l], ln2_b[l])
    return x
```

```python
import contextlib
import math
import numpy as np
import concourse.bass as bass
import concourse.mybir as mybir
from concourse.bass_utils import run_bass_kernel_spmd

F32 = mybir.dt.float32
BF16 = mybir.dt.bfloat16
I32 = mybir.dt.int32
AF = mybir.ActivationFunctionType
ALU = mybir.AluOpType

ENGS = ("pe", "act", "dve", "pool", "sp")
NDSEM = 8
EPOCH = 20000

D = 1024
SEQ = 4096
NT = SEQ // 128
NB = SEQ // 512
DFF = 2816
NFC = DFF // 128
ALPHA = 2.0 ** 0.25
PI = math.pi


class Sched:
    def __init__(self, nc, tag):
        self.nc = nc
        self.tag = tag
        self.ops = {e: [] for e in ENGS}
        self.last_w = {}
        self.rd_c = {}
        self.rd_d = {}
        self.ndma = {e: 0 for e in ENGS}

    def op(self, eng, fn, r=(), w=(), dma=False):
        idx = len(self.ops[eng])
        deps = set()
        for t in r:
            lw = self.last_w.get(t)
            if lw is not None:
                deps.add(lw)
        for t in w:
            lw = self.last_w.get(t)
            if lw is not None:
                deps.add(lw)
            for re_, ri in self.rd_c.get(t, {}).items():
                deps.add((re_, ri))
            for x in self.rd_d.get(t, ()):
                deps.add(x)
        o = dict(fn=fn, deps=deps, dma=dma, signal=dma)
        if dma:
            o["dma_i"] = self.ndma[eng]
            self.ndma[eng] += 1
        self.ops[eng].append(o)
        me = (eng, idx)
        for t in r:
            if dma:
                self.rd_d.setdefault(t, []).append(me)
            else:
                self.rd_c.setdefault(t, {})[eng] = idx
        for t in w:
            self.last_w[t] = me
            self.rd_c[t] = {}
            self.rd_d[t] = []
        return me

    def dma(self, q, out, in_, r=(), w=()):
        return self.op(q, lambda e: e.dma_start(out=out, in_=in_), r=r, w=w, dma=True)

    def emit(self):
        nc = self.nc
        ops = self.ops
        for eng in ENGS:
            for o in ops[eng]:
                nd = set()
                for (de, di) in o["deps"]:
                    d = ops[de][di]
                    if de == eng and eng == "pe" and not d["dma"] and not o["dma"]:
                        continue
                    nd.add((de, di))
                o["deps"] = nd
                for (de, di) in nd:
                    ops[de][di]["signal"] = True
        nsig = {}
        for eng in ENGS:
            c = 0
            for o in ops[eng]:
                if o["dma"]:
                    i = o["dma_i"]
                    o["h"] = (("d", eng, i % NDSEM), 16 * (i // NDSEM + 1))
                elif o["signal"]:
                    o["h"] = (("c", eng, c // EPOCH), c % EPOCH + 1)
                    c += 1
            nsig[eng] = c
        with contextlib.ExitStack() as st:
            sems = {}
            for eng in ENGS:
                for ep in range((nsig[eng] + EPOCH - 1) // EPOCH):
                    sems[("c", eng, ep)] = nc.alloc_semaphore(name=f"{self.tag}c_{eng}_{ep}")
                for k in range(min(NDSEM, self.ndma[eng])):
                    sems[("d", eng, k)] = nc.alloc_semaphore(name=f"{self.tag}d_{eng}_{k}")
            self.sem_handles = list(sems.values())
            block = st.enter_context(nc.Block())

            def run(eng, e):
                known = {}
                for o in ops[eng]:
                    waits = {}
                    for (de, di) in o["deps"]:
                        sk, v = ops[de][di]["h"]
                        if waits.get(sk, 0) < v:
                            waits[sk] = v
                    if o["dma"] and o["dma_i"] >= NDSEM:
                        sk = ("d", eng, o["dma_i"] % NDSEM)
                        v = 16 * (o["dma_i"] // NDSEM)
                        if waits.get(sk, 0) < v:
                            waits[sk] = v
                    for sk, v in waits.items():
                        if known.get(sk, 0) >= v:
                            continue
                        e.wait_ge(sems[sk], v)
                        known[sk] = v
                    ins = o["fn"](e)
                    if o["dma"]:
                        ins.then_inc(sems[o["h"][0]], 16)
                    elif o["signal"]:
                        ins.then_inc(sems[o["h"][0]], 1)
                n = self.ndma[eng]
                for k in range(min(NDSEM, n)):
                    cnt = (n - 1 - k) // NDSEM + 1
                    if known.get(("d", eng, k), 0) < 16 * cnt:
                        e.wait_ge(sems[("d", eng, k)], 16 * cnt)

            @block.sync
            def _(e):
                run("sp", e)

            @block.tensor
            def _(e):
                run("pe", e)

            @block.scalar
            def _(e):
                run("act", e)

            @block.vector
            def _(e):
                run("dve", e)

            @block.gpsimd
            def _(e):
                run("pool", e)


class Banks:
    def __init__(self, tiles):
        self.tiles = tiles
        self.free = list(range(len(tiles)))

    def get(self):
        k = self.free.pop(0)
        return k

    def put(self, k):
        self.free.append(k)


class Pass:
    def __init__(self, nc, tag):
        self.nc = nc
        self.tag = tag
        self.S = Sched(nc, tag)
        self.st = contextlib.ExitStack()
        self.n = 0

    def sb(self, shape, dt, name=None):
        self.n += 1
        return self.st.enter_context(self.nc.sbuf_tensor(f"{self.tag}_{name or 't'}{self.n}", list(shape), dt))

    def psum_banks(self, nf=6):
        self.pf = [self.st.enter_context(self.nc.psum_tensor(f"{self.tag}_pf{k}", [128, 512], F32)) for k in range(nf)]
        self.pbs = [self.st.enter_context(self.nc.psum_tensor(f"{self.tag}_pb{k}", [128, 1024], BF16)) for k in range(8 - nf)]
        self.banks = Banks(self.pf)
        self.pbi = 0

    def bget(self):
        k = self.banks.get()
        return k, self.pf[k], ("pf", k)

    def bput(self, k):
        self.banks.put(k)

    def pbhalf(self):
        self.pbi = (self.pbi + 1) % len(self.pbs)
        return self.pbs[self.pbi][:, 0:512], ("pb", self.pbi)

    def mm(self, out, lhsT, rhs, start, stop, r, w):
        self.S.op("pe", lambda e: e.matmul(out, lhsT=lhsT, rhs=rhs, start=start, stop=stop), r=r, w=w)

    def tr(self, out, in_, ident, r, w):
        self.S.op("pe", lambda e: e.transpose(out=out, in_=in_, identity=ident), r=r, w=w)

    def act(self, out, in_, func, r, w, bias=None, scale=None, accum_out=None):
        kw = {}
        if bias is not None:
            kw["bias"] = bias
        if scale is not None:
            kw["scale"] = scale
        if accum_out is not None:
            kw["accum_out"] = accum_out
        self.S.op("act", lambda e: e.activation(out=out, in_=in_, func=func, **kw), r=r, w=w)

    def tt(self, eng, out, in0, in1, op, r, w):
        self.S.op(eng, lambda e: e.tensor_tensor(out=out, in0=in0, in1=in1, op=op), r=r, w=w)

    def ts(self, eng, out, in0, s1, op0, r, w, s2=None, op1=None):
        if op1 is None:
            self.S.op(eng, lambda e: e.tensor_scalar(out=out, in0=in0, scalar1=s1, scalar2=None, op0=op0), r=r, w=w)
        else:
            self.S.op(eng, lambda e: e.tensor_scalar(out=out, in0=in0, scalar1=s1, scalar2=s2, op0=op0, op1=op1), r=r, w=w)

    def stt(self, eng, out, in0, scalar, in1, op0, op1, r, w):
        self.S.op(eng, lambda e: e.scalar_tensor_tensor(out=out, in0=in0, scalar=scalar, in1=in1, op0=op0, op1=op1), r=r, w=w)

    def cp(self, eng, out, in_, r, w):
        if eng == "act":
            self.S.op("act", lambda e: e.copy(out=out, in_=in_), r=r, w=w)
        else:
            self.S.op(eng, lambda e: e.tensor_copy(out=out, in_=in_), r=r, w=w)

    def memset(self, eng, ap, v, w):
        self.S.op(eng, lambda e: e.memset(ap, v), w=w)

    def recip(self, out, in_, r, w):
        self.S.op("dve", lambda e: e.reciprocal(out=out, in_=in_), r=r, w=w)

    def dma(self, q, out, in_, r=(), w=()):
        self.S.dma(q, out, in_, r=r, w=w)

    def finish(self):
        self.S.emit()
        self.st.close()
        self.nc.all_engine_barrier()
        self.nc.clear_and_free_semaphores(self.S.sem_handles)
        self.nc.all_engine_barrier()


def bc(ap, shape):
    return ap.to_broadcast(list(shape))


C_ID, C_U, C_LT, C_SLT, C_BO, C_UT, C_ONE, C_M16, C_D32, C_D64, C_MISC = [i * 128 for i in range(11)]
NCONST = 11 * 128 + 16


def host_consts():
    c = np.zeros((128, NCONST), np.float32)
    i = np.arange(128)
    same = (i[:, None] // 64) == (i[None, :] // 64)
    c[:, C_ID:C_ID + 128] = np.eye(128)
    c[:, C_U:C_U + 128] = same & (i[:, None] <= i[None, :])
    c[:, C_LT:C_LT + 128] = same & (i[:, None] >= i[None, :])
    c[:, C_SLT:C_SLT + 128] = same & (i[:, None] > i[None, :])
    c[:, C_BO:C_BO + 128] = same
    c[:, C_UT:C_UT + 128] = (i[:, None] <= i[None, :])
    c[:, C_ONE:C_ONE + 128] = 1.0
    m16 = (i[:, None] // 16) == (i[None, :] // 16)
    m32 = (i[:, None] // 32) == (i[None, :] // 32)
    c[:, C_M16:C_M16 + 128] = m16
    c[:, C_D32:C_D32 + 128] = m32 & ~m16
    c[:, C_D64:C_D64 + 128] = same & ~m32
    inv_freq = (1.0 / (np.float32(10000.0) ** (np.arange(0, 64, 2, dtype=np.float32) / np.float32(64)))).astype(np.float32)
    c[0:32, C_MISC + 0] = inv_freq
    c[32:64, C_MISC + 0] = inv_freq
    c[0:32, C_MISC + 1] = -1.0
    c[32:64, C_MISC + 1] = 1.0
    c[0:64, C_MISC + 2] = 1.0
    c[64:128, C_MISC + 3] = 1.0
    return c


WCH = {}
_n = 0
for _name, _cnt in (("in", 6), ("uq", 2), ("ukv", 2), ("o", 2), ("gate", 6), ("up", 6), ("down", 6)):
    WCH[_name] = (_n, _cnt)
    _n += _cnt
NWCH = _n


def build_program(parts=("mix", "gdn", "mla", "ffn"), dbg=False, stop=None, p0skip=()):
    nc = bass.Bass("TRN2", target_bir_lowering=False)

    def din(name, shape, dt=F32):
        return nc.dram_tensor(name, list(shape), dt, kind="ExternalInput").ap()

    x = din("x", [SEQ, D])
    cT = din("cT", [128, 8])
    pos = din("pos", [1, SEQ], I32)
    w_ada = din("w_ada", [D, 6 * D])
    b_ada = din("b_ada", [1, 6 * D])
    w_in = din("w_in", [D, 2888])
    cw = din("cw", [128, 12, 4])
    a_log = din("a_log", [1, 4])
    dt_bias = din("dt_bias", [1, 4])
    dn_g = din("dn_g", [1, 128])
    qn_g = din("qn_g", [128, 4])
    kvn_g = din("kvn_g", [128, 2])
    w_uq = din("w_uq", [512, 768])
    w_ukv = din("w_ukv", [256, 1024])
    w_o = din("w_o", [D, D])
    ln1_g = din("ln1_g", [1, D])
    ln1_b = din("ln1_b", [1, D])
    w_gate = din("w_gate", [D, DFF])
    w_up = din("w_up", [D, DFF])
    w_down = din("w_down", [DFF, D])
    ln2_g = din("ln2_g", [1, D])
    ln2_b = din("ln2_b", [1, D])
    consts = din("consts", [128, NCONST])
    out = nc.dram_tensor("out", [SEQ, D], F32, kind="ExternalOutput").ap()

    def dscr(name, shape, dt):
        return nc.dram_tensor(name, list(shape), dt, kind="Internal").ap()

    wsc = dscr("wsc", [NWCH, 128, 8 * 512], BF16)
    modsc = dscr("modsc", [128, 6 * D], F32)
    hTs = dscr("hTs", [NB, 128, 8 * 512], BF16)
    odn = dscr("odn", [NB, 128, 4 * 512], BF16)
    x1s = dscr("x1s", [SEQ, D], F32)
    dbg_t = nc.dram_tensor("dbg", [128, 4096], F32, kind="ExternalOutput").ap() if dbg else None

    BG_FFN = ("mix" in parts) and ("gdn" in parts)

    P = Pass(nc, "p0")
    P.psum_banks()
    cst = P.sb([128, NCONST], F32)
    P.dma("sp", cst[:], consts, w=["cst"])
    stg = [P.sb([128, 8, 512], F32) for _ in range(2)]
    stb = [P.sb([128, 8, 512], BF16) for _ in range(2)]
    qg_t = P.sb([128, 4], F32)
    kvg_t = P.sb([128, 2], F32)
    P.dma("sp", qg_t[:], qn_g, w=["qg"])
    P.dma("sp", kvg_t[:], kvn_g, w=["kvg"])
    cnt = [0]

    def prep_chunk(ch, pieces, kc_n, scale=None, sc_tok=None):
        if scale is not None and "scaled" in p0skip:
            return
        i = cnt[0] % 2
        cnt[0] += 1
        s, b = stg[i], stb[i]
        tot = sum(p[1].shape[1] for p in pieces)
        if tot < 512:
            P.memset("pool", s[:, 0:kc_n, tot:512], 0.0, w=[("stg", i)])
        for k, (dc, src) in enumerate(pieces):
            ncol = src.shape[1]
            q = "sp" if k % 2 == 0 else "act"
            P.dma(q, s[:, 0:kc_n, dc:dc + ncol], src.rearrange("(k p) n -> p k n", p=128), w=[("stg", i)])
        for kc in range(kc_n):
            eng = ("act", "dve", "pool")[kc % 3] if scale is None else "act"
            if scale is None:
                P.cp(eng, b[:, kc, :], s[:, kc, :], r=[("stg", i)], w=[("stb", i)])
            else:
                P.act(b[:, kc, :], s[:, kc, :], AF.Copy, scale=scale[:, kc:kc + 1], r=[("stg", i), sc_tok], w=[("stb", i)])
        P.dma("pool", wsc[ch].rearrange("p (k n) -> p k n", k=8)[:, 0:kc_n, :], b[:, 0:kc_n, :], r=[("stb", i)], w=[("wsc", ch)])

    if "w" not in p0skip:
        c0 = WCH["in"][0]
        prep_chunk(c0 + 0, [(0, w_in[:, 0:512])], 8)
        prep_chunk(c0 + 1, [(0, w_in[:, 512:1024])], 8)
        prep_chunk(c0 + 2, [(0, w_in[:, 1024:1536])], 8)
        prep_chunk(c0 + 3, [(0, w_in[:, 1536:2048])], 8)
        prep_chunk(c0 + 4, [(0, w_in[:, 2056:2568])], 8)
        if "ch5" not in p0skip:
            prep_chunk(c0 + 5, [(0, w_in[:, 2568:2824]), (256, w_in[:, 2824:2888]), (320, w_in[:, 2856:2888]),
                                (352, w_in[:, 2824:2856]), (384, w_in[:, 2048:2056])], 8)
        c0 = WCH["uq"][0]
        prep_chunk(c0 + 0, [(h * 128, w_uq[:, h * 192:h * 192 + 128]) for h in range(4)], 4, scale=qg_t, sc_tok="qg")
        prep_chunk(c0 + 1, [(h * 64, w_uq[:, h * 192 + 128:h * 192 + 192]) for h in range(4)]
                   + [(256 + h * 64, w_uq[:, h * 192 + 160:h * 192 + 192]) for h in range(4)]
                   + [(256 + h * 64 + 32, w_uq[:, h * 192 + 128:h * 192 + 160]) for h in range(4)], 4, scale=qg_t, sc_tok="qg")
        c0 = WCH["ukv"][0]
        prep_chunk(c0 + 0, [(h * 128, w_ukv[:, h * 256:h * 256 + 128]) for h in range(4)], 2, scale=kvg_t, sc_tok="kvg")
        prep_chunk(c0 + 1, [(h * 128, w_ukv[:, h * 256 + 128:h * 256 + 256]) for h in range(4)], 2, scale=kvg_t, sc_tok="kvg")
        c0 = WCH["o"][0]
        for hf in range(2):
            prep_chunk(c0 + hf, [(0, w_o[:, hf * 512:(hf + 1) * 512])], 8)
        if not BG_FFN:
            for nm, wt in (("gate", w_gate), ("up", w_up)):
                c0 = WCH[nm][0]
                for c in range(6):
                    prep_chunk(c0 + c, [(0, wt[:, c * 512:min((c + 1) * 512, DFF)])], 8)
            c0 = WCH["down"][0]
            for g in range(3):
                nk = 8 if g < 2 else 6
                for hf in range(2):
                    prep_chunk(c0 + g * 2 + hf, [(0, w_down[g * 1024:g * 1024 + nk * 128, hf * 512:(hf + 1) * 512])], nk)

    if "mod" not in p0skip:
        cT_t = P.sb([128, 8], F32)
        cact = P.sb([128, 8], F32)
        cb = P.sb([128, 8, 128], F32)
        P.dma("sp", cT_t[:], cT, w=["cT"])
        P.act(cact[:], cT_t[:], AF.Silu, r=["cT"], w=["cact"])
        P.cp("dve", cb[:], bc(cact[:].unsqueeze(2), [128, 8, 128]), r=["cact"], w=["cb"])
        wst = [P.sb([128, 8, 512], F32) for _ in range(2)]
        bst = [P.sb([128, 512], F32) for _ in range(2)]
        mo = [P.sb([128, 512], F32) for _ in range(2)]
        for nb in range(12):
            i = nb % 2
            P.dma("sp", wst[i][:], w_ada[:, nb * 512:(nb + 1) * 512].rearrange("(k p) n -> p k n", p=128), w=[("wst", i)])
            P.dma("act", bst[i][:], bc(b_ada[0:1, nb * 512:(nb + 1) * 512], [128, 512]), w=[("bst", i)])
            k, pt, ptok = P.bget()
            for kc in range(8):
                P.mm(pt[:], cb[:, kc, :], wst[i][:, kc, :], kc == 0, kc == 7, r=["cb", ("wst", i)], w=[ptok])
            P.tt("dve", mo[i][:], pt[:], bst[i][:], ALU.add, r=[ptok, ("bst", i)], w=[("mo", i)])
            P.bput(k)
            if nb in (2, 3, 8, 9):
                P.ts("dve", mo[i][:], mo[i][:], 1.0, ALU.add, r=[("mo", i)], w=[("mo", i)])
            P.dma("pool", modsc[:, nb * 512:(nb + 1) * 512], mo[i][:], r=[("mo", i)], w=[("modsc", nb)])
    P.finish()
    if stop == "p0":
        return nc

    class Ring:
        def __init__(self, P, n=3):
            self.P = P
            self.bufs = [P.sb([128, 8, 512], BF16) for _ in range(n)]
            self.i = 0
            self.n = n

        def load(self, ch, q="sp", nk=8):
            i = self.i % self.n
            self.i += 1
            self.P.dma(q, self.bufs[i][:, 0:nk, :], wsc[ch].rearrange("p (k n) -> p k n", k=8)[:, 0:nk, :], w=[("ring", i)])
            return self.bufs[i], ("ring", i)

    def load_consts(P):
        cst = P.sb([128, NCONST], F32)
        P.dma("sp", cst[:], consts, w=["cst"])
        idb = P.sb([128, 128], BF16)
        P.cp("dve", idb[:], cst[:, C_ID:C_ID + 128], r=["cst"], w=["idb"])
        return cst, idb

    def make_hT(P, xt, xtok, j, s1, sh, modtok, idb, hT, hTtok, tmpf, hbf, eng="dve"):
        P.tt(eng, tmpf[:], xt, s1[:], ALU.mult, r=[xtok, modtok], w=["tmpf"])
        P.tt(eng, hbf[:], tmpf[:], sh[:], ALU.add, r=["tmpf", modtok], w=["hbf"])
        for half in range(2):
            kb_, pb_, phtok = P.bget()
            ph = pb_[:, 0:256].bitcast(BF16)
            for kk in range(4):
                kc = half * 4 + kk
                P.tr(ph[:, kk * 128:(kk + 1) * 128], hbf[:, kc * 128:(kc + 1) * 128], idb[:], r=["hbf", "idb"], w=[phtok])
            P.cp("act", hT[:, half * 4:half * 4 + 4, j * 128:(j + 1) * 128],
                 ph.rearrange("p (k n) -> p k n", k=4), r=[phtok], w=[hTtok])
            P.bput(kb_)

    def ln_stats(P, y, ytok, junk, st, sttok):
        P.S.op("dve", lambda e: e.tensor_scalar(out=junk[:], in0=y, scalar1=1.0 / D, scalar2=0.0, op0=ALU.mult, op1=ALU.add, accum_out=st[:, 2:3]),
               r=[ytok], w=["junk", sttok])
        P.S.op("dve", lambda e: e.scalar_tensor_tensor(out=junk[:], in0=y, scalar=1.0 / D, in1=y, op0=ALU.mult, op1=ALU.mult,
                                                       accum_out=st[:, 1:2]), r=[ytok, sttok], w=["junk", sttok])
        P.tt("dve", st[:, 3:4], st[:, 2:3], st[:, 2:3], ALU.mult, r=[sttok], w=[sttok])
        P.tt("dve", st[:, 4:5], st[:, 1:2], st[:, 3:4], ALU.subtract, r=[sttok], w=[sttok])
        P.act(st[:, 5:6], st[:, 4:5], AF.Ln, bias=1e-5, r=[sttok], w=[sttok])
        P.act(st[:, 6:7], st[:, 5:6], AF.Exp, scale=-0.5, r=[sttok], w=[sttok])

    def ln_apply(P, y, ytok, g_b, b_b, gbtok, outt, outtok, junk, st, sttok):
        P.ts("dve", junk[:], y, st[:, 2:3], ALU.subtract, s2=st[:, 6:7], op1=ALU.mult, r=[ytok, sttok], w=["junk"])
        P.tt("pool", junk[:], junk[:], g_b[:], ALU.mult, r=["junk", gbtok], w=["junk"])
        P.tt("dve", outt, junk[:], b_b[:], ALU.add, r=["junk", gbtok], w=[outtok])

    SCALE = 1.0 / math.sqrt(192.0)
    C1 = 6.28125
    C2 = 2.0 * PI - 6.28125

    def build_p1a():
        GDN = "gdn" in parts
        NSET = 2
        P = Pass(nc, "p1a")
        P.psum_banks(8)
        cst, idb = load_consts(P)
        idf = cst[:, C_ID:C_ID + 128]
        Umat = cst[:, C_U:C_U + 128]
        BOm = cst[:, C_BO:C_BO + 128]
        onesf = cst[:, C_ONE:C_ONE + 128]
        CI = cst[:, C_MISC + 2:C_MISC + 4]
        s1m = P.sb([128, D], F32)
        shm = P.sb([128, D], F32)
        P.dma("sp", shm[:], modsc[:, 0:D], w=["mod"])
        P.dma("sp", s1m[:], modsc[:, D:2 * D], w=["mod"])
        xt = [P.sb([128, D], F32) for _ in range(2)]
        hb = P.sb([128, 8, 512], BF16)
        tmpf = P.sb([128, D], F32)
        hbf = P.sb([128, D], BF16)
        zt = P.sb([128, 4 * 512], BF16)
        mixd = zt[:].rearrange("p (h n) -> p h n", h=4)
        if not GDN:
            P.memset("pool", zt[:], 0.0, w=["mixd"])
        else:
            ring = Ring(P, 2)
            I3 = P.sb([128, 4, 128], F32)
            LT3 = P.sb([128, 4, 128], F32)
            SLT3 = P.sb([128, 4, 128], F32)
            for h in range(4):
                P.cp("pool", I3[:, h, :], idf, r=["cst"], w=["c3"])
                P.cp("pool", LT3[:, h, :], cst[:, C_LT:C_LT + 128], r=["cst"], w=["c3"])
                P.cp("pool", SLT3[:, h, :], cst[:, C_SLT:C_SLT + 128], r=["cst"], w=["c3"])
            mk16 = P.sb([128, 128], BF16)
            mk32 = P.sb([128, 128], BF16)
            mk64 = P.sb([128, 128], BF16)
            P.cp("pool", mk16[:], cst[:, C_M16:C_M16 + 128], r=["cst"], w=["mk"])
            P.cp("pool", mk32[:], cst[:, C_D32:C_D32 + 128], r=["cst"], w=["mk"])
            P.cp("pool", mk64[:], cst[:, C_D64:C_D64 + 128], r=["cst"], w=["mk"])
            cwt = P.sb([128, 12, 4], F32)
            P.dma("act", cwt[:], cw, w=["cwt"])
            negA = P.sb([128, 4], F32)
            dtb = P.sb([128, 4], F32)
            dng4 = P.sb([128, 4, 128], F32)
            P.dma("act", negA[:], bc(a_log[0:1, :], [128, 4]), w=["negA"])
            P.dma("act", dtb[:], bc(dt_bias[0:1, :], [128, 4]), w=["dtb"])
            for h in range(4):
                P.dma("act", dng4[:, h, :], bc(dn_g[0:1, :], [128, 128]), w=["dng4"])
            P.act(negA[:], negA[:], AF.Exp, r=["negA"], w=["negA"])
            P.ts("dve", negA[:], negA[:], -1.0, ALU.mult, r=["negA"], w=["negA"])
            Sst = [P.sb([128, 4, 128], F32) for _ in range(2)]
            Sbf = P.sb([128, 4, 128], BF16)
            P.memset("dve", Sst[0][:], 0.0, w=[("S", 0)])
            P.memset("dve", Sbf[:], 0.0, w=["Sbf"])
            prec = P.sb([128, 12, 3], F32)
            P.memset("dve", prec[:], 0.0, w=[("prec", ci) for ci in range(12)])
            pre4 = P.sb([128, 4, 515], F32)
            qkvs2 = [P.sb([128, 12, 512], F32) for _ in range(2)]
            sz2 = [P.sb([128, 4, 512], BF16) for _ in range(2)]
            cur = dict(tb=0)
            ba = P.sb([128, 4, 8], F32)
            beta2 = [P.sb([128, 4, 4], F32) for _ in range(2)]
            gg2 = [P.sb([128, 4, 4], F32) for _ in range(2)]
            sq8 = P.sb([128, 8, 512], BF16)
            lnb = [P.sb([128, 512], F32) for _ in range(2)]
            ones_bf = P.sb([128, 128], BF16)
            P.cp("dve", ones_bf[:], onesf, r=["cst"], w=["ones_bf"])
            jk = P.sb([128, 128], F32)
            f3 = lambda: P.sb([128, 4, 128], F32)
            b3 = lambda: P.sb([128, 4, 128], BF16)
            sets = []
            for k in range(NSET):
                bs = dict(A1=f3(), A2=f3(), A3=f3(), A4=f3(),
                          XA=b3(), XB=b3(), YA=b3(), YB=b3(), Xo1=b3(), Xo2=b3(), Yo1=b3(), Yo2=b3(), Pb=b3(), Qb=b3(),
                          RHSw=b3(), RHSu=b3(), qgT=b3(), kd=b3(), attT=b3(), H4=b3(), vnew=b3(),
                          sc=P.sb([128, 32], F32), egl=P.sb([128, 8], F32), oss=P.sb([128, 8], F32), k=k)
                sets.append(bs)
            scan_state = dict(c=0)

        def v3(t):
            return t[:].rearrange("p (h n) -> p h n", h=4)

        def col3(ap):
            return bc(ap.unsqueeze(2), [128, 4, 128])

        def prep_gen(j, bs):
            bp = cur["tb"] % 2
            qkvs, beta, gg = qkvs2[bp], beta2[bp], gg2[bp]
            k_ = bs["k"]
            T = lambda n: (n, k_)
            ts_ = slice(j * 128, (j + 1) * 128)
            G, dec, egcb, mb = bs["A1"], bs["A2"], bs["A3"], bs["A4"]
            tX, tA = bs["A1"], bs["A3"]
            RHSw, RHSu = bs["RHSw"], bs["RHSu"]
            qgT, kd, attT, attb, wT = bs["qgT"], bs["kd"], bs["attT"], bs["H4"], bs["H4"]
            uu = bs["A3"]
            sc, egl = bs["sc"], bs["egl"]
            P.cp("dve", G[:], col3(gg[:, j, :]), r=[("gg", bp)], w=[T("A1")])
            ksm, psm, psmtok = P.bget()
            P.mm(psm[:, 0:4], Umat, gg[:, j, :], True, True, r=["cst", ("gg", bp)], w=[psmtok])
            P.mm(psm[:, 4:8], BOm, gg[:, j, :], True, True, r=["cst", ("gg", bp)], w=[psmtok])
            for h in range(4):
                P.mm(psm[:, 8 + 2 * h:10 + 2 * h], G[:, h, :], CI, True, True, r=[T("A1"), "cst"], w=[psmtok])
            kgc, pgc, pgctok = P.bget()
            for h in range(4):
                P.mm(pgc[:, h * 128:(h + 1) * 128], G[:, h, :], Umat, True, True, r=[T("A1"), "cst"], w=[pgctok])
            P.cp("dve", sc[:, 0:4], psm[:, 0:4], r=[psmtok], w=[T("sc0")])
            P.act(sc[:, 4:8], psm[:, 0:4], AF.Exp, r=[psmtok], w=[T("sc1")])
            P.tt("dve", sc[:, 8:12], psm[:, 4:8], sc[:, 0:4], ALU.subtract, r=[psmtok, T("sc0")], w=[T("sc2")])
            P.act(sc[:, 12:16], sc[:, 8:12], AF.Exp, r=[T("sc2")], w=[T("sc3")])
            P.tt("dve", sc[:, 16:20], beta[:, j, :], sc[:, 4:8], ALU.mult, r=[("beta", bp), T("sc1")], w=[T("sc4")])
            P.ts("dve", sc[:, 20:24], beta[:, j, :], -1.0, ALU.mult, r=[("beta", bp)], w=[T("sc5")])
            P.act(egl[:], psm[:, 8:16], AF.Exp, r=[psmtok], w=[T("egl")])
            P.bput(ksm)
            yield
            P.tt("dve", dec[:], col3(sc[:, 0:4]), v3(pgc), ALU.subtract, r=[T("sc0"), pgctok], w=[T("A2")])
            P.ts("dve", dec[:], dec[:], 0.0, ALU.min, r=[T("A2")], w=[T("A2")])
            P.act(dec[:], dec[:], AF.Exp, r=[T("A2")], w=[T("A2")])
            P.act(egcb[:], v3(pgc), AF.Exp, r=[pgctok], w=[T("A3")])
            P.bput(kgc)
            yield
            yield
            kkt, pkt, pkttok = P.bget()
            kvt, pvt, pvttok = P.bget()
            for h in range(4):
                P.tr(pkt[:, h * 128:(h + 1) * 128], qkvs[:, 4 + h, ts_], idf, r=[("qkvs", bp, 4 + h), "cst"], w=[pkttok])
            for h in range(4):
                P.tr(pvt[:, h * 128:(h + 1) * 128], qkvs[:, 8 + h, ts_], idf, r=[("qkvs", bp, 8 + h), "cst"], w=[pvttok])
            kkk, pkk, pkktok = P.bget()
            kqk, pqk, pqktok = P.bget()
            for h in range(4):
                P.mm(pkk[:, h * 128:(h + 1) * 128], qkvs[:, 4 + h, ts_], qkvs[:, 4 + h, ts_], True, True, r=[("qkvs", bp, 4 + h)], w=[pkktok])
            for h in range(4):
                P.mm(pqk[:, h * 128:(h + 1) * 128], qkvs[:, h, ts_], qkvs[:, 4 + h, ts_], True, True, r=[("qkvs", bp, h), ("qkvs", bp, 4 + h)], w=[pqktok])
            P.tt("dve", qgT[:], qkvs[:, 0:4, ts_], egcb[:], ALU.mult, r=[("qkvs", bp, ci) for ci in range(4)] + [T("A3")], w=[T("qgT")])
            for h in range(4):
                hs = slice(h * 128, (h + 1) * 128)
                P.act(RHSw[:, h, :], pkt[:, hs], AF.Copy, scale=sc[:, 16 + h:17 + h], r=[pkttok, T("sc4")], w=[T("RHSw")])
                P.act(kd[:, h, :], pkt[:, hs], AF.Copy, scale=sc[:, 12 + h:13 + h], r=[pkttok, T("sc3")], w=[T("kd")])
                P.act(RHSu[:, h, :], pvt[:, hs], AF.Copy, scale=beta[:, j, h:h + 1], r=[pvttok, ("beta", bp)], w=[T("RHSu")])
            P.bput(kkt)
            P.bput(kvt)
            yield
            P.tt("pool", mb[:], SLT3[:], col3(sc[:, 20:24]), ALU.mult, r=["c3", T("sc5")], w=[T("A4")])
            P.tt("dve", tX[:], v3(pkk), dec[:], ALU.mult, r=[pkktok, T("A2"), T("A1")], w=[T("A1")])
            P.bput(kkk)
            yield
            P.tt("dve", tA[:], v3(pqk), dec[:], ALU.mult, r=[pqktok, T("A2"), T("A3"), T("qgT")], w=[T("A3")])
            P.bput(kqk)
            yield
            X0b = bs["XB"]
            Xs = [bs["XA"], bs["XB"]]
            Ys = [bs["YA"], bs["YB"]]
            xs_t = [T("XA"), T("XB")]
            ys_t = [T("YA"), T("YB")]
            Xo1, Xo2, Yo1, Yo2, Pb, Qb = bs["Xo1"], bs["Xo2"], bs["Yo1"], bs["Yo2"], bs["Pb"], bs["Qb"]
            M1b, M2b = bs["XA"], bs["YA"]
            m3 = lambda mk: bc(mk[:].unsqueeze(1), [128, 4, 128])
            P.tt("pool", X0b[:], tX[:], mb[:], ALU.mult, r=[T("A1"), T("A4")], w=[T("XB")])
            P.tt("pool", attb[:], tA[:], LT3[:], ALU.mult, r=[T("A3"), "c3"], w=[T("H4")])
            kb_, pb_, pTtok = P.bget()
            pTb = pb_[:, 0:256].bitcast(BF16)
            for h in range(4):
                P.tr(pTb[:, h * 128:(h + 1) * 128], X0b[:, h, :], idb[:], r=[T("XB"), "idb"], w=[pTtok])
            pT3 = pTb.rearrange("p (h n) -> p h n", h=4)
            P.tt("dve", Ys[0][:], pT3, m3(mk16), ALU.mult, r=[pTtok, "mk"], w=[T("YA")])
            P.tt("dve", Yo1[:], pT3, m3(mk32), ALU.mult, r=[pTtok, "mk"], w=[T("Yo1")])
            P.bput(kb_)
            yield
            P.tt("dve", Xs[0][:], X0b[:], m3(mk16), ALU.mult, r=[T("XB"), "mk"], w=[T("XA")])
            P.tt("pool", Xo1[:], X0b[:], m3(mk32), ALU.mult, r=[T("XB"), "mk"], w=[T("Xo1")])
            P.cp("pool", Xo2[:], X0b[:], r=[T("XB")], w=[T("Xo2")])
            kb2, pb2, phtok = P.bget()
            ph = pb2[:, 0:256].bitcast(BF16)
            for h in range(4):
                P.tr(ph[:, h * 128:(h + 1) * 128], attb[:, h, :], idb[:], r=[T("H4"), "idb"], w=[phtok])
            P.cp("act", attT[:], ph.rearrange("p (h n) -> p h n", h=4), r=[phtok], w=[T("attT")])
            P.bput(kb2)
            yield
            P.tt("pool", Qb[:], I3[:], Xs[0][:], ALU.add, r=["c3", T("XA")], w=[T("Qb")])
            P.tt("dve", Pb[:], I3[:], Ys[0][:], ALU.add, r=["c3", T("YA")], w=[T("Pb")])
            yield
            a = 0
            for rnd in range(1, 5):
                b = 1 - a
                do_sq = rnd <= 3
                do_pr = rnd >= 2
                if do_sq:
                    kY, pY, pYtok = P.bget()
                    for h in range(4):
                        P.mm(pY[:, h * 128:(h + 1) * 128], Xs[a][:, h, :], Ys[a][:, h, :], True, True, r=[xs_t[a], ys_t[a]], w=[pYtok])
                    kX, pX, pXtok = P.bget()
                    for h in range(4):
                        P.mm(pX[:, h * 128:(h + 1) * 128], Ys[a][:, h, :], Xs[a][:, h, :], True, True, r=[xs_t[a], ys_t[a]], w=[pXtok])
                if do_pr:
                    kP, pP, pPtok = P.bget()
                    for h in range(4):
                        P.mm(pP[:, h * 128:(h + 1) * 128], Qb[:, h, :], Ys[a][:, h, :], True, True, r=[T("Qb"), ys_t[a]], w=[pPtok])
                    kQ, pQ, pQtok = P.bget()
                    for h in range(4):
                        P.mm(pQ[:, h * 128:(h + 1) * 128], Pb[:, h, :], Xs[a][:, h, :], True, True, r=[T("Pb"), xs_t[a]], w=[pQtok])
                if do_sq:
                    P.cp("act", Ys[b][:], v3(pY), r=[pYtok], w=[ys_t[b]])
                    P.bput(kY)
                    P.cp("act", Xs[b][:], v3(pX), r=[pXtok], w=[xs_t[b]])
                    P.bput(kX)
                if do_pr:
                    P.tt("dve", Pb[:], Pb[:], v3(pP), ALU.add, r=[T("Pb"), pPtok], w=[T("Pb")])
                    P.bput(kP)
                    P.tt("dve", Qb[:], Qb[:], v3(pQ), ALU.add, r=[T("Qb"), pQtok], w=[T("Qb")])
                    P.bput(kQ)
                a = b
                yield
            for lvl, (Xo, Yo, xo_t, yo_t) in enumerate(((Xo1, Yo1, T("Xo1"), T("Yo1")), (Xo2, Yo2, T("Xo2"), T("Yo2")))):
                last = lvl == 1
                k2, p2, p2tok = P.bget()
                for h in range(4):
                    P.mm(p2[:, h * 128:(h + 1) * 128], Xo[:, h, :], Pb[:, h, :], True, True, r=[xo_t, T("Pb")], w=[p2tok])
                if not last:
                    k1, p1, p1tok = P.bget()
                    for h in range(4):
                        P.mm(p1[:, h * 128:(h + 1) * 128], Yo[:, h, :], Qb[:, h, :], True, True, r=[yo_t, T("Qb")], w=[p1tok])
                if last:
                    P.tt("dve", M2b[:], v3(p2), m3(mk64), ALU.mult, r=[p2tok, "mk"], w=[T("YA")])
                else:
                    P.cp("act", M2b[:], v3(p2), r=[p2tok], w=[T("YA")])
                P.bput(k2)
                if not last:
                    P.cp("act", M1b[:], v3(p1), r=[p1tok], w=[T("XA")])
                    P.bput(k1)
                yield
                kP, pP, pPtok = P.bget()
                for h in range(4):
                    P.mm(pP[:, h * 128:(h + 1) * 128], Qb[:, h, :], M2b[:, h, :], True, True, r=[T("Qb"), T("YA")], w=[pPtok])
                if not last:
                    kQ, pQ, pQtok = P.bget()
                    for h in range(4):
                        P.mm(pQ[:, h * 128:(h + 1) * 128], Pb[:, h, :], M1b[:, h, :], True, True, r=[T("Pb"), T("XA")], w=[pQtok])
                P.tt("dve", Pb[:], Pb[:], v3(pP), ALU.add, r=[T("Pb"), pPtok], w=[T("Pb")])
                P.bput(kP)
                if not last:
                    P.tt("dve", Qb[:], Qb[:], v3(pQ), ALU.add, r=[T("Qb"), pQtok], w=[T("Qb")])
                    P.bput(kQ)
                yield
            kw_, pw, pwtok = P.bget()
            ku_, pu, putok = P.bget()
            for h in range(4):
                P.mm(pw[:, h * 128:(h + 1) * 128], RHSw[:, h, :], Pb[:, h, :], True, True, r=[T("RHSw"), T("Pb")], w=[pwtok])
            for h in range(4):
                P.mm(pu[:, h * 128:(h + 1) * 128], Pb[:, h, :], RHSu[:, h, :], True, True, r=[T("RHSu"), T("Pb")], w=[putok])
            P.cp("act", wT[:], v3(pw), r=[pwtok], w=[T("H4")])
            P.cp("act", uu[:], v3(pu), r=[putok], w=[T("A3")])
            P.bput(kw_)
            P.bput(ku_)

        def scan_gen(j, bs):
            k_ = bs["k"]
            T = lambda n: (n, k_)
            qgT, kd, attT, wT, vnew, uu, egl = bs["qgT"], bs["kd"], bs["attT"], bs["H4"], bs["vnew"], bs["A3"], bs["egl"]
            ko, po, potok = P.bget()
            bs["ko"] = (ko, po, potok)
            for r_ in range(2):
                rows = slice(64 * r_, 64 * r_ + 64)
                c = scan_state["c"]
                So, Sn = Sst[c % 2], Sst[(c + 1) % 2]
                so_t, sn_t = ("S", c % 2), ("S", (c + 1) % 2)
                scan_state["c"] = c + 1
                ka, pa, patok = P.bget()
                for h in range(4):
                    P.mm(pa[rows, h * 128:(h + 1) * 128], wT[:, h, rows], Sbf[:, h, :], True, True, r=[T("H4"), "Sbf"], w=[patok])
                P.tt("dve", vnew[rows], uu[rows], v3(pa)[rows], ALU.subtract, r=[T("A3"), patok], w=[T("vnew")])
                P.bput(ka)
                yield
                for h in range(4):
                    P.mm(po[rows, h * 128:(h + 1) * 128], qgT[:, h, rows], Sbf[:, h, :], True, False, r=[T("qgT"), "Sbf"], w=[potok])
                    P.mm(po[rows, h * 128:(h + 1) * 128], attT[rows, h, rows], vnew[rows, h, :], False, True, r=[T("attT"), T("vnew")], w=[potok])
                ks, ps_, pstok = P.bget()
                for h in range(4):
                    P.mm(ps_[:, h * 128:(h + 1) * 128], kd[rows, h, :], vnew[rows, h, :], True, True, r=[T("kd"), T("vnew")], w=[pstok])
                for h in range(4):
                    egc_ = egl[:, 2 * h + r_:2 * h + r_ + 1]
                    P.stt("dve", Sbf[:, h, :], So[:, h, :], egc_, ps_[:, h * 128:(h + 1) * 128], ALU.mult, ALU.add,
                          r=[so_t, T("egl"), pstok], w=["Sbf"])
                for h in range(4):
                    egc_ = egl[:, 2 * h + r_:2 * h + r_ + 1]
                    P.stt("dve", Sn[:, h, :], So[:, h, :], egc_, ps_[:, h * 128:(h + 1) * 128], ALU.mult, ALU.add,
                          r=[so_t, T("egl"), pstok], w=[sn_t])
                P.bput(ks)
                yield

        def post_gen(j, bs):
            k_ = bs["k"]
            T = lambda n: (n, k_)
            ts_ = slice(j * 128, (j + 1) * 128)
            og, ogb, oss = bs["A4"], bs["H4"], bs["oss"]
            ko, po, potok = bs["ko"]
            for h in range(4):
                P.act(jk[:], po[:, h * 128:(h + 1) * 128], AF.Square, accum_out=oss[:, h:h + 1], r=[potok], w=["jk", T("oss")])
            P.act(oss[:, 4:8], oss[:, 0:4], AF.Ln, scale=1.0 / 128.0, bias=1e-6, r=[T("oss")], w=[T("oss")])
            P.act(oss[:, 4:8], oss[:, 4:8], AF.Exp, scale=-0.5, r=[T("oss")], w=[T("oss")])
            P.tt("dve", og[:], v3(po), col3(oss[:, 4:8]), ALU.mult, r=[potok, T("oss"), T("A4")], w=[T("A4")])
            P.bput(ko)
            yield
            P.tt("pool", ogb[:], og[:], sz2[cur["tb"] % 2][:, j, :].rearrange("p (h n) -> p h n", h=4), ALU.mult,
                 r=[T("A4"), ("sz", cur["tb"] % 2, j), T("H4")], w=[T("H4")])
            kb_, pb_, phtok = P.bget()
            ph = pb_[:, 0:256].bitcast(BF16)
            for h in range(4):
                P.tr(ph[:, h * 128:(h + 1) * 128], ogb[:, h, :], idb[:], r=[T("H4"), "idb"], w=[phtok])
            P.cp("act", mixd[:, :, ts_], ph.rearrange("p (h n) -> p h n", h=4), r=[phtok], w=["mixd"])
            P.bput(kb_)

        def l2norm_chunk(c, ci, bp):
            qkvs = qkvs2[bp]
            i = ci % 2
            k, pt, ptok = P.bget()
            P.mm(pt[:], ones_bf[:], sq8[:, ci, :], True, True, r=["ones_bf", ("sq8", ci)], w=[ptok])
            if c == 0:
                P.act(lnb[i][:], pt[:], AF.Ln, scale=128.0, bias=128.0e-6, r=[ptok], w=[("lnb", i)])
            else:
                P.act(lnb[i][:], pt[:], AF.Ln, bias=1e-6, r=[ptok], w=[("lnb", i)])
            P.bput(k)
            P.act(lnb[i][:], lnb[i][:], AF.Exp, scale=-0.5, r=[("lnb", i)], w=[("lnb", i)])
            P.tt("dve", qkvs[:, ci, :], qkvs[:, ci, :], lnb[i][:], ALU.mult, r=[("qkvs", bp, ci), ("lnb", i)], w=[("qkvs", bp, ci)])

        def block_stage0(tb):
            sz = sz2[tb % 2]
            szp = tb % 2
            bp = tb % 2
            qkvs, beta, gg = qkvs2[bp], beta2[bp], gg2[bp]
            for c in range(3):
                W, wtok = ring.load(WCH["in"][0] + c)
                for m in range(4):
                    ci = c * 4 + m
                    k, pt, ptok = P.bget()
                    for kc in range(8):
                        P.mm(pt[:], W[:, kc, m * 128:(m + 1) * 128], hb[:, kc, :], kc == 0, kc == 7, r=[wtok, "hT"], w=[ptok])
                    P.cp("act", pre4[:, m, 3:515], pt[:], r=[ptok], w=[("pre4", m)])
                    P.bput(k)
                    P.cp("pool", pre4[:, m, 0:3], prec[:, ci, :], r=[("prec", ci)], w=[("pre4", m)])
                    P.ts("dve", qkvs[:, ci, :], pre4[:, m, 0:512], cwt[:, ci, 0:1], ALU.mult, r=[("pre4", m), "cwt"], w=[("qkvs", bp, ci)])
                    for jj in range(1, 4):
                        P.stt("dve", qkvs[:, ci, :], pre4[:, m, jj:jj + 512], cwt[:, ci, jj:jj + 1], qkvs[:, ci, :], ALU.mult, ALU.add,
                              r=[("pre4", m), "cwt", ("qkvs", bp, ci)], w=[("qkvs", bp, ci)])
                    P.cp("pool", prec[:, ci, :], pre4[:, m, 512:515], r=[("pre4", m)], w=[("prec", ci)])
                    P.act(qkvs[:, ci, :], qkvs[:, ci, :], AF.Silu, r=[("qkvs", bp, ci)], w=[("qkvs", bp, ci)])
                    if c < 2:
                        P.act(sq8[:, ci, :], qkvs[:, ci, :], AF.Square, r=[("qkvs", bp, ci)], w=[("sq8", ci)])
                    yield
            W, wtok = ring.load(WCH["in"][0] + 3)
            for j in range(4):
                k, pt, ptok = P.bget()
                for kc in range(8):
                    P.mm(pt[:], hb[:, kc, j * 128:(j + 1) * 128], W[:, kc, :], kc == 0, kc == 7, r=[wtok, "hT"], w=[ptok])
                P.act(sz[:, j, :], pt[:], AF.Silu, r=[ptok], w=[("sz", szp, j)])
                P.bput(k)
                P.tt("pool", sz[:, j, :].rearrange("p (h n) -> p h n", h=4), sz[:, j, :].rearrange("p (h n) -> p h n", h=4), dng4[:],
                     ALU.mult, r=[("sz", szp, j), "dng4"], w=[("sz", szp, j)])
                yield
            W, wtok = ring.load(WCH["in"][0] + 5)
            k, pt, ptok = P.bget()
            for j in range(4):
                for kc in range(8):
                    P.mm(pt[:, j * 8:(j + 1) * 8], hb[:, kc, j * 128:(j + 1) * 128], W[:, kc, 384:392], kc == 0, kc == 7, r=[wtok, "hT"], w=[ptok])
            P.cp("dve", ba[:], pt[:, 0:32].rearrange("p (j n) -> p j n", j=4), r=[ptok], w=["ba"])
            P.bput(k)
            P.act(beta[:], ba[:, :, 0:4], AF.Sigmoid, r=["ba"], w=[("beta", bp)])
            P.tt("dve", gg[:], ba[:, :, 4:8], bc(dtb[:].unsqueeze(1), [128, 4, 4]), ALU.add, r=["ba", "dtb"], w=[("gg", bp)])
            P.act(gg[:], gg[:], AF.Exp, r=[("gg", bp)], w=[("gg", bp)])
            P.act(gg[:], gg[:], AF.Ln, bias=1.0, r=[("gg", bp)], w=[("gg", bp)])
            P.tt("dve", gg[:], gg[:], bc(negA[:].unsqueeze(1), [128, 4, 4]), ALU.mult, r=[("gg", bp), "negA"], w=[("gg", bp)])
            yield
            for ci in range(8):
                l2norm_chunk(ci // 4, ci, bp)
                if ci % 2 == 1:
                    yield

        def bg_ffn_prep():
            stg_ = [P.sb([128, 512], F32) for _ in range(2)]
            stb_ = [P.sb([128, 512], BF16) for _ in range(2)]
            pieces = []
            for nm, wt in (("gate", w_gate), ("up", w_up)):
                for c in range(6):
                    ncol = min(512, DFF - c * 512)
                    for kc in range(8):
                        pieces.append((WCH[nm][0] + c, kc, wt[kc * 128:(kc + 1) * 128, c * 512:c * 512 + ncol], ncol))
            for g in range(3):
                nk = 8 if g < 2 else 6
                for hf in range(2):
                    for kc in range(nk):
                        r0 = g * 1024 + kc * 128
                        pieces.append((WCH["down"][0] + g * 2 + hf, kc, w_down[r0:r0 + 128, hf * 512:(hf + 1) * 512], 512))

            def load(n):
                ch, kc, src, ncol = pieces[n]
                i = n % 2
                if ncol < 512:
                    P.memset("pool", stg_[i][:, ncol:512], 0.0, w=[("bgs", i)])
                P.dma("sp", stg_[i][:, 0:ncol], src, w=[("bgs", i)])

            load(0)
            for n in range(len(pieces)):
                ch, kc, src, ncol = pieces[n]
                i = n % 2
                if n + 1 < len(pieces):
                    load(n + 1)
                P.cp("act", stb_[i][:], stg_[i][:], r=[("bgs", i)], w=[("bgb", i)])
                yield
                P.dma("act", wsc[ch][:, kc * 512:(kc + 1) * 512], stb_[i][:], r=[("bgb", i)], w=[("wscbg", ch, kc)])
                yield

        def front_gen(tb):
            for j in range(4):
                t = tb * 4 + j
                i = t % 2
                P.dma("sp", xt[i][:], x[t * 128:(t + 1) * 128, :], w=[("xt", i)])
                make_hT(P, xt[i][:], ("xt", i), j, s1m, shm, "mod", idb, hb, "hT", tmpf, hbf, eng="pool")
                yield
            P.dma("act", hTs[tb].rearrange("p (k n) -> p k n", k=8), hb[:], r=["hT"], w=[("hTs", tb)])
            if GDN:
                yield from block_stage0(tb)

        def exhaust(g):
            if g is not None:
                for _ in g:
                    pass

        bg = bg_ffn_prep() if BG_FFN else None
        bgs = [bg]

        def bg_step():
            if bgs[0] is not None:
                try:
                    next(bgs[0])
                except StopIteration:
                    bgs[0] = None

        exhaust(front_gen(0))
        for tb in range(NB):
            if not GDN:
                P.dma("pool", odn[tb], zt[:], r=["mixd"], w=[("odn", tb)])
                if tb + 1 < NB:
                    exhaust(front_gen(tb + 1))
                continue
            cur["tb"] = tb
            state = {}
            gens = {}
            set_of = {}
            nyield = {}
            free_sets = list(range(NSET))
            nxt = 0
            scan_next = 0
            done = 0
            fgen = None
            fstarted = False
            while done < 4:
                while nxt < 4 and free_sets:
                    k = free_sets.pop(0)
                    set_of[nxt] = sets[k]
                    gens[nxt] = prep_gen(nxt, sets[k])
                    state[nxt] = "prep"
                    nyield[nxt] = 0
                    nxt += 1
                if scan_next < 4 and state.get(scan_next) == "ready" and not any(v == "scan" for v in state.values()):
                    gens[scan_next] = scan_gen(scan_next, set_of[scan_next])
                    state[scan_next] = "scan"
                if not fstarted and tb + 1 < NB:
                    fgen = front_gen(tb + 1)
                    fstarted = True
                order_ = sorted([j for j in gens if state[j] in ("prep", "scan", "post")], key=lambda j: (state[j] != "scan", j))
                for j in order_:
                    try:
                        next(gens[j])
                        nyield[j] = nyield.get(j, 0) + 1
                    except StopIteration:
                        if state[j] == "prep":
                            state[j] = "ready"
                        elif state[j] == "scan":
                            state[j] = "post"
                            gens[j] = post_gen(j, set_of[j])
                            scan_next += 1
                        elif state[j] == "post":
                            state[j] = "done"
                            free_sets.append(set_of[j]["k"])
                            done += 1
                if fgen is not None:
                    try:
                        next(fgen)
                    except StopIteration:
                        fgen = None
                bg_step()
            P.dma("pool", odn[tb], zt[:], r=["mixd"], w=[("odn", tb)])
            if tb + 1 < NB and not fstarted:
                fgen = front_gen(tb + 1)
            exhaust(fgen)
        exhaust(bgs[0])
        P.finish()

    def build_p1b():
        P = Pass(nc, "p1b")
        P.psum_banks(8)
        cst = P.sb([128, NCONST], F32)
        P.dma("sp", cst[:], consts, w=["cst"])
        ones_b = P.sb([128, 128], BF16)
        utb = P.sb([128, 128], BF16)
        P.cp("dve", ones_b[:], cst[:, C_ONE:C_ONE + 128], r=["cst"], w=["ones_b"])
        P.cp("dve", utb[:], cst[:, C_UT:C_UT + 128], r=["cst"], w=["utb"])
        invf = cst[0:64, C_MISC + 0:C_MISC + 1]
        sgn = cst[0:64, C_MISC + 1:C_MISC + 2]
        gtm = P.sb([128, D], F32)
        g1 = P.sb([128, D], F32)
        b1 = P.sb([128, D], F32)
        P.dma("sp", gtm[:], modsc[:, 2 * D:3 * D], w=["mod"])
        P.dma("act", g1[:], bc(ln1_g[0:1, :], [128, D]), w=["ln"])
        P.dma("act", b1[:], bc(ln1_b[0:1, :], [128, D]), w=["ln"])
        kTc = P.sb([128, 4, SEQ], BF16)
        krTc = P.sb([128, SEQ], BF16)
        Vc = P.sb([128, NT, 512], BF16)
        ring = Ring(P, 2)
        hT = P.sb([128, 8, 512], BF16)
        cqT = P.sb([128, 4, 512], BF16)
        cqsq = P.sb([128, 4, 512], BF16)
        ckvT = P.sb([128, 2, 512], BF16)
        ckvsq = P.sb([128, 2, 512], BF16)
        rq_b = P.sb([128, 512], F32)
        rkv_b = P.sb([128, 512], F32)
        rkv_c = P.sb([128, 4], F32)
        cosT = P.sb([64, 512], F32)
        sinT = P.sb([64, 512], F32)
        posi = P.sb([64, 512], I32)
        ta = P.sb([64, 512], F32)
        tb_ = P.sb([64, 512], F32)
        tc = P.sb([64, 512], F32)
        ki = P.sb([64, 512], I32)
        t1 = P.sb([128, 512], F32)
        t2 = P.sb([128, 512], F32)
        qTn2 = [P.sb([128, 4, 512], BF16) for _ in range(2)]
        qrT2 = [P.sb([128, 4, 512], BF16) for _ in range(2)]
        P.memset("pool", krTc[64:128, :], 0.0, w=["krpad"])
        for q_ in qrT2:
            P.memset("pool", q_[64:128, :, :], 0.0, w=["qrpad"])
        sqa = P.sb([128, 512], BF16)
        sqb = P.sb([64, 512], BF16)
        pT = [P.sb([128, 512], BF16) for _ in range(5)]
        mixT = P.sb([128, 8, 512], BF16)
        xt = [P.sb([128, D], F32) for _ in range(4)]
        pending_ln = []
        tmpy = P.sb([128, D], F32)
        junk = P.sb([128, D], F32)
        st4 = [P.sb([128, 8], F32) for _ in range(4)]
        ot = P.sb([128, D], F32)
        km2 = P.sb([128, 4], F32)
        sm = P.sb([128, 8], F32)
        bias2 = [P.sb([128, 4], F32) for _ in range(2)]
        P.memset("dve", km2[:], 0.0, w=["km2"])
        npt = [0]

        def sumsq_b(dst, pieces, r):
            n = len(pieces)
            for k, (ap, npart) in enumerate(pieces):
                P.mm(dst, ones_b[0:npart, :], ap, k == 0, k == n - 1, r=["ones_b"] + r, w=[dst_tok[0]])

        def mla_prep(tb, par):
            t0 = tb * 512
            qTn = qTn2[par]
            qrT = qrT2[par]
            bias_h = bias2[par]
            P.dma("sp", hT[:], hTs[tb].rearrange("p (k n) -> p k n", k=8), w=["hT"])
            P.dma("act", posi[:], bc(pos[0:1, t0:t0 + 512], [64, 512]), w=["posi"])
            P.cp("dve", ta[:], posi[:], r=["posi"], w=["ta"])
            P.ts("dve", ta[:], ta[:], invf, ALU.mult, r=["ta", "cst"], w=["ta"])
            P.ts("dve", tb_[:], ta[:], 1.0 / (2.0 * PI), ALU.mult, r=["ta"], w=["tb_"])
            P.cp("dve", ki[:], tb_[:], r=["tb_"], w=["ki"])
            P.cp("dve", tb_[:], ki[:], r=["ki"], w=["tb_"])
            P.stt("dve", ta[:], tb_[:], -C1, ta[:], ALU.mult, ALU.add, r=["tb_", "ta"], w=["ta"])
            P.stt("dve", ta[:], tb_[:], -C2, ta[:], ALU.mult, ALU.add, r=["tb_", "ta"], w=["ta"])
            P.ts("dve", tc[:], ta[:], PI / 2.0, ALU.add, r=["ta"], w=["tc"])
            P.ts("dve", tb_[:], tc[:], PI, ALU.is_gt, r=["tc"], w=["tb_"])
            P.stt("dve", tc[:], tb_[:], -2.0 * PI, tc[:], ALU.mult, ALU.add, r=["tb_", "tc"], w=["tc"])
            P.ts("dve", tc[:], tc[:], -PI, ALU.max, s2=PI, op1=ALU.min, r=["tc"], w=["tc"])
            P.ts("dve", ta[:], ta[:], -PI, ALU.max, s2=PI, op1=ALU.min, r=["ta"], w=["ta"])
            P.act(cosT[:], tc[:], AF.Sin, r=["tc"], w=["cosT"])
            P.act(sinT[:], ta[:], AF.Sin, r=["ta"], w=["sinT"])
            P.ts("dve", sinT[:], sinT[:], sgn, ALU.mult, r=["sinT", "cst"], w=["sinT"])
            yield

            def rope_out(dst, pa, patok, pbk, pbtok, extra=None, extok=None, dtok=None):
                P.tt("dve", t1[0:64, :], pa[0:64, :], cosT[:], ALU.mult, r=[patok, "cosT"], w=["t1"])
                P.tt("dve", t2[0:64, :], pbk[0:64, :], sinT[:], ALU.mult, r=[pbtok, "sinT"], w=["t2"])
                if extra is None:
                    P.tt("dve", dst, t1[0:64, :], t2[0:64, :], ALU.add, r=["t1", "t2"], w=[dtok])
                else:
                    P.tt("pool", t1[0:64, :], t1[0:64, :], t2[0:64, :], ALU.add, r=["t1", "t2"], w=["t1"])
                    P.tt("dve", dst, t1[0:64, :], extra, ALU.mult, r=["t1", extok], w=[dtok])

            w4, w4tok = ring.load(WCH["in"][0] + 4)
            for fc in range(4):
                k, pt, ptok = P.bget()
                for kc in range(8):
                    P.mm(pt[:], w4[:, kc, fc * 128:(fc + 1) * 128], hT[:, kc, :], kc == 0, kc == 7, r=[w4tok, "hT"], w=[ptok])
                P.cp("act", cqT[:, fc, :], pt[:], r=[ptok], w=["cqT"])
                P.act(cqsq[:, fc, :], pt[:], AF.Square, r=[ptok], w=["cqsq"])
                P.bput(k)
                yield
            k, pt, ptok = P.bget()
            for fc in range(4):
                P.mm(pt[:], ones_b[:], cqsq[:, fc, :], fc == 0, fc == 3, r=["ones_b", "cqsq"], w=[ptok])
            P.act(t1[:], pt[:], AF.Ln, scale=1.0 / 512.0, bias=1e-6, r=[ptok], w=["t1"])
            P.act(rq_b[:], t1[:], AF.Exp, scale=-0.5, r=["t1"], w=["rq_b"])
            P.bput(k)
            yield
            w5, w5tok = ring.load(WCH["in"][0] + 5)
            for fc in range(2):
                k, pt, ptok = P.bget()
                for kc in range(8):
                    P.mm(pt[:], w5[:, kc, fc * 128:(fc + 1) * 128], hT[:, kc, :], kc == 0, kc == 7, r=[w5tok, "hT"], w=[ptok])
                P.cp("act", ckvT[:, fc, :], pt[:], r=[ptok], w=["ckvT"])
                P.act(ckvsq[:, fc, :], pt[:], AF.Square, r=[ptok], w=["ckvsq"])
                P.bput(k)
                yield
            k, pt, ptok = P.bget()
            for fc in range(2):
                P.mm(pt[:], ones_b[:], ckvsq[:, fc, :], fc == 0, fc == 1, r=["ones_b", "ckvsq"], w=[ptok])
            P.act(t1[:], pt[:], AF.Ln, scale=1.0 / 256.0, bias=1e-6, r=[ptok], w=["t1"])
            P.act(rkv_b[:], t1[:], AF.Exp, scale=-0.5, r=["t1"], w=["rkv_b"])
            P.bput(k)
            yield
            k, pt, ptok = P.bget()
            for j in range(4):
                for fc in range(2):
                    P.mm(pt[:, j:j + 1], ckvsq[:, fc, j * 128:(j + 1) * 128], ones_b[:, 0:1], fc == 0, fc == 1,
                         r=["ones_b", "ckvsq"], w=[ptok])
            P.act(sm[:, 0:4], pt[:, 0:4], AF.Ln, scale=1.0 / 256.0, bias=1e-6, r=[ptok], w=["sm"])
            P.act(rkv_c[:], sm[:, 0:4], AF.Exp, scale=-0.5, r=["sm"], w=["rkv_c"])
            P.bput(k)
            yield
            ka, pa, patok = P.bget()
            kb, pbk, pbtok = P.bget()
            for kc in range(8):
                P.mm(pa[0:64, :], w5[:, kc, 256:320], hT[:, kc, :], kc == 0, kc == 7, r=[w5tok, "hT"], w=[patok])
            for kc in range(8):
                P.mm(pbk[0:64, :], w5[:, kc, 320:384], hT[:, kc, :], kc == 0, kc == 7, r=[w5tok, "hT"], w=[pbtok])
            rope_out(krTc[0:64, t0:t0 + 512], pa, patok, pbk, pbtok, dtok=("krT", tb))
            P.bput(ka)
            P.bput(kb)
            yield
            P.act(sqb[:], krTc[0:64, t0:t0 + 512], AF.Square, r=[("krT", tb)], w=["sqb"])
            wk, wktok = ring.load(WCH["ukv"][0] + 0, nk=2)
            for h in range(4):
                k, pt, ptok = P.bget()
                for fc in range(2):
                    P.mm(pt[:], wk[:, fc, h * 128:(h + 1) * 128], ckvT[:, fc, :], fc == 0, fc == 1, r=[wktok, "ckvT"], w=[ptok])
                P.tt("dve", kTc[:, h, t0:t0 + 512], pt[:], rkv_b[:], ALU.mult, r=[ptok, "rkv_b"], w=[("kT", h, tb)])
                P.bput(k)
                yield
                P.act(sqa[:], kTc[:, h, t0:t0 + 512], AF.Square, r=[("kT", h, tb)], w=["sqa"])
                k, pt, ptok = P.bget()
                P.mm(pt[:], ones_b[:], sqa[:], True, False, r=["ones_b", "sqa"], w=[ptok])
                P.mm(pt[:], ones_b[0:64, :], sqb[:], False, True, r=["ones_b", "sqb"], w=[ptok])
                P.S.op("dve", lambda e, o=sm[:, 5:6], i_=pt[:]: e.reduce_max(out=o, in_=i_, axis=mybir.AxisListType.X), r=[ptok], w=["sm"])
                P.tt("dve", km2[:, h:h + 1], km2[:, h:h + 1], sm[:, 5:6], ALU.max, r=["sm", "km2"], w=["km2"])
                P.bput(k)
                yield
            wv, wvtok = ring.load(WCH["ukv"][0] + 1, nk=2)
            for j in range(4):
                k, pt, ptok = P.bget()
                for fc in range(2):
                    P.mm(pt[:], ckvT[:, fc, j * 128:(j + 1) * 128], wv[:, fc, :], fc == 0, fc == 1, r=[wvtok, "ckvT"], w=[ptok])
                P.act(Vc[:, tb * 4 + j, :], pt[:], AF.Copy, scale=rkv_c[:, j:j + 1], r=[ptok, "rkv_c"], w=[("V", tb * 4 + j)])
                P.bput(k)
                yield
            wq0, wq0tok = ring.load(WCH["uq"][0] + 0, nk=4)
            for h in range(4):
                k, pt, ptok = P.bget()
                for fc in range(4):
                    P.mm(pt[:], wq0[:, fc, h * 128:(h + 1) * 128], cqT[:, fc, :], fc == 0, fc == 3, r=[wq0tok, "cqT"], w=[ptok])
                P.tt("dve", qTn[:, h, :], pt[:], rq_b[:], ALU.mult, r=[ptok, "rq_b"], w=[("qTn", par, h)])
                P.bput(k)
                yield
            wq1, wq1tok = ring.load(WCH["uq"][0] + 1, nk=4)
            for h in range(4):
                ka, pa, patok = P.bget()
                kb, pbk, pbtok = P.bget()
                for fc in range(4):
                    P.mm(pa[0:64, :], wq1[:, fc, h * 64:(h + 1) * 64], cqT[:, fc, :], fc == 0, fc == 3, r=[wq1tok, "cqT"], w=[patok])
                for fc in range(4):
                    P.mm(pbk[0:64, :], wq1[:, fc, 256 + h * 64:256 + (h + 1) * 64], cqT[:, fc, :], fc == 0, fc == 3, r=[wq1tok, "cqT"], w=[pbtok])
                rope_out(qrT[0:64, h, :], pa, patok, pbk, pbtok, extra=rq_b[0:64, :], extok="rq_b", dtok=("qrT", par, h))
                P.bput(ka)
                P.bput(kb)
                yield
            for h in range(4):
                P.act(sqa[:], qTn[:, h, :], AF.Square, r=[("qTn", par, h)], w=["sqa"])
                P.act(sqb[:], qrT[0:64, h, :], AF.Square, r=[("qrT", par, h)], w=["sqb"])
                k, pt, ptok = P.bget()
                P.mm(pt[:], ones_b[:], sqa[:], True, False, r=["ones_b", "sqa"], w=[ptok])
                P.mm(pt[:], ones_b[0:64, :], sqb[:], False, True, r=["ones_b", "sqb"], w=[ptok])
                P.S.op("dve", lambda e, o=sm[:, 6:7], i_=pt[:]: e.reduce_max(out=o, in_=i_, axis=mybir.AxisListType.X), r=[ptok], w=["sm"])
                P.bput(k)
                yield
                P.tt("dve", sm[:, 7:8], sm[:, 6:7], km2[:, h:h + 1], ALU.mult, r=["sm", "km2"], w=["sm"])
                P.act(sm[:, 7:8], sm[:, 7:8], AF.Ln, bias=1e-30, r=["sm"], w=["sm"])
                P.act(sm[:, 7:8], sm[:, 7:8], AF.Exp, scale=0.5, r=["sm"], w=["sm"])
                P.ts("dve", bias_h[:, h:h + 1], sm[:, 7:8], -SCALE, ALU.mult, r=["sm"], w=[("bias", par, h)])
        gprep = mla_prep(0, 0)
        for _ in gprep:
            pass
        for tb in range(NB):
            par = tb % 2
            qTn = qTn2[par]
            qrT = qrT2[par]
            bias_h = bias2[par]
            gnext = [mla_prep(tb + 1, 1 - par) if tb + 1 < NB else None]

            def step_next():
                if gnext[0] is not None:
                    try:
                        next(gnext[0])
                    except StopIteration:
                        gnext[0] = None
            P.dma("act", mixT[:, 0:4, :], odn[tb].rearrange("p (k n) -> p k n", k=4), w=[("mixT", k) for k in range(4)])
            for h in range(4):
                ko, po, potok = P.bget()
                kl, pl, pltok = P.bget()
                nkt = 4 * tb + 4
                order = list(range(4 * tb, nkt)) + list(range(0, 4 * tb))
                LAG = 2
                pend = []

                def pv(item):
                    n__, kt_, q0_, pi__ = item
                    p__ = pT[pi__]
                    first = n__ == 0
                    last = n__ == len(order) - 1
                    P.mm(po[:, q0_:512], Vc[:, kt_, h * 128:(h + 1) * 128], p__[:, q0_:512], first, last, r=[("V", kt_), ("pT", pi__)], w=[potok])
                    P.mm(pl[:, q0_:512], ones_b[:], p__[:, q0_:512], first, last, r=["ones_b", ("pT", pi__)], w=[pltok])

                for n_, kt in enumerate(order):
                    i = kt - 4 * tb
                    q0 = max(i, 0) * 128
                    k, ps_, pstok = P.bget()
                    P.mm(ps_[:, q0:512], kTc[:, h, kt * 128:(kt + 1) * 128], qTn[:, h, q0:512], True, False,
                         r=[("kT", h, kt // 4), ("qTn", par, h)], w=[pstok])
                    P.mm(ps_[:, q0:512], krTc[:, kt * 128:(kt + 1) * 128], qrT[:, h, q0:512], False, True,
                         r=[("krT", kt // 4), ("qrT", par, h), "krpad", "qrpad"], w=[pstok])
                    pi_ = npt[0] % len(pT)
                    npt[0] += 1
                    p_ = pT[pi_]
                    P.act(p_[:, q0:512], ps_[:, q0:512], AF.Exp, scale=SCALE, bias=bias_h[:, h:h + 1], r=[pstok, ("bias", par, h)], w=[("pT", pi_)])
                    P.bput(k)
                    if i >= 0:
                        P.tt("pool", p_[:, q0:q0 + 128], p_[:, q0:q0 + 128], utb[:], ALU.mult, r=[("pT", pi_), "utb"], w=[("pT", pi_)])
                    pend.append((n_, kt, q0, pi_))
                    if len(pend) > LAG:
                        pv(pend.pop(0))
                    step_next()
                while pend:
                    pv(pend.pop(0))
                P.recip(tmpy[:, 0:512], pl[:], r=[pltok], w=["tmpy"])
                P.tt("dve", mixT[:, 4 + h, :], po[:], tmpy[:, 0:512], ALU.mult, r=[potok, "tmpy"], w=[("mixT", 4 + h)])
                P.bput(ko)
                P.bput(kl)
                for _ in range(1 if h == 0 else 2):
                    if pending_ln:
                        pending_ln.pop(0)()
            while pending_ln:
                pending_ln.pop(0)()
            while gnext[0] is not None:
                step_next()
            wo = [ring.load(WCH["o"][0] + hf) for hf in range(2)]
            for j in range(4):
                t = tb * 4 + j
                i = j
                P.dma("sp", xt[i][:], x[t * 128:(t + 1) * 128, :], w=[("xt", i)])
                for hf in range(2):
                    k, pt, ptok = P.bget()
                    for kc in range(8):
                        P.mm(pt[:], mixT[:, kc, j * 128:(j + 1) * 128], wo[hf][0][:, kc, :], kc == 0, kc == 7,
                             r=[("mixT", kc), wo[hf][1]], w=[ptok])
                    sl = slice(hf * 512, (hf + 1) * 512)
                    P.tt("dve", tmpy[:, sl], pt[:], gtm[:, sl], ALU.mult, r=[ptok, "mod"], w=["tmpy"])
                    P.bput(k)
                P.stt("dve", xt[i][:], xt[i][:], ALPHA, tmpy[:], ALU.mult, ALU.add, r=[("xt", i), "tmpy"], w=[("xt", i)])

                def ln1a(i=i):
                    ln_stats(P, xt[i][:], ("xt", i), junk, st4[i], ("st", i))

                def ln1b(t=t, i=i):
                    ln_apply(P, xt[i][:], ("xt", i), g1, b1, "ln", ot[:], "ot", junk, st4[i], ("st", i))
                    P.dma("pool", x1s[t * 128:(t + 1) * 128, :], ot[:], r=["ot"], w=[("x1s", t)])
                pending_ln += [ln1a, ln1b]
        while pending_ln:
            pending_ln.pop(0)()
        P.finish()

    if "mix" in parts:
        build_p1a()
        build_p1b()

    P = Pass(nc, "p2")
    P.psum_banks(8)
    cst, idb = load_consts(P)
    ring = Ring(P, 3)
    s1f = P.sb([128, D], F32)
    shf = P.sb([128, D], F32)
    gtf = P.sb([128, D], F32)
    g2 = P.sb([128, D], F32)
    b2 = P.sb([128, D], F32)
    P.dma("sp", shf[:], modsc[:, 3 * D:4 * D], w=["mod"])
    P.dma("sp", s1f[:], modsc[:, 4 * D:5 * D], w=["mod"])
    P.dma("sp", gtf[:], modsc[:, 5 * D:6 * D], w=["mod"])
    P.dma("act", g2[:], bc(ln2_g[0:1, :], [128, D]), w=["ln"])
    P.dma("act", b2[:], bc(ln2_b[0:1, :], [128, D]), w=["ln"])
    xblk2 = [P.sb([128, 4, D], F32) for _ in range(2)]
    hT2 = [P.sb([128, 8, 512], BF16) for _ in range(2)]
    aT = P.sb([128, NFC, 512], BF16)
    tmpf = P.sb([128, D], F32)
    tmpg = P.sb([128, 512], F32)
    hbf = P.sb([128, D], BF16)
    sg = [P.sb([128, 512], F32) for _ in range(2)]
    junk = P.sb([128, D], F32)
    st4 = [P.sb([128, 8], F32) for _ in range(4)]
    ot = [P.sb([128, D], F32) for _ in range(2)]

    def p2_prep(tb):
        par = tb % 2
        for j in range(4):
            t = tb * 4 + j
            P.dma("sp", xblk2[par][:, j, :], x1s[t * 128:(t + 1) * 128, :], w=[("xblk", par, j)])
            make_hT(P, xblk2[par][:, j, :], ("xblk", par, j), j, s1f, shf, "mod", idb, hT2[par], ("hT", par), tmpf, hbf)

    p2_prep(0)
    pending_ln = []
    for tb in range(NB):
        par = tb % 2
        xblk = xblk2[par]
        hT = hT2[par]
        hTtok = ("hT", par)
        for c in range(6):
            wg, wgtok = ring.load(WCH["gate"][0] + c, "sp")
            wu, wutok = ring.load(WCH["up"][0] + c, "sp")
            for m in range(4 if c < 5 else 2):
                fc = c * 4 + m
                kg, pg, pgtok = P.bget()
                ku, pu, putok = P.bget()
                for kc in range(8):
                    P.mm(pg[:], wg[:, kc, m * 128:(m + 1) * 128], hT[:, kc, :], kc == 0, kc == 7, r=[wgtok, hTtok], w=[pgtok])
                for kc in range(8):
                    P.mm(pu[:], wu[:, kc, m * 128:(m + 1) * 128], hT[:, kc, :], kc == 0, kc == 7, r=[wutok, hTtok], w=[putok])
                i = fc % 2
                P.act(sg[i][:], pg[:], AF.Silu, r=[pgtok], w=[("sg", i)])
                P.tt("dve", aT[:, fc, :], sg[i][:], pu[:], ALU.mult, r=[("sg", i), putok], w=[("aT", fc)])
                P.bput(kg)
                P.bput(ku)
            for _ in range(1 if c == 0 else 2):
                if pending_ln:
                    pending_ln.pop(0)()
            if c == 4 and tb + 1 < NB:
                assert not pending_ln
                p2_prep(tb + 1)
        for hf in range(2):
            acc = [P.bget() for _ in range(4)]
            for g in range(3):
                nk = 8 if g < 2 else 6
                wd, wdtok = ring.load(WCH["down"][0] + g * 2 + hf, "sp", nk=nk)
                for j in range(4):
                    k, pt, ptok = acc[j]
                    for kk in range(nk):
                        fc = g * 8 + kk
                        P.mm(pt[:], aT[:, fc, j * 128:(j + 1) * 128], wd[:, kk, :], fc == 0, fc == NFC - 1,
                             r=[("aT", fc), wdtok], w=[ptok])
            for j in range(4):
                k, pt, ptok = acc[j]
                sl = slice(hf * 512, (hf + 1) * 512)
                P.tt("dve", tmpg[:], pt[:], gtf[:, sl], ALU.mult, r=[ptok, "mod"], w=["tmpg"])
                P.stt("dve", xblk[:, j, sl], xblk[:, j, sl], ALPHA, tmpg[:], ALU.mult, ALU.add,
                      r=[("xblk", par, j), "tmpg"], w=[("xblk", par, j)])
                P.bput(k)
        def mk_ln(tb=tb, par=par, xblk=xblk):
            fs = []
            for j in range(4):
                t = tb * 4 + j

                def fa(j=j):
                    ln_stats(P, xblk[:, j, :], ("xblk", par, j), junk, st4[j], ("st", j))

                def fb(j=j, t=t):
                    i = t % 2
                    ln_apply(P, xblk[:, j, :], ("xblk", par, j), g2, b2, "ln", ot[i][:], ("ot", i), junk, st4[j], ("st", j))
                    P.dma("pool", out[t * 128:(t + 1) * 128, :], ot[i][:], r=[("ot", i)], w=[("out", t)])
                fs += [fa, fb]
            return fs
        pending_ln = mk_ln()
    while pending_ln:
        pending_ln.pop(0)()
    P.finish()
    return nc


_NC_CACHE = {}


def _prep_inputs(inp, b):
    f = lambda a: np.ascontiguousarray(a, dtype=np.float32)
    conv_w = np.asarray(inp["conv_w"])[0, :, 0, :]
    m = {
        "x": f(inp["x"][b]),
        "cT": f(np.asarray(inp["c"])[b].reshape(8, 128).T),
        "pos": np.ascontiguousarray(np.asarray(inp["positions"])[b][None, :].astype(np.int32)),
        "w_ada": f(inp["w_ada"][0]),
        "b_ada": f(inp["b_ada"][0][None, :]),
        "w_in": f(inp["w_in"][0]),
        "cw": f(conv_w.reshape(4, 12, 128).transpose(2, 1, 0)),
        "a_log": f(inp["a_log"][0][None, :]),
        "dt_bias": f(inp["dt_bias"][0][None, :]),
        "dn_g": f(inp["dn_norm_g"][0][None, :]),
        "qn_g": f(np.asarray(inp["q_norm_g"])[0].reshape(4, 128).T),
        "kvn_g": f(np.asarray(inp["kv_norm_g"])[0].reshape(2, 128).T),
        "w_uq": f(inp["w_uq"][0]),
        "w_ukv": f(inp["w_ukv"][0]),
        "w_o": f(inp["w_o"][0]),
        "ln1_g": f(inp["ln1_g"][0][None, :]),
        "ln1_b": f(inp["ln1_b"][0][None, :]),
        "w_gate": f(inp["w_gate"][0]),
        "w_up": f(inp["w_up"][0]),
        "w_down": f(inp["w_down"][0]),
        "ln2_g": f(inp["ln2_g"][0][None, :]),
        "ln2_b": f(inp["ln2_b"][0][None, :]),
        "consts": host_consts(),
    }
    return m


def kernel(**inputs):
    inp = {k: np.asarray(v) for k, v in inputs.items()}
    if "nc" not in _NC_CACHE:
        _NC_CACHE["nc"] = build_program()
    nc = _NC_CACHE["nc"]
    in_maps = [_prep_inputs(inp, b) for b in range(8)]
    res = run_bass_kernel_spmd(nc, in_maps, core_ids=list(range(8)))
    return np.stack([np.asarray(r["out"]) for r in res.results], axis=0).astype(np.float32)
```

```python
import contextlib
import math
import numpy as np
import concourse.bass as bass
import concourse.mybir as mybir
from concourse.bass_utils import run_bass_kernel_spmd

F32 = mybir.dt.float32
BF16 = mybir.dt.bfloat16
I32 = mybir.dt.int32
AF = mybir.ActivationFunctionType
ALU = mybir.AluOpType

ENGS = ("pe", "act", "dve", "pool", "sp")
NDSEM = 8
EPOCH = 20000

D = 1024
SEQ = 4096
NT = SEQ // 128
NB = SEQ // 512
DFF = 2816
NFC = DFF // 128
ALPHA = 2.0 ** 0.25
PI = math.pi


class Sched:
    def __init__(self, nc, tag):
        self.nc = nc
        self.tag = tag
        self.ops = {e: [] for e in ENGS}
        self.last_w = {}
        self.rd_c = {}
        self.rd_d = {}
        self.ndma = {e: 0 for e in ENGS}

    def op(self, eng, fn, r=(), w=(), dma=False):
        idx = len(self.ops[eng])
        deps = set()
        for t in r:
            lw = self.last_w.get(t)
            if lw is not None:
                deps.add(lw)
        for t in w:
            lw = self.last_w.get(t)
            if lw is not None:
                deps.add(lw)
            for re_, ri in self.rd_c.get(t, {}).items():
                deps.add((re_, ri))
            for x in self.rd_d.get(t, ()):
                deps.add(x)
        o = dict(fn=fn, deps=deps, dma=dma, signal=dma)
        if dma:
            o["dma_i"] = self.ndma[eng]
            self.ndma[eng] += 1
        self.ops[eng].append(o)
        me = (eng, idx)
        for t in r:
            if dma:
                self.rd_d.setdefault(t, []).append(me)
            else:
                self.rd_c.setdefault(t, {})[eng] = idx
        for t in w:
            self.last_w[t] = me
            self.rd_c[t] = {}
            self.rd_d[t] = []
        return me

    def dma(self, q, out, in_, r=(), w=()):
        return self.op(q, lambda e: e.dma_start(out=out, in_=in_), r=r, w=w, dma=True)

    def emit(self):
        nc = self.nc
        ops = self.ops
        for eng in ENGS:
            for o in ops[eng]:
                nd = set()
                for (de, di) in o["deps"]:
                    d = ops[de][di]
                    if de == eng and eng == "pe" and not d["dma"] and not o["dma"]:
                        continue
                    nd.add((de, di))
                o["deps"] = nd
                for (de, di) in nd:
                    ops[de][di]["signal"] = True
        nsig = {}
        for eng in ENGS:
            c = 0
            for o in ops[eng]:
                if o["dma"]:
                    i = o["dma_i"]
                    o["h"] = (("d", eng, i % NDSEM), 16 * (i // NDSEM + 1))
                elif o["signal"]:
                    o["h"] = (("c", eng, c // EPOCH), c % EPOCH + 1)
                    c += 1
            nsig[eng] = c
        with contextlib.ExitStack() as st:
            sems = {}
            for eng in ENGS:
                for ep in range((nsig[eng] + EPOCH - 1) // EPOCH):
                    sems[("c", eng, ep)] = nc.alloc_semaphore(name=f"{self.tag}c_{eng}_{ep}")
                for k in range(min(NDSEM, self.ndma[eng])):
                    sems[("d", eng, k)] = nc.alloc_semaphore(name=f"{self.tag}d_{eng}_{k}")
            self.sem_handles = list(sems.values())
            block = st.enter_context(nc.Block())

            def run(eng, e):
                known = {}
                for o in ops[eng]:
                    waits = {}
                    for (de, di) in o["deps"]:
                        sk, v = ops[de][di]["h"]
                        if waits.get(sk, 0) < v:
                            waits[sk] = v
                    if o["dma"] and o["dma_i"] >= NDSEM:
                        sk = ("d", eng, o["dma_i"] % NDSEM)
                        v = 16 * (o["dma_i"] // NDSEM)
                        if waits.get(sk, 0) < v:
                            waits[sk] = v
                    for sk, v in waits.items():
                        if known.get(sk, 0) >= v:
                            continue
                        e.wait_ge(sems[sk], v)
                        known[sk] = v
                    ins = o["fn"](e)
                    if o["dma"]:
                        ins.then_inc(sems[o["h"][0]], 16)
                    elif o["signal"]:
                        ins.then_inc(sems[o["h"][0]], 1)
                n = self.ndma[eng]
                for k in range(min(NDSEM, n)):
                    cnt = (n - 1 - k) // NDSEM + 1
                    if known.get(("d", eng, k), 0) < 16 * cnt:
                        e.wait_ge(sems[("d", eng, k)], 16 * cnt)

            @block.sync
            def _(e):
                run("sp", e)

            @block.tensor
            def _(e):
                run("pe", e)

            @block.scalar
            def _(e):
                run("act", e)

            @block.vector
            def _(e):
                run("dve", e)

            @block.gpsimd
            def _(e):
                run("pool", e)


class Banks:
    def __init__(self, tiles):
        self.tiles = tiles
        self.free = list(range(len(tiles)))

    def get(self):
        k = self.free.pop(0)
        return k

    def put(self, k):
        self.free.append(k)


class Pass:
    def __init__(self, nc, tag):
        self.nc = nc
        self.tag = tag
        self.S = Sched(nc, tag)
        self.st = contextlib.ExitStack()
        self.n = 0

    def sb(self, shape, dt, name=None):
        self.n += 1
        return self.st.enter_context(self.nc.sbuf_tensor(f"{self.tag}_{name or 't'}{self.n}", list(shape), dt))

    def psum_banks(self, nf=6):
        self.pf = [self.st.enter_context(self.nc.psum_tensor(f"{self.tag}_pf{k}", [128, 512], F32)) for k in range(nf)]
        self.pbs = [self.st.enter_context(self.nc.psum_tensor(f"{self.tag}_pb{k}", [128, 1024], BF16)) for k in range(8 - nf)]
        self.banks = Banks(self.pf)
        self.pbi = 0

    def bget(self):
        k = self.banks.get()
        return k, self.pf[k], ("pf", k)

    def bput(self, k):
        self.banks.put(k)

    def pbhalf(self):
        self.pbi = (self.pbi + 1) % len(self.pbs)
        return self.pbs[self.pbi][:, 0:512], ("pb", self.pbi)

    def mm(self, out, lhsT, rhs, start, stop, r, w):
        self.S.op("pe", lambda e: e.matmul(out, lhsT=lhsT, rhs=rhs, start=start, stop=stop), r=r, w=w)

    def tr(self, out, in_, ident, r, w):
        self.S.op("pe", lambda e: e.transpose(out=out, in_=in_, identity=ident), r=r, w=w)

    def act(self, out, in_, func, r, w, bias=None, scale=None, accum_out=None):
        kw = {}
        if bias is not None:
            kw["bias"] = bias
        if scale is not None:
            kw["scale"] = scale
        if accum_out is not None:
            kw["accum_out"] = accum_out
        self.S.op("act", lambda e: e.activation(out=out, in_=in_, func=func, **kw), r=r, w=w)

    def tt(self, eng, out, in0, in1, op, r, w):
        self.S.op(eng, lambda e: e.tensor_tensor(out=out, in0=in0, in1=in1, op=op), r=r, w=w)

    def ts(self, eng, out, in0, s1, op0, r, w, s2=None, op1=None):
        if op1 is None:
            self.S.op(eng, lambda e: e.tensor_scalar(out=out, in0=in0, scalar1=s1, scalar2=None, op0=op0), r=r, w=w)
        else:
            self.S.op(eng, lambda e: e.tensor_scalar(out=out, in0=in0, scalar1=s1, scalar2=s2, op0=op0, op1=op1), r=r, w=w)

    def stt(self, eng, out, in0, scalar, in1, op0, op1, r, w):
        self.S.op(eng, lambda e: e.scalar_tensor_tensor(out=out, in0=in0, scalar=scalar, in1=in1, op0=op0, op1=op1), r=r, w=w)

    def cp(self, eng, out, in_, r, w):
        if eng == "act":
            self.S.op("act", lambda e: e.copy(out=out, in_=in_), r=r, w=w)
        else:
            self.S.op(eng, lambda e: e.tensor_copy(out=out, in_=in_), r=r, w=w)

    def memset(self, eng, ap, v, w):
        self.S.op(eng, lambda e: e.memset(ap, v), w=w)

    def recip(self, out, in_, r, w):
        self.S.op("dve", lambda e: e.reciprocal(out=out, in_=in_), r=r, w=w)

    def dma(self, q, out, in_, r=(), w=()):
        self.S.dma(q, out, in_, r=r, w=w)

    def finish(self):
        self.S.emit()
        self.st.close()
        self.nc.all_engine_barrier()
        self.nc.clear_and_free_semaphores(self.S.sem_handles)
        self.nc.all_engine_barrier()


def bc(ap, shape):
    return ap.to_broadcast(list(shape))


C_ID, C_U, C_LT, C_SLT, C_BO, C_UT, C_ONE, C_M16, C_D32, C_D64, C_MISC = [i * 128 for i in range(11)]
NCONST = 11 * 128 + 16


def host_consts():
    c = np.zeros((128, NCONST), np.float32)
    i = np.arange(128)
    same = (i[:, None] // 64) == (i[None, :] // 64)
    c[:, C_ID:C_ID + 128] = np.eye(128)
    c[:, C_U:C_U + 128] = same & (i[:, None] <= i[None, :])
    c[:, C_LT:C_LT + 128] = same & (i[:, None] >= i[None, :])
    c[:, C_SLT:C_SLT + 128] = same & (i[:, None] > i[None, :])
    c[:, C_BO:C_BO + 128] = same
    c[:, C_UT:C_UT + 128] = (i[:, None] <= i[None, :])
    c[:, C_ONE:C_ONE + 128] = 1.0
    m16 = (i[:, None] // 16) == (i[None, :] // 16)
    m32 = (i[:, None] // 32) == (i[None, :] // 32)
    c[:, C_M16:C_M16 + 128] = m16
    c[:, C_D32:C_D32 + 128] = m32 & ~m16
    c[:, C_D64:C_D64 + 128] = same & ~m32
    inv_freq = (1.0 / (np.float32(10000.0) ** (np.arange(0, 64, 2, dtype=np.float32) / np.float32(64)))).astype(np.float32)
    c[0:32, C_MISC + 0] = inv_freq
    c[32:64, C_MISC + 0] = inv_freq
    c[0:32, C_MISC + 1] = -1.0
    c[32:64, C_MISC + 1] = 1.0
    c[0:64, C_MISC + 2] = 1.0
    c[64:128, C_MISC + 3] = 1.0
    return c


WCH = {}
_n = 0
for _name, _cnt in (("in", 6), ("uq", 2), ("ukv", 2), ("o", 2), ("gate", 6), ("up", 6), ("down", 6)):
    WCH[_name] = (_n, _cnt)
    _n += _cnt
NWCH = _n


def build_program(parts=("mix", "gdn", "mla", "ffn"), dbg=False, stop=None, p0skip=()):
    nc = bass.Bass("TRN2", target_bir_lowering=False)

    def din(name, shape, dt=F32):
        return nc.dram_tensor(name, list(shape), dt, kind="ExternalInput").ap()

    x = din("x", [SEQ, D])
    cT = din("cT", [128, 8])
    pos = din("pos", [1, SEQ], I32)
    w_ada = din("w_ada", [D, 6 * D])
    b_ada = din("b_ada", [1, 6 * D])
    w_in = din("w_in", [D, 2888])
    cw = din("cw", [128, 12, 4])
    a_log = din("a_log", [1, 4])
    dt_bias = din("dt_bias", [1, 4])
    dn_g = din("dn_g", [1, 128])
    qn_g = din("qn_g", [128, 4])
    kvn_g = din("kvn_g", [128, 2])
    w_uq = din("w_uq", [512, 768])
    w_ukv = din("w_ukv", [256, 1024])
    w_o = din("w_o", [D, D])
    ln1_g = din("ln1_g", [1, D])
    ln1_b = din("ln1_b", [1, D])
    w_gate = din("w_gate", [D, DFF])
    w_up = din("w_up", [D, DFF])
    w_down = din("w_down", [DFF, D])
    ln2_g = din("ln2_g", [1, D])
    ln2_b = din("ln2_b", [1, D])
    consts = din("consts", [128, NCONST])
    out = nc.dram_tensor("out", [SEQ, D], F32, kind="ExternalOutput").ap()

    def dscr(name, shape, dt):
        return nc.dram_tensor(name, list(shape), dt, kind="Internal").ap()

    wsc = dscr("wsc", [NWCH, 128, 8 * 512], BF16)
    modsc = dscr("modsc", [128, 6 * D], F32)
    hTs = dscr("hTs", [NB, 128, 8 * 512], BF16)
    odn = dscr("odn", [NB, 128, 4 * 512], BF16)
    x1s = dscr("x1s", [SEQ, D], F32)
    dbg_t = nc.dram_tensor("dbg", [128, 4096], F32, kind="ExternalOutput").ap() if dbg else None

    BG_FFN = ("mix" in parts) and ("gdn" in parts)

    P = Pass(nc, "p0")
    P.psum_banks()
    cst = P.sb([128, NCONST], F32)
    P.dma("sp", cst[:], consts, w=["cst"])
    stg = [P.sb([128, 8, 512], F32) for _ in range(2)]
    stb = [P.sb([128, 8, 512], BF16) for _ in range(2)]
    qg_t = P.sb([128, 4], F32)
    kvg_t = P.sb([128, 2], F32)
    P.dma("sp", qg_t[:], qn_g, w=["qg"])
    P.dma("sp", kvg_t[:], kvn_g, w=["kvg"])
    cnt = [0]

    def prep_chunk(ch, pieces, kc_n, scale=None, sc_tok=None):
        if scale is not None and "scaled" in p0skip:
            return
        i = cnt[0] % 2
        cnt[0] += 1
        s, b = stg[i], stb[i]
        tot = sum(p[1].shape[1] for p in pieces)
        if tot < 512:
            P.memset("pool", s[:, 0:kc_n, tot:512], 0.0, w=[("stg", i)])
        for k, (dc, src) in enumerate(pieces):
            ncol = src.shape[1]
            q = "sp" if k % 2 == 0 else "act"
            P.dma(q, s[:, 0:kc_n, dc:dc + ncol], src.rearrange("(k p) n -> p k n", p=128), w=[("stg", i)])
        for kc in range(kc_n):
            eng = ("act", "dve", "pool")[kc % 3] if scale is None else "act"
            if scale is None:
                P.cp(eng, b[:, kc, :], s[:, kc, :], r=[("stg", i)], w=[("stb", i)])
            else:
                P.act(b[:, kc, :], s[:, kc, :], AF.Copy, scale=scale[:, kc:kc + 1], r=[("stg", i), sc_tok], w=[("stb", i)])
        P.dma("pool", wsc[ch].rearrange("p (k n) -> p k n", k=8)[:, 0:kc_n, :], b[:, 0:kc_n, :], r=[("stb", i)], w=[("wsc", ch)])

    if "w" not in p0skip:
        c0 = WCH["in"][0]
        prep_chunk(c0 + 0, [(0, w_in[:, 0:512])], 8)
        prep_chunk(c0 + 1, [(0, w_in[:, 512:1024])], 8)
        prep_chunk(c0 + 2, [(0, w_in[:, 1024:1536])], 8)
        prep_chunk(c0 + 3, [(0, w_in[:, 1536:2048])], 8)
        prep_chunk(c0 + 4, [(0, w_in[:, 2056:2568])], 8)
        if "ch5" not in p0skip:
            prep_chunk(c0 + 5, [(0, w_in[:, 2568:2824]), (256, w_in[:, 2824:2888]), (320, w_in[:, 2856:2888]),
                                (352, w_in[:, 2824:2856]), (384, w_in[:, 2048:2056])], 8)
        c0 = WCH["uq"][0]
        prep_chunk(c0 + 0, [(h * 128, w_uq[:, h * 192:h * 192 + 128]) for h in range(4)], 4, scale=qg_t, sc_tok="qg")
        prep_chunk(c0 + 1, [(h * 64, w_uq[:, h * 192 + 128:h * 192 + 192]) for h in range(4)]
                   + [(256 + h * 64, w_uq[:, h * 192 + 160:h * 192 + 192]) for h in range(4)]
                   + [(256 + h * 64 + 32, w_uq[:, h * 192 + 128:h * 192 + 160]) for h in range(4)], 4, scale=qg_t, sc_tok="qg")
        c0 = WCH["ukv"][0]
        prep_chunk(c0 + 0, [(h * 128, w_ukv[:, h * 256:h * 256 + 128]) for h in range(4)], 2, scale=kvg_t, sc_tok="kvg")
        prep_chunk(c0 + 1, [(h * 128, w_ukv[:, h * 256 + 128:h * 256 + 256]) for h in range(4)], 2, scale=kvg_t, sc_tok="kvg")
        c0 = WCH["o"][0]
        for hf in range(2):
            prep_chunk(c0 + hf, [(0, w_o[:, hf * 512:(hf + 1) * 512])], 8)
        if not BG_FFN:
            for nm, wt in (("gate", w_gate), ("up", w_up)):
                c0 = WCH[nm][0]
                for c in range(6):
                    prep_chunk(c0 + c, [(0, wt[:, c * 512:min((c + 1) * 512, DFF)])], 8)
            c0 = WCH["down"][0]
            for g in range(3):
                nk = 8 if g < 2 else 6
                for hf in range(2):
                    prep_chunk(c0 + g * 2 + hf, [(0, w_down[g * 1024:g * 1024 + nk * 128, hf * 512:(hf + 1) * 512])], nk)

    if "mod" not in p0skip:
        cT_t = P.sb([128, 8], F32)
        cact = P.sb([128, 8], F32)
        cb = P.sb([128, 8, 128], F32)
        P.dma("sp", cT_t[:], cT, w=["cT"])
        P.act(cact[:], cT_t[:], AF.Silu, r=["cT"], w=["cact"])
        P.cp("dve", cb[:], bc(cact[:].unsqueeze(2), [128, 8, 128]), r=["cact"], w=["cb"])
        wst = [P.sb([128, 8, 512], F32) for _ in range(2)]
        bst = [P.sb([128, 512], F32) for _ in range(2)]
        mo = [P.sb([128, 512], F32) for _ in range(2)]
        for nb in range(12):
            i = nb % 2
            P.dma("sp", wst[i][:], w_ada[:, nb * 512:(nb + 1) * 512].rearrange("(k p) n -> p k n", p=128), w=[("wst", i)])
            P.dma("act", bst[i][:], bc(b_ada[0:1, nb * 512:(nb + 1) * 512], [128, 512]), w=[("bst", i)])
            k, pt, ptok = P.bget()
            for kc in range(8):
                P.mm(pt[:], cb[:, kc, :], wst[i][:, kc, :], kc == 0, kc == 7, r=["cb", ("wst", i)], w=[ptok])
            P.tt("dve", mo[i][:], pt[:], bst[i][:], ALU.add, r=[ptok, ("bst", i)], w=[("mo", i)])
            P.bput(k)
            if nb in (2, 3, 8, 9):
                P.ts("dve", mo[i][:], mo[i][:], 1.0, ALU.add, r=[("mo", i)], w=[("mo", i)])
            P.dma("pool", modsc[:, nb * 512:(nb + 1) * 512], mo[i][:], r=[("mo", i)], w=[("modsc", nb)])
    P.finish()
    if stop == "p0":
        return nc

    class Ring:
        def __init__(self, P, n=3):
            self.P = P
            self.bufs = [P.sb([128, 8, 512], BF16) for _ in range(n)]
            self.i = 0
            self.n = n

        def load(self, ch, q="sp", nk=8):
            i = self.i % self.n
            self.i += 1
            self.P.dma(q, self.bufs[i][:, 0:nk, :], wsc[ch].rearrange("p (k n) -> p k n", k=8)[:, 0:nk, :], w=[("ring", i)])
            return self.bufs[i], ("ring", i)

    def load_consts(P):
        cst = P.sb([128, NCONST], F32)
        P.dma("sp", cst[:], consts, w=["cst"])
        idb = P.sb([128, 128], BF16)
        P.cp("dve", idb[:], cst[:, C_ID:C_ID + 128], r=["cst"], w=["idb"])
        return cst, idb

    def make_hT(P, xt, xtok, j, s1, sh, modtok, idb, hT, hTtok, tmpf, hbf, eng="dve"):
        P.tt(eng, tmpf[:], xt, s1[:], ALU.mult, r=[xtok, modtok], w=["tmpf"])
        P.tt(eng, hbf[:], tmpf[:], sh[:], ALU.add, r=["tmpf", modtok], w=["hbf"])
        for half in range(2):
            kb_, pb_, phtok = P.bget()
            ph = pb_[:, 0:256].bitcast(BF16)
            for kk in range(4):
                kc = half * 4 + kk
                P.tr(ph[:, kk * 128:(kk + 1) * 128], hbf[:, kc * 128:(kc + 1) * 128], idb[:], r=["hbf", "idb"], w=[phtok])
            P.cp("act", hT[:, half * 4:half * 4 + 4, j * 128:(j + 1) * 128],
                 ph.rearrange("p (k n) -> p k n", k=4), r=[phtok], w=[hTtok])
            P.bput(kb_)

    def ln_stats(P, y, ytok, junk, st, sttok):
        P.S.op("dve", lambda e: e.tensor_scalar(out=junk[:], in0=y, scalar1=1.0 / D, scalar2=0.0, op0=ALU.mult, op1=ALU.add, accum_out=st[:, 2:3]),
               r=[ytok], w=["junk", sttok])
        P.S.op("dve", lambda e: e.scalar_tensor_tensor(out=junk[:], in0=y, scalar=1.0 / D, in1=y, op0=ALU.mult, op1=ALU.mult,
                                                       accum_out=st[:, 1:2]), r=[ytok, sttok], w=["junk", sttok])
        P.tt("dve", st[:, 3:4], st[:, 2:3], st[:, 2:3], ALU.mult, r=[sttok], w=[sttok])
        P.tt("dve", st[:, 4:5], st[:, 1:2], st[:, 3:4], ALU.subtract, r=[sttok], w=[sttok])

    def ln_apply(P, y, ytok, g_b, b_b, gbtok, outt, outtok, junk, st, sttok):
        P.act(st[:, 5:6], st[:, 4:5], AF.Ln, bias=1e-5, r=[sttok], w=[sttok])
        P.act(st[:, 6:7], st[:, 5:6], AF.Exp, scale=-0.5, r=[sttok], w=[sttok])
        P.ts("dve", junk[:], y, st[:, 2:3], ALU.subtract, s2=st[:, 6:7], op1=ALU.mult, r=[ytok, sttok], w=["junk"])
        P.tt("pool", junk[:], junk[:], g_b[:], ALU.mult, r=["junk", gbtok], w=["junk"])
        P.tt("dve", outt, junk[:], b_b[:], ALU.add, r=["junk", gbtok], w=[outtok])

    SCALE = 1.0 / math.sqrt(192.0)
    C1 = 6.28125
    C2 = 2.0 * PI - 6.28125

    def build_p1a():
        GDN = "gdn" in parts
        NSET = 2
        P = Pass(nc, "p1a")
        P.psum_banks(8)
        cst, idb = load_consts(P)
        idf = cst[:, C_ID:C_ID + 128]
        Umat = cst[:, C_U:C_U + 128]
        BOm = cst[:, C_BO:C_BO + 128]
        onesf = cst[:, C_ONE:C_ONE + 128]
        CI = cst[:, C_MISC + 2:C_MISC + 4]
        s1m = P.sb([128, D], F32)
        shm = P.sb([128, D], F32)
        P.dma("sp", shm[:], modsc[:, 0:D], w=["mod"])
        P.dma("sp", s1m[:], modsc[:, D:2 * D], w=["mod"])
        xt = [P.sb([128, D], F32) for _ in range(2)]
        hb = P.sb([128, 8, 512], BF16)
        tmpf = P.sb([128, D], F32)
        hbf = P.sb([128, D], BF16)
        zt = P.sb([128, 4 * 512], BF16)
        mixd = zt[:].rearrange("p (h n) -> p h n", h=4)
        if not GDN:
            P.memset("pool", zt[:], 0.0, w=["mixd"])
        else:
            ring = Ring(P, 2)
            I3 = P.sb([128, 4, 128], F32)
            LT3 = P.sb([128, 4, 128], F32)
            SLT3 = P.sb([128, 4, 128], F32)
            for h in range(4):
                P.cp("pool", I3[:, h, :], idf, r=["cst"], w=["c3"])
                P.cp("pool", LT3[:, h, :], cst[:, C_LT:C_LT + 128], r=["cst"], w=["c3"])
                P.cp("pool", SLT3[:, h, :], cst[:, C_SLT:C_SLT + 128], r=["cst"], w=["c3"])
            mk16 = P.sb([128, 128], BF16)
            mk32 = P.sb([128, 128], BF16)
            mk64 = P.sb([128, 128], BF16)
            P.cp("pool", mk16[:], cst[:, C_M16:C_M16 + 128], r=["cst"], w=["mk"])
            P.cp("pool", mk32[:], cst[:, C_D32:C_D32 + 128], r=["cst"], w=["mk"])
            P.cp("pool", mk64[:], cst[:, C_D64:C_D64 + 128], r=["cst"], w=["mk"])
            cwt = P.sb([128, 12, 4], F32)
            P.dma("act", cwt[:], cw, w=["cwt"])
            negA = P.sb([128, 4], F32)
            dtb = P.sb([128, 4], F32)
            dng4 = P.sb([128, 4, 128], F32)
            P.dma("act", negA[:], bc(a_log[0:1, :], [128, 4]), w=["negA"])
            P.dma("act", dtb[:], bc(dt_bias[0:1, :], [128, 4]), w=["dtb"])
            for h in range(4):
                P.dma("act", dng4[:, h, :], bc(dn_g[0:1, :], [128, 128]), w=["dng4"])
            P.act(negA[:], negA[:], AF.Exp, r=["negA"], w=["negA"])
            P.ts("dve", negA[:], negA[:], -1.0, ALU.mult, r=["negA"], w=["negA"])
            Sst = [P.sb([128, 4, 128], F32) for _ in range(2)]
            Sbf = P.sb([128, 4, 128], BF16)
            P.memset("dve", Sst[0][:], 0.0, w=[("S", 0)])
            P.memset("dve", Sbf[:], 0.0, w=["Sbf"])
            prec = P.sb([128, 12, 3], F32)
            P.memset("dve", prec[:], 0.0, w=[("prec", ci) for ci in range(12)])
            pre4 = P.sb([128, 4, 515], F32)
            qkvs2 = [P.sb([128, 12, 512], F32) for _ in range(2)]
            sz2 = [P.sb([128, 4, 512], BF16) for _ in range(2)]
            cur = dict(tb=0)
            ba = P.sb([128, 4, 8], F32)
            beta2 = [P.sb([128, 4, 4], F32) for _ in range(2)]
            gg2 = [P.sb([128, 4, 4], F32) for _ in range(2)]
            sq8 = P.sb([128, 8, 512], BF16)
            lnb = [P.sb([128, 512], F32) for _ in range(2)]
            ones_bf = P.sb([128, 128], BF16)
            P.cp("dve", ones_bf[:], onesf, r=["cst"], w=["ones_bf"])
            jk = P.sb([128, 128], F32)
            f3 = lambda: P.sb([128, 4, 128], F32)
            b3 = lambda: P.sb([128, 4, 128], BF16)
            sets = []
            for k in range(NSET):
                bs = dict(A1=f3(), A2=f3(), A3=f3(), A4=f3(),
                          XA=b3(), XB=b3(), YA=b3(), YB=b3(), Xo1=b3(), Xo2=b3(), Yo1=b3(), Yo2=b3(), Pb=b3(), Qb=b3(),
                          RHSw=b3(), RHSu=b3(), qgT=b3(), kd=b3(), attT=b3(), H4=b3(), vnew=b3(),
                          sc=P.sb([128, 32], F32), egl=P.sb([128, 8], F32), oss=P.sb([128, 8], F32), k=k)
                sets.append(bs)
            scan_state = dict(c=0)

        def v3(t):
            return t[:].rearrange("p (h n) -> p h n", h=4)

        def col3(ap):
            return bc(ap.unsqueeze(2), [128, 4, 128])

        def prep_gen(j, bs):
            bp = cur["tb"] % 2
            qkvs, beta, gg = qkvs2[bp], beta2[bp], gg2[bp]
            k_ = bs["k"]
            T = lambda n: (n, k_)
            ts_ = slice(j * 128, (j + 1) * 128)
            G, dec, egcb, mb = bs["A1"], bs["A2"], bs["A3"], bs["A4"]
            tX, tA = bs["A1"], bs["A3"]
            RHSw, RHSu = bs["RHSw"], bs["RHSu"]
            qgT, kd, attT, attb, wT = bs["qgT"], bs["kd"], bs["attT"], bs["H4"], bs["H4"]
            uu = bs["A3"]
            sc, egl = bs["sc"], bs["egl"]
            P.cp("dve", G[:], col3(gg[:, j, :]), r=[("gg", bp)], w=[T("A1")])
            ksm, psm, psmtok = P.bget()
            P.mm(psm[:, 0:4], Umat, gg[:, j, :], True, True, r=["cst", ("gg", bp)], w=[psmtok])
            P.mm(psm[:, 4:8], BOm, gg[:, j, :], True, True, r=["cst", ("gg", bp)], w=[psmtok])
            for h in range(4):
                P.mm(psm[:, 8 + 2 * h:10 + 2 * h], G[:, h, :], CI, True, True, r=[T("A1"), "cst"], w=[psmtok])
            kgc, pgc, pgctok = P.bget()
            for h in range(4):
                P.mm(pgc[:, h * 128:(h + 1) * 128], G[:, h, :], Umat, True, True, r=[T("A1"), "cst"], w=[pgctok])
            P.cp("dve", sc[:, 0:4], psm[:, 0:4], r=[psmtok], w=[T("sc0")])
            P.act(sc[:, 4:8], psm[:, 0:4], AF.Exp, r=[psmtok], w=[T("sc1")])
            P.tt("dve", sc[:, 8:12], psm[:, 4:8], sc[:, 0:4], ALU.subtract, r=[psmtok, T("sc0")], w=[T("sc2")])
            P.act(sc[:, 12:16], sc[:, 8:12], AF.Exp, r=[T("sc2")], w=[T("sc3")])
            P.tt("dve", sc[:, 16:20], beta[:, j, :], sc[:, 4:8], ALU.mult, r=[("beta", bp), T("sc1")], w=[T("sc4")])
            P.ts("dve", sc[:, 20:24], beta[:, j, :], -1.0, ALU.mult, r=[("beta", bp)], w=[T("sc5")])
            P.act(egl[:], psm[:, 8:16], AF.Exp, r=[psmtok], w=[T("egl")])
            P.bput(ksm)
            yield
            P.tt("dve", dec[:], col3(sc[:, 0:4]), v3(pgc), ALU.subtract, r=[T("sc0"), pgctok], w=[T("A2")])
            P.ts("dve", dec[:], dec[:], 0.0, ALU.min, r=[T("A2")], w=[T("A2")])
            P.act(dec[:], dec[:], AF.Exp, r=[T("A2")], w=[T("A2")])
            P.act(egcb[:], v3(pgc), AF.Exp, r=[pgctok], w=[T("A3")])
            P.bput(kgc)
            yield
            yield
            kkt, pkt, pkttok = P.bget()
            kvt, pvt, pvttok = P.bget()
            for h in range(4):
                P.tr(pkt[:, h * 128:(h + 1) * 128], qkvs[:, 4 + h, ts_], idf, r=[("qkvs", bp, 4 + h), "cst"], w=[pkttok])
            for h in range(4):
                P.tr(pvt[:, h * 128:(h + 1) * 128], qkvs[:, 8 + h, ts_], idf, r=[("qkvs", bp, 8 + h), "cst"], w=[pvttok])
            kkk, pkk, pkktok = P.bget()
            kqk, pqk, pqktok = P.bget()
            for h in range(4):
                P.mm(pkk[:, h * 128:(h + 1) * 128], qkvs[:, 4 + h, ts_], qkvs[:, 4 + h, ts_], True, True, r=[("qkvs", bp, 4 + h)], w=[pkktok])
            for h in range(4):
                P.mm(pqk[:, h * 128:(h + 1) * 128], qkvs[:, h, ts_], qkvs[:, 4 + h, ts_], True, True, r=[("qkvs", bp, h), ("qkvs", bp, 4 + h)], w=[pqktok])
            P.tt("dve", qgT[:], qkvs[:, 0:4, ts_], egcb[:], ALU.mult, r=[("qkvs", bp, ci) for ci in range(4)] + [T("A3")], w=[T("qgT")])
            for h in range(4):
                hs = slice(h * 128, (h + 1) * 128)
                P.act(RHSw[:, h, :], pkt[:, hs], AF.Copy, scale=sc[:, 16 + h:17 + h], r=[pkttok, T("sc4")], w=[T("RHSw")])
                P.act(kd[:, h, :], pkt[:, hs], AF.Copy, scale=sc[:, 12 + h:13 + h], r=[pkttok, T("sc3")], w=[T("kd")])
                P.act(RHSu[:, h, :], pvt[:, hs], AF.Copy, scale=beta[:, j, h:h + 1], r=[pvttok, ("beta", bp)], w=[T("RHSu")])
            P.bput(kkt)
            P.bput(kvt)
            yield
            P.tt("pool", mb[:], SLT3[:], col3(sc[:, 20:24]), ALU.mult, r=["c3", T("sc5")], w=[T("A4")])
            P.tt("dve", tX[:], v3(pkk), dec[:], ALU.mult, r=[pkktok, T("A2"), T("A1")], w=[T("A1")])
            P.bput(kkk)
            yield
            P.tt("dve", tA[:], v3(pqk), dec[:], ALU.mult, r=[pqktok, T("A2"), T("A3"), T("qgT")], w=[T("A3")])
            P.bput(kqk)
            yield
            X0b = bs["XB"]
            Xs = [bs["XA"], bs["XB"]]
            Ys = [bs["YA"], bs["YB"]]
            xs_t = [T("XA"), T("XB")]
            ys_t = [T("YA"), T("YB")]
            Xo1, Xo2, Yo1, Yo2, Pb, Qb = bs["Xo1"], bs["Xo2"], bs["Yo1"], bs["Yo2"], bs["Pb"], bs["Qb"]
            M1b, M2b = bs["XA"], bs["YA"]
            m3 = lambda mk: bc(mk[:].unsqueeze(1), [128, 4, 128])
            P.tt("pool", X0b[:], tX[:], mb[:], ALU.mult, r=[T("A1"), T("A4")], w=[T("XB")])
            P.tt("pool", attb[:], tA[:], LT3[:], ALU.mult, r=[T("A3"), "c3"], w=[T("H4")])
            kb_, pb_, pTtok = P.bget()
            pTb = pb_[:, 0:256].bitcast(BF16)
            for h in range(4):
                P.tr(pTb[:, h * 128:(h + 1) * 128], X0b[:, h, :], idb[:], r=[T("XB"), "idb"], w=[pTtok])
            pT3 = pTb.rearrange("p (h n) -> p h n", h=4)
            P.tt("dve", Ys[0][:], pT3, m3(mk16), ALU.mult, r=[pTtok, "mk"], w=[T("YA")])
            P.tt("dve", Yo1[:], pT3, m3(mk32), ALU.mult, r=[pTtok, "mk"], w=[T("Yo1")])
            P.bput(kb_)
            yield
            P.tt("dve", Xs[0][:], X0b[:], m3(mk16), ALU.mult, r=[T("XB"), "mk"], w=[T("XA")])
            P.tt("pool", Xo1[:], X0b[:], m3(mk32), ALU.mult, r=[T("XB"), "mk"], w=[T("Xo1")])
            P.cp("pool", Xo2[:], X0b[:], r=[T("XB")], w=[T("Xo2")])
            kb2, pb2, phtok = P.bget()
            ph = pb2[:, 0:256].bitcast(BF16)
            for h in range(4):
                P.tr(ph[:, h * 128:(h + 1) * 128], attb[:, h, :], idb[:], r=[T("H4"), "idb"], w=[phtok])
            P.cp("act", attT[:], ph.rearrange("p (h n) -> p h n", h=4), r=[phtok], w=[T("attT")])
            P.bput(kb2)
            yield
            P.tt("pool", Qb[:], I3[:], Xs[0][:], ALU.add, r=["c3", T("XA")], w=[T("Qb")])
            P.tt("dve", Pb[:], I3[:], Ys[0][:], ALU.add, r=["c3", T("YA")], w=[T("Pb")])
            yield
            a = 0
            for rnd in range(1, 5):
                b = 1 - a
                do_sq = rnd <= 3
                do_pr = rnd >= 2
                if do_sq:
                    kY, pY, pYtok = P.bget()
                    for h in range(4):
                        P.mm(pY[:, h * 128:(h + 1) * 128], Xs[a][:, h, :], Ys[a][:, h, :], True, True, r=[xs_t[a], ys_t[a]], w=[pYtok])
                    kX, pX, pXtok = P.bget()
                    for h in range(4):
                        P.mm(pX[:, h * 128:(h + 1) * 128], Ys[a][:, h, :], Xs[a][:, h, :], True, True, r=[xs_t[a], ys_t[a]], w=[pXtok])
                if do_pr:
                    kP, pP, pPtok = P.bget()
                    for h in range(4):
                        P.mm(pP[:, h * 128:(h + 1) * 128], Qb[:, h, :], Ys[a][:, h, :], True, True, r=[T("Qb"), ys_t[a]], w=[pPtok])
                    kQ, pQ, pQtok = P.bget()
                    for h in range(4):
                        P.mm(pQ[:, h * 128:(h + 1) * 128], Pb[:, h, :], Xs[a][:, h, :], True, True, r=[T("Pb"), xs_t[a]], w=[pQtok])
                if do_sq:
                    P.cp("act", Ys[b][:], v3(pY), r=[pYtok], w=[ys_t[b]])
                    P.bput(kY)
                    P.cp("act", Xs[b][:], v3(pX), r=[pXtok], w=[xs_t[b]])
                    P.bput(kX)
                if do_pr:
                    P.tt("dve", Pb[:], Pb[:], v3(pP), ALU.add, r=[T("Pb"), pPtok], w=[T("Pb")])
                    P.bput(kP)
                    P.tt("dve", Qb[:], Qb[:], v3(pQ), ALU.add, r=[T("Qb"), pQtok], w=[T("Qb")])
                    P.bput(kQ)
                a = b
                yield
            for lvl, (Xo, Yo, xo_t, yo_t) in enumerate(((Xo1, Yo1, T("Xo1"), T("Yo1")), (Xo2, Yo2, T("Xo2"), T("Yo2")))):
                last = lvl == 1
                k2, p2, p2tok = P.bget()
                for h in range(4):
                    P.mm(p2[:, h * 128:(h + 1) * 128], Xo[:, h, :], Pb[:, h, :], True, True, r=[xo_t, T("Pb")], w=[p2tok])
                if not last:
                    k1, p1, p1tok = P.bget()
                    for h in range(4):
                        P.mm(p1[:, h * 128:(h + 1) * 128], Yo[:, h, :], Qb[:, h, :], True, True, r=[yo_t, T("Qb")], w=[p1tok])
                if last:
                    P.tt("dve", M2b[:], v3(p2), m3(mk64), ALU.mult, r=[p2tok, "mk"], w=[T("YA")])
                else:
                    P.cp("act", M2b[:], v3(p2), r=[p2tok], w=[T("YA")])
                P.bput(k2)
                if not last:
                    P.cp("act", M1b[:], v3(p1), r=[p1tok], w=[T("XA")])
                    P.bput(k1)
                yield
                kP, pP, pPtok = P.bget()
                for h in range(4):
                    P.mm(pP[:, h * 128:(h + 1) * 128], Qb[:, h, :], M2b[:, h, :], True, True, r=[T("Qb"), T("YA")], w=[pPtok])
                if not last:
                    kQ, pQ, pQtok = P.bget()
                    for h in range(4):
                        P.mm(pQ[:, h * 128:(h + 1) * 128], Pb[:, h, :], M1b[:, h, :], True, True, r=[T("Pb"), T("XA")], w=[pQtok])
                P.tt("dve", Pb[:], Pb[:], v3(pP), ALU.add, r=[T("Pb"), pPtok], w=[T("Pb")])
                P.bput(kP)
                if not last:
                    P.tt("dve", Qb[:], Qb[:], v3(pQ), ALU.add, r=[T("Qb"), pQtok], w=[T("Qb")])
                    P.bput(kQ)
                yield
            kw_, pw, pwtok = P.bget()
            ku_, pu, putok = P.bget()
            for h in range(4):
                P.mm(pw[:, h * 128:(h + 1) * 128], RHSw[:, h, :], Pb[:, h, :], True, True, r=[T("RHSw"), T("Pb")], w=[pwtok])
            for h in range(4):
                P.mm(pu[:, h * 128:(h + 1) * 128], Pb[:, h, :], RHSu[:, h, :], True, True, r=[T("RHSu"), T("Pb")], w=[putok])
            P.cp("act", wT[:], v3(pw), r=[pwtok], w=[T("H4")])
            P.cp("act", uu[:], v3(pu), r=[putok], w=[T("A3")])
            P.bput(kw_)
            P.bput(ku_)

        def scan_gen(j, bs):
            k_ = bs["k"]
            T = lambda n: (n, k_)
            qgT, kd, attT, wT, vnew, uu, egl = bs["qgT"], bs["kd"], bs["attT"], bs["H4"], bs["vnew"], bs["A3"], bs["egl"]
            ko, po, potok = P.bget()
            bs["ko"] = (ko, po, potok)
            for r_ in range(2):
                rows = slice(64 * r_, 64 * r_ + 64)
                c = scan_state["c"]
                So, Sn = Sst[c % 2], Sst[(c + 1) % 2]
                so_t, sn_t = ("S", c % 2), ("S", (c + 1) % 2)
                scan_state["c"] = c + 1
                ka, pa, patok = P.bget()
                for h in range(4):
                    P.mm(pa[rows, h * 128:(h + 1) * 128], wT[:, h, rows], Sbf[:, h, :], True, True, r=[T("H4"), "Sbf"], w=[patok])
                P.tt("dve", vnew[rows], uu[rows], v3(pa)[rows], ALU.subtract, r=[T("A3"), patok], w=[T("vnew")])
                P.bput(ka)
                yield
                for h in range(4):
                    P.mm(po[rows, h * 128:(h + 1) * 128], qgT[:, h, rows], Sbf[:, h, :], True, False, r=[T("qgT"), "Sbf"], w=[potok])
                    P.mm(po[rows, h * 128:(h + 1) * 128], attT[rows, h, rows], vnew[rows, h, :], False, True, r=[T("attT"), T("vnew")], w=[potok])
                ks, ps_, pstok = P.bget()
                for h in range(4):
                    P.mm(ps_[:, h * 128:(h + 1) * 128], kd[rows, h, :], vnew[rows, h, :], True, True, r=[T("kd"), T("vnew")], w=[pstok])
                for h in range(4):
                    egc_ = egl[:, 2 * h + r_:2 * h + r_ + 1]
                    P.stt("dve", Sbf[:, h, :], So[:, h, :], egc_, ps_[:, h * 128:(h + 1) * 128], ALU.mult, ALU.add,
                          r=[so_t, T("egl"), pstok], w=["Sbf"])
                for h in range(4):
                    egc_ = egl[:, 2 * h + r_:2 * h + r_ + 1]
                    P.stt("dve", Sn[:, h, :], So[:, h, :], egc_, ps_[:, h * 128:(h + 1) * 128], ALU.mult, ALU.add,
                          r=[so_t, T("egl"), pstok], w=[sn_t])
                P.bput(ks)
                yield

        def post_gen(j, bs):
            k_ = bs["k"]
            T = lambda n: (n, k_)
            ts_ = slice(j * 128, (j + 1) * 128)
            og, ogb, oss = bs["A4"], bs["H4"], bs["oss"]
            ko, po, potok = bs["ko"]
            for h in range(4):
                P.act(jk[:], po[:, h * 128:(h + 1) * 128], AF.Square, accum_out=oss[:, h:h + 1], r=[potok], w=["jk", T("oss")])
            P.act(oss[:, 4:8], oss[:, 0:4], AF.Ln, scale=1.0 / 128.0, bias=1e-6, r=[T("oss")], w=[T("oss")])
            P.act(oss[:, 4:8], oss[:, 4:8], AF.Exp, scale=-0.5, r=[T("oss")], w=[T("oss")])
            P.tt("dve", og[:], v3(po), col3(oss[:, 4:8]), ALU.mult, r=[potok, T("oss"), T("A4")], w=[T("A4")])
            P.bput(ko)
            yield
            P.tt("pool", ogb[:], og[:], sz2[cur["tb"] % 2][:, j, :].rearrange("p (h n) -> p h n", h=4), ALU.mult,
                 r=[T("A4"), ("sz", cur["tb"] % 2, j), T("H4")], w=[T("H4")])
            kb_, pb_, phtok = P.bget()
            ph = pb_[:, 0:256].bitcast(BF16)
            for h in range(4):
                P.tr(ph[:, h * 128:(h + 1) * 128], ogb[:, h, :], idb[:], r=[T("H4"), "idb"], w=[phtok])
            P.cp("act", mixd[:, :, ts_], ph.rearrange("p (h n) -> p h n", h=4), r=[phtok], w=["mixd"])
            P.bput(kb_)

        def l2norm_chunk(c, ci, bp):
            qkvs = qkvs2[bp]
            i = ci % 2
            k, pt, ptok = P.bget()
            P.mm(pt[:], ones_bf[:], sq8[:, ci, :], True, True, r=["ones_bf", ("sq8", ci)], w=[ptok])
            if c == 0:
                P.act(lnb[i][:], pt[:], AF.Ln, scale=128.0, bias=128.0e-6, r=[ptok], w=[("lnb", i)])
            else:
                P.act(lnb[i][:], pt[:], AF.Ln, bias=1e-6, r=[ptok], w=[("lnb", i)])
            P.bput(k)
            P.act(lnb[i][:], lnb[i][:], AF.Exp, scale=-0.5, r=[("lnb", i)], w=[("lnb", i)])
            P.tt("dve", qkvs[:, ci, :], qkvs[:, ci, :], lnb[i][:], ALU.mult, r=[("qkvs", bp, ci), ("lnb", i)], w=[("qkvs", bp, ci)])

        def block_stage0(tb):
            sz = sz2[tb % 2]
            szp = tb % 2
            bp = tb % 2
            qkvs, beta, gg = qkvs2[bp], beta2[bp], gg2[bp]
            for c in range(3):
                W, wtok = ring.load(WCH["in"][0] + c)
                for m in range(4):
                    ci = c * 4 + m
                    k, pt, ptok = P.bget()
                    for kc in range(8):
                        P.mm(pt[:], W[:, kc, m * 128:(m + 1) * 128], hb[:, kc, :], kc == 0, kc == 7, r=[wtok, "hT"], w=[ptok])
                    P.cp("act", pre4[:, m, 3:515], pt[:], r=[ptok], w=[("pre4", m)])
                    P.bput(k)
                    P.cp("pool", pre4[:, m, 0:3], prec[:, ci, :], r=[("prec", ci)], w=[("pre4", m)])
                    P.ts("dve", qkvs[:, ci, :], pre4[:, m, 0:512], cwt[:, ci, 0:1], ALU.mult, r=[("pre4", m), "cwt"], w=[("qkvs", bp, ci)])
                    for jj in range(1, 4):
                        P.stt("dve", qkvs[:, ci, :], pre4[:, m, jj:jj + 512], cwt[:, ci, jj:jj + 1], qkvs[:, ci, :], ALU.mult, ALU.add,
                              r=[("pre4", m), "cwt", ("qkvs", bp, ci)], w=[("qkvs", bp, ci)])
                    P.cp("pool", prec[:, ci, :], pre4[:, m, 512:515], r=[("pre4", m)], w=[("prec", ci)])
                    P.act(qkvs[:, ci, :], qkvs[:, ci, :], AF.Silu, r=[("qkvs", bp, ci)], w=[("qkvs", bp, ci)])
                    if c < 2:
                        P.act(sq8[:, ci, :], qkvs[:, ci, :], AF.Square, r=[("qkvs", bp, ci)], w=[("sq8", ci)])
                    yield
            W, wtok = ring.load(WCH["in"][0] + 3)
            for j in range(4):
                k, pt, ptok = P.bget()
                for kc in range(8):
                    P.mm(pt[:], hb[:, kc, j * 128:(j + 1) * 128], W[:, kc, :], kc == 0, kc == 7, r=[wtok, "hT"], w=[ptok])
                P.act(sz[:, j, :], pt[:], AF.Silu, r=[ptok], w=[("sz", szp, j)])
                P.bput(k)
                P.tt("pool", sz[:, j, :].rearrange("p (h n) -> p h n", h=4), sz[:, j, :].rearrange("p (h n) -> p h n", h=4), dng4[:],
                     ALU.mult, r=[("sz", szp, j), "dng4"], w=[("sz", szp, j)])
                yield
            W, wtok = ring.load(WCH["in"][0] + 5)
            k, pt, ptok = P.bget()
            for j in range(4):
                for kc in range(8):
                    P.mm(pt[:, j * 8:(j + 1) * 8], hb[:, kc, j * 128:(j + 1) * 128], W[:, kc, 384:392], kc == 0, kc == 7, r=[wtok, "hT"], w=[ptok])
            P.cp("dve", ba[:], pt[:, 0:32].rearrange("p (j n) -> p j n", j=4), r=[ptok], w=["ba"])
            P.bput(k)
            P.act(beta[:], ba[:, :, 0:4], AF.Sigmoid, r=["ba"], w=[("beta", bp)])
            P.tt("dve", gg[:], ba[:, :, 4:8], bc(dtb[:].unsqueeze(1), [128, 4, 4]), ALU.add, r=["ba", "dtb"], w=[("gg", bp)])
            P.act(gg[:], gg[:], AF.Exp, r=[("gg", bp)], w=[("gg", bp)])
            P.act(gg[:], gg[:], AF.Ln, bias=1.0, r=[("gg", bp)], w=[("gg", bp)])
            P.tt("dve", gg[:], gg[:], bc(negA[:].unsqueeze(1), [128, 4, 4]), ALU.mult, r=[("gg", bp), "negA"], w=[("gg", bp)])
            yield
            for ci in range(8):
                l2norm_chunk(ci // 4, ci, bp)
                if ci % 2 == 1:
                    yield

        def bg_ffn_prep():
            stg_ = [P.sb([128, 512], F32) for _ in range(2)]
            stb_ = [P.sb([128, 512], BF16) for _ in range(2)]
            pieces = []
            for nm, wt in (("gate", w_gate), ("up", w_up)):
                for c in range(6):
                    ncol = min(512, DFF - c * 512)
                    for kc in range(8):
                        pieces.append((WCH[nm][0] + c, kc, wt[kc * 128:(kc + 1) * 128, c * 512:c * 512 + ncol], ncol))
            for g in range(3):
                nk = 8 if g < 2 else 6
                for hf in range(2):
                    for kc in range(nk):
                        r0 = g * 1024 + kc * 128
                        pieces.append((WCH["down"][0] + g * 2 + hf, kc, w_down[r0:r0 + 128, hf * 512:(hf + 1) * 512], 512))

            def load(n):
                ch, kc, src, ncol = pieces[n]
                i = n % 2
                if ncol < 512:
                    P.memset("pool", stg_[i][:, ncol:512], 0.0, w=[("bgs", i)])
                P.dma("sp", stg_[i][:, 0:ncol], src, w=[("bgs", i)])

            load(0)
            for n in range(len(pieces)):
                ch, kc, src, ncol = pieces[n]
                i = n % 2
                if n + 1 < len(pieces):
                    load(n + 1)
                P.cp("act", stb_[i][:], stg_[i][:], r=[("bgs", i)], w=[("bgb", i)])
                yield
                P.dma("act", wsc[ch][:, kc * 512:(kc + 1) * 512], stb_[i][:], r=[("bgb", i)], w=[("wscbg", ch, kc)])
                yield

        def front_gen(tb):
            for j in range(4):
                t = tb * 4 + j
                i = t % 2
                P.dma("sp", xt[i][:], x[t * 128:(t + 1) * 128, :], w=[("xt", i)])
                make_hT(P, xt[i][:], ("xt", i), j, s1m, shm, "mod", idb, hb, "hT", tmpf, hbf, eng="pool")
                yield
            P.dma("act", hTs[tb].rearrange("p (k n) -> p k n", k=8), hb[:], r=["hT"], w=[("hTs", tb)])
            if GDN:
                yield from block_stage0(tb)

        def exhaust(g):
            if g is not None:
                for _ in g:
                    pass

        bg = bg_ffn_prep() if BG_FFN else None
        bgs = [bg]

        def bg_step():
            if bgs[0] is not None:
                try:
                    next(bgs[0])
                except StopIteration:
                    bgs[0] = None

        exhaust(front_gen(0))
        for tb in range(NB):
            if not GDN:
                P.dma("pool", odn[tb], zt[:], r=["mixd"], w=[("odn", tb)])
                if tb + 1 < NB:
                    exhaust(front_gen(tb + 1))
                continue
            cur["tb"] = tb
            state = {}
            gens = {}
            set_of = {}
            nyield = {}
            free_sets = list(range(NSET))
            nxt = 0
            scan_next = 0
            done = 0
            fgen = None
            fstarted = False
            while done < 4:
                while nxt < 4 and free_sets:
                    k = free_sets.pop(0)
                    set_of[nxt] = sets[k]
                    gens[nxt] = prep_gen(nxt, sets[k])
                    state[nxt] = "prep"
                    nyield[nxt] = 0
                    nxt += 1
                if scan_next < 4 and state.get(scan_next) == "ready" and not any(v == "scan" for v in state.values()):
                    gens[scan_next] = scan_gen(scan_next, set_of[scan_next])
                    state[scan_next] = "scan"
                if not fstarted and tb + 1 < NB:
                    fgen = front_gen(tb + 1)
                    fstarted = True
                order_ = sorted([j for j in gens if state[j] in ("prep", "scan", "post")], key=lambda j: (state[j] != "scan", j))
                for j in order_:
                    try:
                        next(gens[j])
                        nyield[j] = nyield.get(j, 0) + 1
                    except StopIteration:
                        if state[j] == "prep":
                            state[j] = "ready"
                        elif state[j] == "scan":
                            state[j] = "post"
                            gens[j] = post_gen(j, set_of[j])
                            scan_next += 1
                        elif state[j] == "post":
                            state[j] = "done"
                            free_sets.append(set_of[j]["k"])
                            done += 1
                if fgen is not None:
                    try:
                        next(fgen)
                    except StopIteration:
                        fgen = None
                bg_step()
            P.dma("pool", odn[tb], zt[:], r=["mixd"], w=[("odn", tb)])
            if tb + 1 < NB and not fstarted:
                fgen = front_gen(tb + 1)
            exhaust(fgen)
        exhaust(bgs[0])
        P.finish()

    def build_p1b():
        P = Pass(nc, "p1b")
        P.psum_banks(8)
        cst = P.sb([128, NCONST], F32)
        P.dma("sp", cst[:], consts, w=["cst"])
        ones_b = P.sb([128, 128], BF16)
        utb = P.sb([128, 128], BF16)
        P.cp("dve", ones_b[:], cst[:, C_ONE:C_ONE + 128], r=["cst"], w=["ones_b"])
        P.cp("dve", utb[:], cst[:, C_UT:C_UT + 128], r=["cst"], w=["utb"])
        invf = cst[0:64, C_MISC + 0:C_MISC + 1]
        sgn = cst[0:64, C_MISC + 1:C_MISC + 2]
        gtm = P.sb([128, D], F32)
        g1 = P.sb([128, D], F32)
        b1 = P.sb([128, D], F32)
        P.dma("sp", gtm[:], modsc[:, 2 * D:3 * D], w=["mod"])
        P.dma("act", g1[:], bc(ln1_g[0:1, :], [128, D]), w=["ln"])
        P.dma("act", b1[:], bc(ln1_b[0:1, :], [128, D]), w=["ln"])
        kTc = P.sb([128, 4, SEQ], BF16)
        krTc = P.sb([128, SEQ], BF16)
        Vc = P.sb([128, NT, 512], BF16)
        ring = Ring(P, 2)
        hT = P.sb([128, 8, 512], BF16)
        cqT = P.sb([128, 4, 512], BF16)
        cqsq = P.sb([128, 4, 512], BF16)
        ckvT = P.sb([128, 2, 512], BF16)
        ckvsq = P.sb([128, 2, 512], BF16)
        rq_b = P.sb([128, 512], F32)
        rkv_b = P.sb([128, 512], F32)
        rkv_c = P.sb([128, 4], F32)
        cosT = P.sb([64, 512], F32)
        sinT = P.sb([64, 512], F32)
        posi = P.sb([64, 512], I32)
        ta = P.sb([64, 512], F32)
        tb_ = P.sb([64, 512], F32)
        tc = P.sb([64, 512], F32)
        ki = P.sb([64, 512], I32)
        t1 = P.sb([128, 512], F32)
        t2 = P.sb([128, 512], F32)
        qTn2 = [P.sb([128, 4, 512], BF16) for _ in range(2)]
        qrT2 = [P.sb([128, 4, 512], BF16) for _ in range(2)]
        P.memset("pool", krTc[64:128, :], 0.0, w=["krpad"])
        for q_ in qrT2:
            P.memset("pool", q_[64:128, :, :], 0.0, w=["qrpad"])
        sqa = P.sb([128, 512], BF16)
        sqb = P.sb([64, 512], BF16)
        pT = [P.sb([128, 512], BF16) for _ in range(5)]
        mixT = P.sb([128, 8, 512], BF16)
        xt = [P.sb([128, D], F32) for _ in range(4)]
        pending_ln = []
        tmpy = P.sb([128, D], F32)
        junk = P.sb([128, D], F32)
        st4 = [P.sb([128, 8], F32) for _ in range(4)]
        ot = P.sb([128, D], F32)
        km2 = P.sb([128, 4], F32)
        sm = P.sb([128, 8], F32)
        bias2 = [P.sb([128, 4], F32) for _ in range(2)]
        P.memset("dve", km2[:], 0.0, w=["km2"])
        npt = [0]

        def sumsq_b(dst, pieces, r):
            n = len(pieces)
            for k, (ap, npart) in enumerate(pieces):
                P.mm(dst, ones_b[0:npart, :], ap, k == 0, k == n - 1, r=["ones_b"] + r, w=[dst_tok[0]])

        def mla_prep(tb, par):
            t0 = tb * 512
            qTn = qTn2[par]
            qrT = qrT2[par]
            bias_h = bias2[par]
            P.dma("sp", hT[:], hTs[tb].rearrange("p (k n) -> p k n", k=8), w=["hT"])
            P.dma("act", posi[:], bc(pos[0:1, t0:t0 + 512], [64, 512]), w=["posi"])
            P.cp("dve", ta[:], posi[:], r=["posi"], w=["ta"])
            P.ts("dve", ta[:], ta[:], invf, ALU.mult, r=["ta", "cst"], w=["ta"])
            P.ts("dve", tb_[:], ta[:], 1.0 / (2.0 * PI), ALU.mult, r=["ta"], w=["tb_"])
            P.cp("dve", ki[:], tb_[:], r=["tb_"], w=["ki"])
            P.cp("dve", tb_[:], ki[:], r=["ki"], w=["tb_"])
            P.stt("dve", ta[:], tb_[:], -C1, ta[:], ALU.mult, ALU.add, r=["tb_", "ta"], w=["ta"])
            P.stt("dve", ta[:], tb_[:], -C2, ta[:], ALU.mult, ALU.add, r=["tb_", "ta"], w=["ta"])
            P.ts("dve", tc[:], ta[:], PI / 2.0, ALU.add, r=["ta"], w=["tc"])
            P.ts("dve", tb_[:], tc[:], PI, ALU.is_gt, r=["tc"], w=["tb_"])
            P.stt("dve", tc[:], tb_[:], -2.0 * PI, tc[:], ALU.mult, ALU.add, r=["tb_", "tc"], w=["tc"])
            P.ts("dve", tc[:], tc[:], -PI, ALU.max, s2=PI, op1=ALU.min, r=["tc"], w=["tc"])
            P.ts("dve", ta[:], ta[:], -PI, ALU.max, s2=PI, op1=ALU.min, r=["ta"], w=["ta"])
            P.act(cosT[:], tc[:], AF.Sin, r=["tc"], w=["cosT"])
            P.act(sinT[:], ta[:], AF.Sin, r=["ta"], w=["sinT"])
            P.ts("dve", sinT[:], sinT[:], sgn, ALU.mult, r=["sinT", "cst"], w=["sinT"])
            yield

            def rope_out(dst, pa, patok, pbk, pbtok, extra=None, extok=None, dtok=None):
                P.tt("dve", t1[0:64, :], pa[0:64, :], cosT[:], ALU.mult, r=[patok, "cosT"], w=["t1"])
                P.tt("dve", t2[0:64, :], pbk[0:64, :], sinT[:], ALU.mult, r=[pbtok, "sinT"], w=["t2"])
                if extra is None:
                    P.tt("dve", dst, t1[0:64, :], t2[0:64, :], ALU.add, r=["t1", "t2"], w=[dtok])
                else:
                    P.tt("pool", t1[0:64, :], t1[0:64, :], t2[0:64, :], ALU.add, r=["t1", "t2"], w=["t1"])
                    P.tt("dve", dst, t1[0:64, :], extra, ALU.mult, r=["t1", extok], w=[dtok])

            w4, w4tok = ring.load(WCH["in"][0] + 4)
            for fc in range(4):
                k, pt, ptok = P.bget()
                for kc in range(8):
                    P.mm(pt[:], w4[:, kc, fc * 128:(fc + 1) * 128], hT[:, kc, :], kc == 0, kc == 7, r=[w4tok, "hT"], w=[ptok])
                P.cp("act", cqT[:, fc, :], pt[:], r=[ptok], w=["cqT"])
                P.act(cqsq[:, fc, :], pt[:], AF.Square, r=[ptok], w=["cqsq"])
                P.bput(k)
                yield
            k, pt, ptok = P.bget()
            for fc in range(4):
                P.mm(pt[:], ones_b[:], cqsq[:, fc, :], fc == 0, fc == 3, r=["ones_b", "cqsq"], w=[ptok])
            P.act(t1[:], pt[:], AF.Ln, scale=1.0 / 512.0, bias=1e-6, r=[ptok], w=["t1"])
            P.act(rq_b[:], t1[:], AF.Exp, scale=-0.5, r=["t1"], w=["rq_b"])
            P.bput(k)
            yield
            w5, w5tok = ring.load(WCH["in"][0] + 5)
            for fc in range(2):
                k, pt, ptok = P.bget()
                for kc in range(8):
                    P.mm(pt[:], w5[:, kc, fc * 128:(fc + 1) * 128], hT[:, kc, :], kc == 0, kc == 7, r=[w5tok, "hT"], w=[ptok])
                P.cp("act", ckvT[:, fc, :], pt[:], r=[ptok], w=["ckvT"])
                P.act(ckvsq[:, fc, :], pt[:], AF.Square, r=[ptok], w=["ckvsq"])
                P.bput(k)
                yield
            k, pt, ptok = P.bget()
            for fc in range(2):
                P.mm(pt[:], ones_b[:], ckvsq[:, fc, :], fc == 0, fc == 1, r=["ones_b", "ckvsq"], w=[ptok])
            P.act(t1[:], pt[:], AF.Ln, scale=1.0 / 256.0, bias=1e-6, r=[ptok], w=["t1"])
            P.act(rkv_b[:], t1[:], AF.Exp, scale=-0.5, r=["t1"], w=["rkv_b"])
            P.bput(k)
            yield
            k, pt, ptok = P.bget()
            for j in range(4):
                for fc in range(2):
                    P.mm(pt[:, j:j + 1], ckvsq[:, fc, j * 128:(j + 1) * 128], ones_b[:, 0:1], fc == 0, fc == 1,
                         r=["ones_b", "ckvsq"], w=[ptok])
            P.act(sm[:, 0:4], pt[:, 0:4], AF.Ln, scale=1.0 / 256.0, bias=1e-6, r=[ptok], w=["sm"])
            P.act(rkv_c[:], sm[:, 0:4], AF.Exp, scale=-0.5, r=["sm"], w=["rkv_c"])
            P.bput(k)
            yield
            ka, pa, patok = P.bget()
            kb, pbk, pbtok = P.bget()
            for kc in range(8):
                P.mm(pa[0:64, :], w5[:, kc, 256:320], hT[:, kc, :], kc == 0, kc == 7, r=[w5tok, "hT"], w=[patok])
            for kc in range(8):
                P.mm(pbk[0:64, :], w5[:, kc, 320:384], hT[:, kc, :], kc == 0, kc == 7, r=[w5tok, "hT"], w=[pbtok])
            rope_out(krTc[0:64, t0:t0 + 512], pa, patok, pbk, pbtok, dtok=("krT", tb))
            P.bput(ka)
            P.bput(kb)
            yield
            P.act(sqb[:], krTc[0:64, t0:t0 + 512], AF.Square, r=[("krT", tb)], w=["sqb"])
            wk, wktok = ring.load(WCH["ukv"][0] + 0, nk=2)
            for h in range(4):
                k, pt, ptok = P.bget()
                for fc in range(2):
                    P.mm(pt[:], wk[:, fc, h * 128:(h + 1) * 128], ckvT[:, fc, :], fc == 0, fc == 1, r=[wktok, "ckvT"], w=[ptok])
                P.tt("dve", kTc[:, h, t0:t0 + 512], pt[:], rkv_b[:], ALU.mult, r=[ptok, "rkv_b"], w=[("kT", h, tb)])
                P.bput(k)
                yield
                P.act(sqa[:], kTc[:, h, t0:t0 + 512], AF.Square, r=[("kT", h, tb)], w=["sqa"])
                k, pt, ptok = P.bget()
                P.mm(pt[:], ones_b[:], sqa[:], True, False, r=["ones_b", "sqa"], w=[ptok])
                P.mm(pt[:], ones_b[0:64, :], sqb[:], False, True, r=["ones_b", "sqb"], w=[ptok])
                P.S.op("dve", lambda e, o=sm[:, 5:6], i_=pt[:]: e.reduce_max(out=o, in_=i_, axis=mybir.AxisListType.X), r=[ptok], w=["sm"])
                P.tt("dve", km2[:, h:h + 1], km2[:, h:h + 1], sm[:, 5:6], ALU.max, r=["sm", "km2"], w=["km2"])
                P.bput(k)
                yield
            wv, wvtok = ring.load(WCH["ukv"][0] + 1, nk=2)
            for j in range(4):
                k, pt, ptok = P.bget()
                for fc in range(2):
                    P.mm(pt[:], ckvT[:, fc, j * 128:(j + 1) * 128], wv[:, fc, :], fc == 0, fc == 1, r=[wvtok, "ckvT"], w=[ptok])
                P.act(Vc[:, tb * 4 + j, :], pt[:], AF.Copy, scale=rkv_c[:, j:j + 1], r=[ptok, "rkv_c"], w=[("V", tb * 4 + j)])
                P.bput(k)
                yield
            wq0, wq0tok = ring.load(WCH["uq"][0] + 0, nk=4)
            for h in range(4):
                k, pt, ptok = P.bget()
                for fc in range(4):
                    P.mm(pt[:], wq0[:, fc, h * 128:(h + 1) * 128], cqT[:, fc, :], fc == 0, fc == 3, r=[wq0tok, "cqT"], w=[ptok])
                P.tt("dve", qTn[:, h, :], pt[:], rq_b[:], ALU.mult, r=[ptok, "rq_b"], w=[("qTn", par, h)])
                P.bput(k)
                yield
            wq1, wq1tok = ring.load(WCH["uq"][0] + 1, nk=4)
            for h in range(4):
                ka, pa, patok = P.bget()
                kb, pbk, pbtok = P.bget()
                for fc in range(4):
                    P.mm(pa[0:64, :], wq1[:, fc, h * 64:(h + 1) * 64], cqT[:, fc, :], fc == 0, fc == 3, r=[wq1tok, "cqT"], w=[patok])
                for fc in range(4):
                    P.mm(pbk[0:64, :], wq1[:, fc, 256 + h * 64:256 + (h + 1) * 64], cqT[:, fc, :], fc == 0, fc == 3, r=[wq1tok, "cqT"], w=[pbtok])
                rope_out(qrT[0:64, h, :], pa, patok, pbk, pbtok, extra=rq_b[0:64, :], extok="rq_b", dtok=("qrT", par, h))
                P.bput(ka)
                P.bput(kb)
                yield
            for h in range(4):
                P.act(sqa[:], qTn[:, h, :], AF.Square, r=[("qTn", par, h)], w=["sqa"])
                P.act(sqb[:], qrT[0:64, h, :], AF.Square, r=[("qrT", par, h)], w=["sqb"])
                k, pt, ptok = P.bget()
                P.mm(pt[:], ones_b[:], sqa[:], True, False, r=["ones_b", "sqa"], w=[ptok])
                P.mm(pt[:], ones_b[0:64, :], sqb[:], False, True, r=["ones_b", "sqb"], w=[ptok])
                P.S.op("dve", lambda e, o=sm[:, 6:7], i_=pt[:]: e.reduce_max(out=o, in_=i_, axis=mybir.AxisListType.X), r=[ptok], w=["sm"])
                P.bput(k)
                yield
                P.tt("dve", sm[:, 7:8], sm[:, 6:7], km2[:, h:h + 1], ALU.mult, r=["sm", "km2"], w=["sm"])
                P.act(sm[:, 7:8], sm[:, 7:8], AF.Ln, bias=1e-30, r=["sm"], w=["sm"])
                P.act(sm[:, 7:8], sm[:, 7:8], AF.Exp, scale=0.5, r=["sm"], w=["sm"])
                P.ts("dve", bias_h[:, h:h + 1], sm[:, 7:8], -SCALE, ALU.mult, r=["sm"], w=[("bias", par, h)])
        gprep = mla_prep(0, 0)
        for _ in gprep:
            pass
        for tb in range(NB):
            par = tb % 2
            qTn = qTn2[par]
            qrT = qrT2[par]
            bias_h = bias2[par]
            gnext = [mla_prep(tb + 1, 1 - par) if tb + 1 < NB else None]

            def step_next():
                if gnext[0] is not None:
                    try:
                        next(gnext[0])
                    except StopIteration:
                        gnext[0] = None
            P.dma("act", mixT[:, 0:4, :], odn[tb].rearrange("p (k n) -> p k n", k=4), w=[("mixT", k) for k in range(4)])
            for h in range(4):
                ko, po, potok = P.bget()
                kl, pl, pltok = P.bget()
                nkt = 4 * tb + 4
                order = list(range(4 * tb, nkt)) + list(range(0, 4 * tb))
                LAG = 2
                pend = []

                def pv(item):
                    n__, kt_, q0_, pi__ = item
                    p__ = pT[pi__]
                    first = n__ == 0
                    last = n__ == len(order) - 1
                    P.mm(po[:, q0_:512], Vc[:, kt_, h * 128:(h + 1) * 128], p__[:, q0_:512], first, last, r=[("V", kt_), ("pT", pi__)], w=[potok])
                    P.mm(pl[:, q0_:512], ones_b[:], p__[:, q0_:512], first, last, r=["ones_b", ("pT", pi__)], w=[pltok])

                for n_, kt in enumerate(order):
                    i = kt - 4 * tb
                    q0 = max(i, 0) * 128
                    k, ps_, pstok = P.bget()
                    P.mm(ps_[:, q0:512], kTc[:, h, kt * 128:(kt + 1) * 128], qTn[:, h, q0:512], True, False,
                         r=[("kT", h, kt // 4), ("qTn", par, h)], w=[pstok])
                    P.mm(ps_[:, q0:512], krTc[:, kt * 128:(kt + 1) * 128], qrT[:, h, q0:512], False, True,
                         r=[("krT", kt // 4), ("qrT", par, h), "krpad", "qrpad"], w=[pstok])
                    pi_ = npt[0] % len(pT)
                    npt[0] += 1
                    p_ = pT[pi_]
                    P.act(p_[:, q0:512], ps_[:, q0:512], AF.Exp, scale=SCALE, bias=bias_h[:, h:h + 1], r=[pstok, ("bias", par, h)], w=[("pT", pi_)])
                    P.bput(k)
                    if i >= 0:
                        P.tt("pool", p_[:, q0:q0 + 128], p_[:, q0:q0 + 128], utb[:], ALU.mult, r=[("pT", pi_), "utb"], w=[("pT", pi_)])
                    pend.append((n_, kt, q0, pi_))
                    if len(pend) > LAG:
                        pv(pend.pop(0))
                    step_next()
                while pend:
                    pv(pend.pop(0))
                P.recip(tmpy[:, 0:512], pl[:], r=[pltok], w=["tmpy"])
                P.tt("dve", mixT[:, 4 + h, :], po[:], tmpy[:, 0:512], ALU.mult, r=[potok, "tmpy"], w=[("mixT", 4 + h)])
                P.bput(ko)
                P.bput(kl)
                for _ in range(1 if h == 0 else 2):
                    if pending_ln:
                        pending_ln.pop(0)()
            while pending_ln:
                pending_ln.pop(0)()
            while gnext[0] is not None:
                step_next()
            wo = [ring.load(WCH["o"][0] + hf) for hf in range(2)]
            for j in range(4):
                t = tb * 4 + j
                i = j
                P.dma("sp", xt[i][:], x[t * 128:(t + 1) * 128, :], w=[("xt", i)])
                for hf in range(2):
                    k, pt, ptok = P.bget()
                    for kc in range(8):
                        P.mm(pt[:], mixT[:, kc, j * 128:(j + 1) * 128], wo[hf][0][:, kc, :], kc == 0, kc == 7,
                             r=[("mixT", kc), wo[hf][1]], w=[ptok])
                    sl = slice(hf * 512, (hf + 1) * 512)
                    P.tt("dve", tmpy[:, sl], pt[:], gtm[:, sl], ALU.mult, r=[ptok, "mod"], w=["tmpy"])
                    P.bput(k)
                P.stt("dve", xt[i][:], xt[i][:], ALPHA, tmpy[:], ALU.mult, ALU.add, r=[("xt", i), "tmpy"], w=[("xt", i)])

                def ln1a(i=i):
                    ln_stats(P, xt[i][:], ("xt", i), junk, st4[i], ("st", i))

                def ln1b(t=t, i=i):
                    ln_apply(P, xt[i][:], ("xt", i), g1, b1, "ln", ot[:], "ot", junk, st4[i], ("st", i))
                    P.dma("pool", x1s[t * 128:(t + 1) * 128, :], ot[:], r=["ot"], w=[("x1s", t)])
                pending_ln += [ln1a, ln1b]
        while pending_ln:
            pending_ln.pop(0)()
        P.finish()

    if "mix" in parts:
        build_p1a()
        build_p1b()

    P = Pass(nc, "p2")
    P.psum_banks(8)
    cst, idb = load_consts(P)
    ring = Ring(P, 3)
    s1f = P.sb([128, D], F32)
    shf = P.sb([128, D], F32)
    gtf = P.sb([128, D], F32)
    g2 = P.sb([128, D], F32)
    b2 = P.sb([128, D], F32)
    P.dma("sp", shf[:], modsc[:, 3 * D:4 * D], w=["mod"])
    P.dma("sp", s1f[:], modsc[:, 4 * D:5 * D], w=["mod"])
    P.dma("sp", gtf[:], modsc[:, 5 * D:6 * D], w=["mod"])
    P.dma("act", g2[:], bc(ln2_g[0:1, :], [128, D]), w=["ln"])
    P.dma("act", b2[:], bc(ln2_b[0:1, :], [128, D]), w=["ln"])
    xblk2 = [P.sb([128, 4, D], F32) for _ in range(2)]
    hT2 = [P.sb([128, 8, 512], BF16) for _ in range(2)]
    aT = P.sb([128, NFC, 512], BF16)
    tmpf = P.sb([128, D], F32)
    tmpg = P.sb([128, 512], F32)
    hbf = P.sb([128, D], BF16)
    sg = [P.sb([128, 512], F32) for _ in range(2)]
    junk = P.sb([128, D], F32)
    st4 = [P.sb([128, 8], F32) for _ in range(4)]
    ot = [P.sb([128, D], F32) for _ in range(2)]

    def p2_prep(tb):
        par = tb % 2
        for j in range(4):
            t = tb * 4 + j
            P.dma("sp", xblk2[par][:, j, :], x1s[t * 128:(t + 1) * 128, :], w=[("xblk", par, j)])
            make_hT(P, xblk2[par][:, j, :], ("xblk", par, j), j, s1f, shf, "mod", idb, hT2[par], ("hT", par), tmpf, hbf)

    p2_prep(0)
    pending_ln = []
    for tb in range(NB):
        par = tb % 2
        xblk = xblk2[par]
        hT = hT2[par]
        hTtok = ("hT", par)
        for c in range(6):
            wg, wgtok = ring.load(WCH["gate"][0] + c, "sp")
            wu, wutok = ring.load(WCH["up"][0] + c, "sp")
            for m in range(4 if c < 5 else 2):
                fc = c * 4 + m
                kg, pg, pgtok = P.bget()
                ku, pu, putok = P.bget()
                for kc in range(8):
                    P.mm(pg[:], wg[:, kc, m * 128:(m + 1) * 128], hT[:, kc, :], kc == 0, kc == 7, r=[wgtok, hTtok], w=[pgtok])
                for kc in range(8):
                    P.mm(pu[:], wu[:, kc, m * 128:(m + 1) * 128], hT[:, kc, :], kc == 0, kc == 7, r=[wutok, hTtok], w=[putok])
                i = fc % 2
                P.act(sg[i][:], pg[:], AF.Silu, r=[pgtok], w=[("sg", i)])
                P.tt("dve", aT[:, fc, :], sg[i][:], pu[:], ALU.mult, r=[("sg", i), putok], w=[("aT", fc)])
                P.bput(kg)
                P.bput(ku)
            for _ in range(1 if c == 0 else 2):
                if pending_ln:
                    pending_ln.pop(0)()
            if c == 4 and tb + 1 < NB:
                assert not pending_ln
                p2_prep(tb + 1)
        for hf in range(2):
            acc = [P.bget() for _ in range(4)]
            for g in range(3):
                nk = 8 if g < 2 else 6
                wd, wdtok = ring.load(WCH["down"][0] + g * 2 + hf, "sp", nk=nk)
                for j in range(4):
                    k, pt, ptok = acc[j]
                    for kk in range(nk):
                        fc = g * 8 + kk
                        P.mm(pt[:], aT[:, fc, j * 128:(j + 1) * 128], wd[:, kk, :], fc == 0, fc == NFC - 1,
                             r=[("aT", fc), wdtok], w=[ptok])
            for j in range(4):
                k, pt, ptok = acc[j]
                sl = slice(hf * 512, (hf + 1) * 512)
                P.tt("dve", tmpg[:], pt[:], gtf[:, sl], ALU.mult, r=[ptok, "mod"], w=["tmpg"])
                P.stt("dve", xblk[:, j, sl], xblk[:, j, sl], ALPHA, tmpg[:], ALU.mult, ALU.add,
                      r=[("xblk", par, j), "tmpg"], w=[("xblk", par, j)])
                P.bput(k)
        def mk_ln(tb=tb, par=par, xblk=xblk):
            fs = []
            for j in range(4):
                t = tb * 4 + j

                def fa(j=j):
                    ln_stats(P, xblk[:, j, :], ("xblk", par, j), junk, st4[j], ("st", j))

                def fb(j=j, t=t):
                    i = t % 2
                    ln_apply(P, xblk[:, j, :], ("xblk", par, j), g2, b2, "ln", ot[i][:], ("ot", i), junk, st4[j], ("st", j))
                    P.dma("pool", out[t * 128:(t + 1) * 128, :], ot[i][:], r=[("ot", i)], w=[("out", t)])
                fs += [fa, fb]
            return fs
        pending_ln = mk_ln()
    while pending_ln:
        pending_ln.pop(0)()
    P.finish()
    return nc


_NC_CACHE = {}


def _prep_inputs(inp, b):
    f = lambda a: np.ascontiguousarray(a, dtype=np.float32)
    conv_w = np.asarray(inp["conv_w"])[0, :, 0, :]
    m = {
        "x": f(inp["x"][b]),
        "cT": f(np.asarray(inp["c"])[b].reshape(8, 128).T),
        "pos": np.ascontiguousarray(np.asarray(inp["positions"])[b][None, :].astype(np.int32)),
        "w_ada": f(inp["w_ada"][0]),
        "b_ada": f(inp["b_ada"][0][None, :]),
        "w_in": f(inp["w_in"][0]),
        "cw": f(conv_w.reshape(4, 12, 128).transpose(2, 1, 0)),
        "a_log": f(inp["a_log"][0][None, :]),
        "dt_bias": f(inp["dt_bias"][0][None, :]),
        "dn_g": f(inp["dn_norm_g"][0][None, :]),
        "qn_g": f(np.asarray(inp["q_norm_g"])[0].reshape(4, 128).T),
        "kvn_g": f(np.asarray(inp["kv_norm_g"])[0].reshape(2, 128).T),
        "w_uq": f(inp["w_uq"][0]),
        "w_ukv": f(inp["w_ukv"][0]),
        "w_o": f(inp["w_o"][0]),
        "ln1_g": f(inp["ln1_g"][0][None, :]),
        "ln1_b": f(inp["ln1_b"][0][None, :]),
        "w_gate": f(inp["w_gate"][0]),
        "w_up": f(inp["w_up"][0]),
        "w_down": f(inp["w_down"][0]),
        "ln2_g": f(inp["ln2_g"][0][None, :]),
        "ln2_b": f(inp["ln2_b"][0][None, :]),
        "consts": host_consts(),
    }
    return m


def kernel(**inputs):
    inp = {k: np.asarray(v) for k, v in inputs.items()}
    if "nc" not in _NC_CACHE:
        _NC_CACHE["nc"] = build_program()
    nc = _NC_CACHE["nc"]
    in_maps = [_prep_inputs(inp, b) for b in range(8)]
    res = run_bass_kernel_spmd(nc, in_maps, core_ids=list(range(8)))
    return np.stack([np.asarray(r["out"]) for r in res.results], axis=0).astype(np.float32)
```

```python
import contextlib
import math
import numpy as np
import concourse.bass as bass
import concourse.mybir as mybir
from concourse.bass_utils import run_bass_kernel_spmd

F32 = mybir.dt.float32
BF16 = mybir.dt.bfloat16
I32 = mybir.dt.int32
AF = mybir.ActivationFunctionType
ALU = mybir.AluOpType

ENGS = ("pe", "act", "dve", "pool", "sp")
NDSEM = 8
EPOCH = 20000

D = 1024
SEQ = 4096
NT = SEQ // 128
NB = SEQ // 512
DFF = 2816
NFC = DFF // 128
ALPHA = 2.0 ** 0.25
PI = math.pi


class Sched:
    def __init__(self, nc, tag):
        self.nc = nc
        self.tag = tag
        self.ops = {e: [] for e in ENGS}
        self.last_w = {}
        self.rd_c = {}
        self.rd_d = {}
        self.ndma = {e: 0 for e in ENGS}

    def op(self, eng, fn, r=(), w=(), dma=False):
        idx = len(self.ops[eng])
        deps = set()
        for t in r:
            lw = self.last_w.get(t)
            if lw is not None:
                deps.add(lw)
        for t in w:
            lw = self.last_w.get(t)
            if lw is not None:
                deps.add(lw)
            for re_, ri in self.rd_c.get(t, {}).items():
                deps.add((re_, ri))
            for x in self.rd_d.get(t, ()):
                deps.add(x)
        o = dict(fn=fn, deps=deps, dma=dma, signal=dma)
        if dma:
            o["dma_i"] = self.ndma[eng]
            self.ndma[eng] += 1
        self.ops[eng].append(o)
        me = (eng, idx)
        for t in r:
            if dma:
                self.rd_d.setdefault(t, []).append(me)
            else:
                self.rd_c.setdefault(t, {})[eng] = idx
        for t in w:
            self.last_w[t] = me
            self.rd_c[t] = {}
            self.rd_d[t] = []
        return me

    def dma(self, q, out, in_, r=(), w=()):
        return self.op(q, lambda e: e.dma_start(out=out, in_=in_), r=r, w=w, dma=True)

    def emit(self):
        nc = self.nc
        ops = self.ops
        for eng in ENGS:
            for o in ops[eng]:
                nd = set()
                for (de, di) in o["deps"]:
                    d = ops[de][di]
                    if de == eng and eng == "pe" and not d["dma"] and not o["dma"]:
                        continue
                    nd.add((de, di))
                o["deps"] = nd
                for (de, di) in nd:
                    ops[de][di]["signal"] = True
        nsig = {}
        for eng in ENGS:
            c = 0
            for o in ops[eng]:
                if o["dma"]:
                    i = o["dma_i"]
                    o["h"] = (("d", eng, i % NDSEM), 16 * (i // NDSEM + 1))
                elif o["signal"]:
                    o["h"] = (("c", eng, c // EPOCH), c % EPOCH + 1)
                    c += 1
            nsig[eng] = c
        with contextlib.ExitStack() as st:
            sems = {}
            for eng in ENGS:
                for ep in range((nsig[eng] + EPOCH - 1) // EPOCH):
                    sems[("c", eng, ep)] = nc.alloc_semaphore(name=f"{self.tag}c_{eng}_{ep}")
                for k in range(min(NDSEM, self.ndma[eng])):
                    sems[("d", eng, k)] = nc.alloc_semaphore(name=f"{self.tag}d_{eng}_{k}")
            self.sem_handles = list(sems.values())
            block = st.enter_context(nc.Block())

            def run(eng, e):
                known = {}
                for o in ops[eng]:
                    waits = {}
                    for (de, di) in o["deps"]:
                        sk, v = ops[de][di]["h"]
                        if waits.get(sk, 0) < v:
                            waits[sk] = v
                    if o["dma"] and o["dma_i"] >= NDSEM:
                        sk = ("d", eng, o["dma_i"] % NDSEM)
                        v = 16 * (o["dma_i"] // NDSEM)
                        if waits.get(sk, 0) < v:
                            waits[sk] = v
                    for sk, v in waits.items():
                        if known.get(sk, 0) >= v:
                            continue
                        e.wait_ge(sems[sk], v)
                        known[sk] = v
                    ins = o["fn"](e)
                    if o["dma"]:
                        ins.then_inc(sems[o["h"][0]], 16)
                    elif o["signal"]:
                        ins.then_inc(sems[o["h"][0]], 1)
                n = self.ndma[eng]
                for k in range(min(NDSEM, n)):
                    cnt = (n - 1 - k) // NDSEM + 1
                    if known.get(("d", eng, k), 0) < 16 * cnt:
                        e.wait_ge(sems[("d", eng, k)], 16 * cnt)

            @block.sync
            def _(e):
                run("sp", e)

            @block.tensor
            def _(e):
                run("pe", e)

            @block.scalar
            def _(e):
                run("act", e)

            @block.vector
            def _(e):
                run("dve", e)

            @block.gpsimd
            def _(e):
                run("pool", e)


class Banks:
    def __init__(self, tiles):
        self.tiles = tiles
        self.free = list(range(len(tiles)))

    def get(self):
        k = self.free.pop(0)
        return k

    def put(self, k):
        self.free.append(k)


class Pass:
    def __init__(self, nc, tag):
        self.nc = nc
        self.tag = tag
        self.S = Sched(nc, tag)
        self.st = contextlib.ExitStack()
        self.n = 0

    def sb(self, shape, dt, name=None):
        self.n += 1
        return self.st.enter_context(self.nc.sbuf_tensor(f"{self.tag}_{name or 't'}{self.n}", list(shape), dt))

    def psum_banks(self, nf=6):
        self.pf = [self.st.enter_context(self.nc.psum_tensor(f"{self.tag}_pf{k}", [128, 512], F32)) for k in range(nf)]
        self.pbs = [self.st.enter_context(self.nc.psum_tensor(f"{self.tag}_pb{k}", [128, 1024], BF16)) for k in range(8 - nf)]
        self.banks = Banks(self.pf)
        self.pbi = 0

    def bget(self):
        k = self.banks.get()
        return k, self.pf[k], ("pf", k)

    def bput(self, k):
        self.banks.put(k)

    def pbhalf(self):
        self.pbi = (self.pbi + 1) % len(self.pbs)
        return self.pbs[self.pbi][:, 0:512], ("pb", self.pbi)

    def mm(self, out, lhsT, rhs, start, stop, r, w):
        self.S.op("pe", lambda e: e.matmul(out, lhsT=lhsT, rhs=rhs, start=start, stop=stop), r=r, w=w)

    def tr(self, out, in_, ident, r, w):
        self.S.op("pe", lambda e: e.transpose(out=out, in_=in_, identity=ident), r=r, w=w)

    def act(self, out, in_, func, r, w, bias=None, scale=None, accum_out=None):
        kw = {}
        if bias is not None:
            kw["bias"] = bias
        if scale is not None:
            kw["scale"] = scale
        if accum_out is not None:
            kw["accum_out"] = accum_out
        self.S.op("act", lambda e: e.activation(out=out, in_=in_, func=func, **kw), r=r, w=w)

    def tt(self, eng, out, in0, in1, op, r, w):
        self.S.op(eng, lambda e: e.tensor_tensor(out=out, in0=in0, in1=in1, op=op), r=r, w=w)

    def ts(self, eng, out, in0, s1, op0, r, w, s2=None, op1=None):
        if op1 is None:
            self.S.op(eng, lambda e: e.tensor_scalar(out=out, in0=in0, scalar1=s1, scalar2=None, op0=op0), r=r, w=w)
        else:
            self.S.op(eng, lambda e: e.tensor_scalar(out=out, in0=in0, scalar1=s1, scalar2=s2, op0=op0, op1=op1), r=r, w=w)

    def stt(self, eng, out, in0, scalar, in1, op0, op1, r, w):
        self.S.op(eng, lambda e: e.scalar_tensor_tensor(out=out, in0=in0, scalar=scalar, in1=in1, op0=op0, op1=op1), r=r, w=w)

    def cp(self, eng, out, in_, r, w):
        if eng == "act":
            self.S.op("act", lambda e: e.copy(out=out, in_=in_), r=r, w=w)
        else:
            self.S.op(eng, lambda e: e.tensor_copy(out=out, in_=in_), r=r, w=w)

    def memset(self, eng, ap, v, w):
        self.S.op(eng, lambda e: e.memset(ap, v), w=w)

    def recip(self, out, in_, r, w):
        self.S.op("dve", lambda e: e.reciprocal(out=out, in_=in_), r=r, w=w)

    def dma(self, q, out, in_, r=(), w=()):
        self.S.dma(q, out, in_, r=r, w=w)

    def finish(self):
        self.S.emit()
        self.st.close()
        self.nc.all_engine_barrier()
        self.nc.clear_and_free_semaphores(self.S.sem_handles)
        self.nc.all_engine_barrier()


def bc(ap, shape):
    return ap.to_broadcast(list(shape))


C_ID, C_U, C_LT, C_SLT, C_BO, C_UT, C_ONE, C_M16, C_D32, C_D64, C_MISC = [i * 128 for i in range(11)]
NCONST = 11 * 128 + 16


def host_consts():
    c = np.zeros((128, NCONST), np.float32)
    i = np.arange(128)
    same = (i[:, None] // 64) == (i[None, :] // 64)
    c[:, C_ID:C_ID + 128] = np.eye(128)
    c[:, C_U:C_U + 128] = same & (i[:, None] <= i[None, :])
    c[:, C_LT:C_LT + 128] = same & (i[:, None] >= i[None, :])
    c[:, C_SLT:C_SLT + 128] = same & (i[:, None] > i[None, :])
    c[:, C_BO:C_BO + 128] = same
    c[:, C_UT:C_UT + 128] = (i[:, None] <= i[None, :])
    c[:, C_ONE:C_ONE + 128] = 1.0
    m16 = (i[:, None] // 16) == (i[None, :] // 16)
    m32 = (i[:, None] // 32) == (i[None, :] // 32)
    c[:, C_M16:C_M16 + 128] = m16
    c[:, C_D32:C_D32 + 128] = m32 & ~m16
    c[:, C_D64:C_D64 + 128] = same & ~m32
    inv_freq = (1.0 / (np.float32(10000.0) ** (np.arange(0, 64, 2, dtype=np.float32) / np.float32(64)))).astype(np.float32)
    c[0:32, C_MISC + 0] = inv_freq
    c[32:64, C_MISC + 0] = inv_freq
    c[0:32, C_MISC + 1] = -1.0
    c[32:64, C_MISC + 1] = 1.0
    c[0:64, C_MISC + 2] = 1.0
    c[64:128, C_MISC + 3] = 1.0
    return c


WCH = {}
_n = 0
for _name, _cnt in (("in", 6), ("uq", 2), ("ukv", 2), ("o", 2), ("gate", 6), ("up", 6), ("down", 6)):
    WCH[_name] = (_n, _cnt)
    _n += _cnt
NWCH = _n


def build_program(parts=("mix", "gdn", "mla", "ffn"), dbg=False, stop=None, p0skip=()):
    nc = bass.Bass("TRN2", target_bir_lowering=False)

    def din(name, shape, dt=F32):
        return nc.dram_tensor(name, list(shape), dt, kind="ExternalInput").ap()

    x = din("x", [SEQ, D])
    cT = din("cT", [128, 8])
    pos = din("pos", [1, SEQ], I32)
    w_ada = din("w_ada", [D, 6 * D])
    b_ada = din("b_ada", [1, 6 * D])
    w_in = din("w_in", [D, 2888])
    cw = din("cw", [128, 12, 4])
    a_log = din("a_log", [1, 4])
    dt_bias = din("dt_bias", [1, 4])
    dn_g = din("dn_g", [1, 128])
    qn_g = din("qn_g", [128, 4])
    kvn_g = din("kvn_g", [128, 2])
    w_uq = din("w_uq", [512, 768])
    w_ukv = din("w_ukv", [256, 1024])
    w_o = din("w_o", [D, D])
    ln1_g = din("ln1_g", [1, D])
    ln1_b = din("ln1_b", [1, D])
    w_gate = din("w_gate", [D, DFF])
    w_up = din("w_up", [D, DFF])
    w_down = din("w_down", [DFF, D])
    ln2_g = din("ln2_g", [1, D])
    ln2_b = din("ln2_b", [1, D])
    consts = din("consts", [128, NCONST])
    out = nc.dram_tensor("out", [SEQ, D], F32, kind="ExternalOutput").ap()

    def dscr(name, shape, dt):
        return nc.dram_tensor(name, list(shape), dt, kind="Internal").ap()

    wsc = dscr("wsc", [NWCH, 128, 8 * 512], BF16)
    modsc = dscr("modsc", [128, 6 * D], F32)
    hTs = dscr("hTs", [NB, 128, 8 * 512], BF16)
    odn = dscr("odn", [NB, 128, 4 * 512], BF16)
    x1s = dscr("x1s", [SEQ, D], F32)
    dbg_t = nc.dram_tensor("dbg", [128, 4096], F32, kind="ExternalOutput").ap() if dbg else None

    BG_FFN = ("mix" in parts) and ("gdn" in parts)

    P = Pass(nc, "p0")
    P.psum_banks()
    cst = P.sb([128, NCONST], F32)
    P.dma("sp", cst[:], consts, w=["cst"])
    stg = [P.sb([128, 8, 512], F32) for _ in range(2)]
    stb = [P.sb([128, 8, 512], BF16) for _ in range(2)]
    qg_t = P.sb([128, 4], F32)
    kvg_t = P.sb([128, 2], F32)
    P.dma("sp", qg_t[:], qn_g, w=["qg"])
    P.dma("sp", kvg_t[:], kvn_g, w=["kvg"])
    cnt = [0]

    def prep_chunk(ch, pieces, kc_n, scale=None, sc_tok=None):
        if scale is not None and "scaled" in p0skip:
            return
        i = cnt[0] % 2
        cnt[0] += 1
        s, b = stg[i], stb[i]
        tot = sum(p[1].shape[1] for p in pieces)
        if tot < 512:
            P.memset("pool", s[:, 0:kc_n, tot:512], 0.0, w=[("stg", i)])
        for k, (dc, src) in enumerate(pieces):
            ncol = src.shape[1]
            q = "sp" if k % 2 == 0 else "act"
            P.dma(q, s[:, 0:kc_n, dc:dc + ncol], src.rearrange("(k p) n -> p k n", p=128), w=[("stg", i)])
        for kc in range(kc_n):
            eng = ("act", "dve", "pool")[kc % 3] if scale is None else "act"
            if scale is None:
                P.cp(eng, b[:, kc, :], s[:, kc, :], r=[("stg", i)], w=[("stb", i)])
            else:
                P.act(b[:, kc, :], s[:, kc, :], AF.Copy, scale=scale[:, kc:kc + 1], r=[("stg", i), sc_tok], w=[("stb", i)])
        P.dma("pool", wsc[ch].rearrange("p (k n) -> p k n", k=8)[:, 0:kc_n, :], b[:, 0:kc_n, :], r=[("stb", i)], w=[("wsc", ch)])

    if "w" not in p0skip:
        c0 = WCH["in"][0]
        prep_chunk(c0 + 0, [(0, w_in[:, 0:512])], 8)
        prep_chunk(c0 + 1, [(0, w_in[:, 512:1024])], 8)
        prep_chunk(c0 + 2, [(0, w_in[:, 1024:1536])], 8)
        prep_chunk(c0 + 3, [(0, w_in[:, 1536:2048])], 8)
        prep_chunk(c0 + 4, [(0, w_in[:, 2056:2568])], 8)
        if "ch5" not in p0skip:
            prep_chunk(c0 + 5, [(0, w_in[:, 2568:2824]), (256, w_in[:, 2824:2888]), (320, w_in[:, 2856:2888]),
                                (352, w_in[:, 2824:2856]), (384, w_in[:, 2048:2056])], 8)
        c0 = WCH["uq"][0]
        prep_chunk(c0 + 0, [(h * 128, w_uq[:, h * 192:h * 192 + 128]) for h in range(4)], 4, scale=qg_t, sc_tok="qg")
        prep_chunk(c0 + 1, [(h * 64, w_uq[:, h * 192 + 128:h * 192 + 192]) for h in range(4)]
                   + [(256 + h * 64, w_uq[:, h * 192 + 160:h * 192 + 192]) for h in range(4)]
                   + [(256 + h * 64 + 32, w_uq[:, h * 192 + 128:h * 192 + 160]) for h in range(4)], 4, scale=qg_t, sc_tok="qg")
        c0 = WCH["ukv"][0]
        prep_chunk(c0 + 0, [(h * 128, w_ukv[:, h * 256:h * 256 + 128]) for h in range(4)], 2, scale=kvg_t, sc_tok="kvg")
        prep_chunk(c0 + 1, [(h * 128, w_ukv[:, h * 256 + 128:h * 256 + 256]) for h in range(4)], 2, scale=kvg_t, sc_tok="kvg")
        c0 = WCH["o"][0]
        for hf in range(2):
            prep_chunk(c0 + hf, [(0, w_o[:, hf * 512:(hf + 1) * 512])], 8)
        if not BG_FFN:
            for nm, wt in (("gate", w_gate), ("up", w_up)):
                c0 = WCH[nm][0]
                for c in range(6):
                    prep_chunk(c0 + c, [(0, wt[:, c * 512:min((c + 1) * 512, DFF)])], 8)
            c0 = WCH["down"][0]
            for g in range(3):
                nk = 8 if g < 2 else 6
                for hf in range(2):
                    prep_chunk(c0 + g * 2 + hf, [(0, w_down[g * 1024:g * 1024 + nk * 128, hf * 512:(hf + 1) * 512])], nk)

    if "mod" not in p0skip:
        cT_t = P.sb([128, 8], F32)
        cact = P.sb([128, 8], F32)
        cb = P.sb([128, 8, 128], F32)
        P.dma("sp", cT_t[:], cT, w=["cT"])
        P.act(cact[:], cT_t[:], AF.Silu, r=["cT"], w=["cact"])
        P.cp("dve", cb[:], bc(cact[:].unsqueeze(2), [128, 8, 128]), r=["cact"], w=["cb"])
        wst = [P.sb([128, 8, 512], F32) for _ in range(2)]
        bst = [P.sb([128, 512], F32) for _ in range(2)]
        mo = [P.sb([128, 512], F32) for _ in range(2)]
        for nb in range(12):
            i = nb % 2
            P.dma("sp", wst[i][:], w_ada[:, nb * 512:(nb + 1) * 512].rearrange("(k p) n -> p k n", p=128), w=[("wst", i)])
            P.dma("act", bst[i][:], bc(b_ada[0:1, nb * 512:(nb + 1) * 512], [128, 512]), w=[("bst", i)])
            k, pt, ptok = P.bget()
            for kc in range(8):
                P.mm(pt[:], cb[:, kc, :], wst[i][:, kc, :], kc == 0, kc == 7, r=["cb", ("wst", i)], w=[ptok])
            P.tt("dve", mo[i][:], pt[:], bst[i][:], ALU.add, r=[ptok, ("bst", i)], w=[("mo", i)])
            P.bput(k)
            if nb in (2, 3, 8, 9):
                P.ts("dve", mo[i][:], mo[i][:], 1.0, ALU.add, r=[("mo", i)], w=[("mo", i)])
            P.dma("pool", modsc[:, nb * 512:(nb + 1) * 512], mo[i][:], r=[("mo", i)], w=[("modsc", nb)])
    P.finish()
    if stop == "p0":
        return nc

    class Ring:
        def __init__(self, P, n=3):
            self.P = P
            self.bufs = [P.sb([128, 8, 512], BF16) for _ in range(n)]
            self.i = 0
            self.n = n

        def load(self, ch, q="sp", nk=8):
            i = self.i % self.n
            self.i += 1
            self.P.dma(q, self.bufs[i][:, 0:nk, :], wsc[ch].rearrange("p (k n) -> p k n", k=8)[:, 0:nk, :], w=[("ring", i)])
            return self.bufs[i], ("ring", i)

    def load_consts(P):
        cst = P.sb([128, NCONST], F32)
        P.dma("sp", cst[:], consts, w=["cst"])
        idb = P.sb([128, 128], BF16)
        P.cp("dve", idb[:], cst[:, C_ID:C_ID + 128], r=["cst"], w=["idb"])
        return cst, idb

    def make_hT(P, xt, xtok, j, s1, sh, modtok, idb, hT, hTtok, tmpf, hbf, eng="dve"):
        P.tt(eng, tmpf[:], xt, s1[:], ALU.mult, r=[xtok, modtok], w=["tmpf"])
        P.tt(eng, hbf[:], tmpf[:], sh[:], ALU.add, r=["tmpf", modtok], w=["hbf"])
        for half in range(2):
            kb_, pb_, phtok = P.bget()
            ph = pb_[:, 0:256].bitcast(BF16)
            for kk in range(4):
                kc = half * 4 + kk
                P.tr(ph[:, kk * 128:(kk + 1) * 128], hbf[:, kc * 128:(kc + 1) * 128], idb[:], r=["hbf", "idb"], w=[phtok])
            P.cp("act", hT[:, half * 4:half * 4 + 4, j * 128:(j + 1) * 128],
                 ph.rearrange("p (k n) -> p k n", k=4), r=[phtok], w=[hTtok])
            P.bput(kb_)

    def ln_stats(P, y, ytok, junk, st, sttok):
        P.S.op("dve", lambda e: e.tensor_scalar(out=junk[:], in0=y, scalar1=1.0 / D, scalar2=0.0, op0=ALU.mult, op1=ALU.add, accum_out=st[:, 2:3]),
               r=[ytok], w=["junk", sttok])
        P.S.op("dve", lambda e: e.scalar_tensor_tensor(out=junk[:], in0=y, scalar=1.0 / D, in1=y, op0=ALU.mult, op1=ALU.mult,
                                                       accum_out=st[:, 1:2]), r=[ytok, sttok], w=["junk", sttok])
        P.tt("dve", st[:, 3:4], st[:, 2:3], st[:, 2:3], ALU.mult, r=[sttok], w=[sttok])
        P.tt("dve", st[:, 4:5], st[:, 1:2], st[:, 3:4], ALU.subtract, r=[sttok], w=[sttok])

    def ln_apply(P, y, ytok, g_b, b_b, gbtok, outt, outtok, junk, st, sttok):
        P.act(st[:, 5:6], st[:, 4:5], AF.Ln, bias=1e-5, r=[sttok], w=[sttok])
        P.act(st[:, 6:7], st[:, 5:6], AF.Exp, scale=-0.5, r=[sttok], w=[sttok])
        P.ts("dve", junk[:], y, st[:, 2:3], ALU.subtract, s2=st[:, 6:7], op1=ALU.mult, r=[ytok, sttok], w=["junk"])
        P.tt("pool", junk[:], junk[:], g_b[:], ALU.mult, r=["junk", gbtok], w=["junk"])
        P.tt("dve", outt, junk[:], b_b[:], ALU.add, r=["junk", gbtok], w=[outtok])

    SCALE = 1.0 / math.sqrt(192.0)
    C1 = 6.28125
    C2 = 2.0 * PI - 6.28125

    def build_p1a():
        GDN = "gdn" in parts
        NSET = 2
        P = Pass(nc, "p1a")
        P.psum_banks(8)
        cst, idb = load_consts(P)
        idf = cst[:, C_ID:C_ID + 128]
        Umat = cst[:, C_U:C_U + 128]
        BOm = cst[:, C_BO:C_BO + 128]
        onesf = cst[:, C_ONE:C_ONE + 128]
        CI = cst[:, C_MISC + 2:C_MISC + 4]
        s1m = P.sb([128, D], F32)
        shm = P.sb([128, D], F32)
        P.dma("sp", shm[:], modsc[:, 0:D], w=["mod"])
        P.dma("sp", s1m[:], modsc[:, D:2 * D], w=["mod"])
        xt = [P.sb([128, D], F32) for _ in range(2)]
        hb = P.sb([128, 8, 512], BF16)
        tmpf = P.sb([128, D], F32)
        hbf = P.sb([128, D], BF16)
        zt = P.sb([128, 4 * 512], BF16)
        mixd = zt[:].rearrange("p (h n) -> p h n", h=4)
        if not GDN:
            P.memset("pool", zt[:], 0.0, w=["mixd"])
        else:
            ring = Ring(P, 2)
            I3 = P.sb([128, 4, 128], F32)
            LT3 = P.sb([128, 4, 128], F32)
            SLT3 = P.sb([128, 4, 128], F32)
            for h in range(4):
                P.cp("pool", I3[:, h, :], idf, r=["cst"], w=["c3"])
                P.cp("pool", LT3[:, h, :], cst[:, C_LT:C_LT + 128], r=["cst"], w=["c3"])
                P.cp("pool", SLT3[:, h, :], cst[:, C_SLT:C_SLT + 128], r=["cst"], w=["c3"])
            mk16 = P.sb([128, 128], BF16)
            mk32 = P.sb([128, 128], BF16)
            mk64 = P.sb([128, 128], BF16)
            P.cp("pool", mk16[:], cst[:, C_M16:C_M16 + 128], r=["cst"], w=["mk"])
            P.cp("pool", mk32[:], cst[:, C_D32:C_D32 + 128], r=["cst"], w=["mk"])
            P.cp("pool", mk64[:], cst[:, C_D64:C_D64 + 128], r=["cst"], w=["mk"])
            cwt = P.sb([128, 12, 4], F32)
            P.dma("act", cwt[:], cw, w=["cwt"])
            negA = P.sb([128, 4], F32)
            dtb = P.sb([128, 4], F32)
            dng4 = P.sb([128, 4, 128], F32)
            P.dma("act", negA[:], bc(a_log[0:1, :], [128, 4]), w=["negA"])
            P.dma("act", dtb[:], bc(dt_bias[0:1, :], [128, 4]), w=["dtb"])
            for h in range(4):
                P.dma("act", dng4[:, h, :], bc(dn_g[0:1, :], [128, 128]), w=["dng4"])
            P.act(negA[:], negA[:], AF.Exp, r=["negA"], w=["negA"])
            P.ts("dve", negA[:], negA[:], -1.0, ALU.mult, r=["negA"], w=["negA"])
            Sst = [P.sb([128, 4, 128], F32) for _ in range(2)]
            Sbf = P.sb([128, 4, 128], BF16)
            P.memset("dve", Sst[0][:], 0.0, w=[("S", 0)])
            P.memset("dve", Sbf[:], 0.0, w=["Sbf"])
            prec = P.sb([128, 12, 3], F32)
            P.memset("dve", prec[:], 0.0, w=[("prec", ci) for ci in range(12)])
            pre4 = P.sb([128, 4, 515], F32)
            qkvs2 = [P.sb([128, 12, 512], F32) for _ in range(2)]
            sz2 = [P.sb([128, 4, 512], BF16) for _ in range(2)]
            cur = dict(tb=0)
            ba = P.sb([128, 4, 8], F32)
            beta2 = [P.sb([128, 4, 4], F32) for _ in range(2)]
            gg2 = [P.sb([128, 4, 4], F32) for _ in range(2)]
            sq8 = P.sb([128, 8, 512], BF16)
            lnb = [P.sb([128, 512], F32) for _ in range(2)]
            ones_bf = P.sb([128, 128], BF16)
            P.cp("dve", ones_bf[:], onesf, r=["cst"], w=["ones_bf"])
            jk = P.sb([128, 128], F32)
            f3 = lambda: P.sb([128, 4, 128], F32)
            b3 = lambda: P.sb([128, 4, 128], BF16)
            sets = []
            for k in range(NSET):
                bs = dict(A1=f3(), A2=f3(), A3=f3(), A4=f3(),
                          XA=b3(), XB=b3(), YA=b3(), YB=b3(), Xo1=b3(), Xo2=b3(), Yo1=b3(), Yo2=b3(), Pb=b3(), Qb=b3(),
                          RHSw=b3(), RHSu=b3(), qgT=b3(), kd=b3(), attT=b3(), H4=b3(), vnew=b3(),
                          sc=P.sb([128, 32], F32), egl=P.sb([128, 8], F32), oss=P.sb([128, 8], F32), k=k)
                sets.append(bs)
            scan_state = dict(c=0)

        def v3(t):
            return t[:].rearrange("p (h n) -> p h n", h=4)

        def col3(ap):
            return bc(ap.unsqueeze(2), [128, 4, 128])

        def prep_gen(j, bs):
            bp = cur["tb"] % 2
            qkvs, beta, gg = qkvs2[bp], beta2[bp], gg2[bp]
            k_ = bs["k"]
            T = lambda n: (n, k_)
            ts_ = slice(j * 128, (j + 1) * 128)
            G, dec, egcb, mb = bs["A1"], bs["A2"], bs["A3"], bs["A4"]
            tX, tA = bs["A1"], bs["A3"]
            RHSw, RHSu = bs["RHSw"], bs["RHSu"]
            qgT, kd, attT, attb, wT = bs["qgT"], bs["kd"], bs["attT"], bs["H4"], bs["H4"]
            uu = bs["A3"]
            sc, egl = bs["sc"], bs["egl"]
            P.cp("dve", G[:], col3(gg[:, j, :]), r=[("gg", bp)], w=[T("A1")])
            ksm, psm, psmtok = P.bget()
            P.mm(psm[:, 0:4], Umat, gg[:, j, :], True, True, r=["cst", ("gg", bp)], w=[psmtok])
            P.mm(psm[:, 4:8], BOm, gg[:, j, :], True, True, r=["cst", ("gg", bp)], w=[psmtok])
            for h in range(4):
                P.mm(psm[:, 8 + 2 * h:10 + 2 * h], G[:, h, :], CI, True, True, r=[T("A1"), "cst"], w=[psmtok])
            kgc, pgc, pgctok = P.bget()
            for h in range(4):
                P.mm(pgc[:, h * 128:(h + 1) * 128], G[:, h, :], Umat, True, True, r=[T("A1"), "cst"], w=[pgctok])
            P.cp("dve", sc[:, 0:4], psm[:, 0:4], r=[psmtok], w=[T("sc0")])
            P.act(sc[:, 4:8], psm[:, 0:4], AF.Exp, r=[psmtok], w=[T("sc1")])
            P.tt("dve", sc[:, 8:12], psm[:, 4:8], sc[:, 0:4], ALU.subtract, r=[psmtok, T("sc0")], w=[T("sc2")])
            P.act(sc[:, 12:16], sc[:, 8:12], AF.Exp, r=[T("sc2")], w=[T("sc3")])
            P.tt("dve", sc[:, 16:20], beta[:, j, :], sc[:, 4:8], ALU.mult, r=[("beta", bp), T("sc1")], w=[T("sc4")])
            P.ts("dve", sc[:, 20:24], beta[:, j, :], -1.0, ALU.mult, r=[("beta", bp)], w=[T("sc5")])
            P.act(egl[:], psm[:, 8:16], AF.Exp, r=[psmtok], w=[T("egl")])
            P.bput(ksm)
            yield
            P.tt("dve", dec[:], col3(sc[:, 0:4]), v3(pgc), ALU.subtract, r=[T("sc0"), pgctok], w=[T("A2")])
            P.ts("dve", dec[:], dec[:], 0.0, ALU.min, r=[T("A2")], w=[T("A2")])
            P.act(dec[:], dec[:], AF.Exp, r=[T("A2")], w=[T("A2")])
            P.act(egcb[:], v3(pgc), AF.Exp, r=[pgctok], w=[T("A3")])
            P.bput(kgc)
            yield
            yield
            kkt, pkt, pkttok = P.bget()
            kvt, pvt, pvttok = P.bget()
            for h in range(4):
                P.tr(pkt[:, h * 128:(h + 1) * 128], qkvs[:, 4 + h, ts_], idf, r=[("qkvs", bp, 4 + h), "cst"], w=[pkttok])
            for h in range(4):
                P.tr(pvt[:, h * 128:(h + 1) * 128], qkvs[:, 8 + h, ts_], idf, r=[("qkvs", bp, 8 + h), "cst"], w=[pvttok])
            kkk, pkk, pkktok = P.bget()
            kqk, pqk, pqktok = P.bget()
            for h in range(4):
                P.mm(pkk[:, h * 128:(h + 1) * 128], qkvs[:, 4 + h, ts_], qkvs[:, 4 + h, ts_], True, True, r=[("qkvs", bp, 4 + h)], w=[pkktok])
            for h in range(4):
                P.mm(pqk[:, h * 128:(h + 1) * 128], qkvs[:, h, ts_], qkvs[:, 4 + h, ts_], True, True, r=[("qkvs", bp, h), ("qkvs", bp, 4 + h)], w=[pqktok])
            P.tt("dve", qgT[:], qkvs[:, 0:4, ts_], egcb[:], ALU.mult, r=[("qkvs", bp, ci) for ci in range(4)] + [T("A3")], w=[T("qgT")])
            for h in range(4):
                hs = slice(h * 128, (h + 1) * 128)
                P.act(RHSw[:, h, :], pkt[:, hs], AF.Copy, scale=sc[:, 16 + h:17 + h], r=[pkttok, T("sc4")], w=[T("RHSw")])
                P.act(kd[:, h, :], pkt[:, hs], AF.Copy, scale=sc[:, 12 + h:13 + h], r=[pkttok, T("sc3")], w=[T("kd")])
                P.act(RHSu[:, h, :], pvt[:, hs], AF.Copy, scale=beta[:, j, h:h + 1], r=[pvttok, ("beta", bp)], w=[T("RHSu")])
            P.bput(kkt)
            P.bput(kvt)
            yield
            P.tt("pool", mb[:], SLT3[:], col3(sc[:, 20:24]), ALU.mult, r=["c3", T("sc5")], w=[T("A4")])
            P.tt("dve", tX[:], v3(pkk), dec[:], ALU.mult, r=[pkktok, T("A2"), T("A1")], w=[T("A1")])
            P.bput(kkk)
            yield
            P.tt("dve", tA[:], v3(pqk), dec[:], ALU.mult, r=[pqktok, T("A2"), T("A3"), T("qgT")], w=[T("A3")])
            P.bput(kqk)
            yield
            X0b = bs["XB"]
            Xs = [bs["XA"], bs["XB"]]
            Ys = [bs["YA"], bs["YB"]]
            xs_t = [T("XA"), T("XB")]
            ys_t = [T("YA"), T("YB")]
            Xo1, Xo2, Yo1, Yo2, Pb, Qb = bs["Xo1"], bs["Xo2"], bs["Yo1"], bs["Yo2"], bs["Pb"], bs["Qb"]
            M1b, M2b = bs["XA"], bs["YA"]
            m3 = lambda mk: bc(mk[:].unsqueeze(1), [128, 4, 128])
            P.tt("pool", X0b[:], tX[:], mb[:], ALU.mult, r=[T("A1"), T("A4")], w=[T("XB")])
            P.tt("pool", attb[:], tA[:], LT3[:], ALU.mult, r=[T("A3"), "c3"], w=[T("H4")])
            kb_, pb_, pTtok = P.bget()
            pTb = pb_[:, 0:256].bitcast(BF16)
            for h in range(4):
                P.tr(pTb[:, h * 128:(h + 1) * 128], X0b[:, h, :], idb[:], r=[T("XB"), "idb"], w=[pTtok])
            pT3 = pTb.rearrange("p (h n) -> p h n", h=4)
            P.tt("dve", Ys[0][:], pT3, m3(mk16), ALU.mult, r=[pTtok, "mk"], w=[T("YA")])
            P.tt("dve", Yo1[:], pT3, m3(mk32), ALU.mult, r=[pTtok, "mk"], w=[T("Yo1")])
            P.bput(kb_)
            yield
            P.tt("dve", Xs[0][:], X0b[:], m3(mk16), ALU.mult, r=[T("XB"), "mk"], w=[T("XA")])
            P.tt("pool", Xo1[:], X0b[:], m3(mk32), ALU.mult, r=[T("XB"), "mk"], w=[T("Xo1")])
            P.cp("pool", Xo2[:], X0b[:], r=[T("XB")], w=[T("Xo2")])
            kb2, pb2, phtok = P.bget()
            ph = pb2[:, 0:256].bitcast(BF16)
            for h in range(4):
                P.tr(ph[:, h * 128:(h + 1) * 128], attb[:, h, :], idb[:], r=[T("H4"), "idb"], w=[phtok])
            P.cp("act", attT[:], ph.rearrange("p (h n) -> p h n", h=4), r=[phtok], w=[T("attT")])
            P.bput(kb2)
            yield
            P.tt("pool", Qb[:], I3[:], Xs[0][:], ALU.add, r=["c3", T("XA")], w=[T("Qb")])
            P.tt("dve", Pb[:], I3[:], Ys[0][:], ALU.add, r=["c3", T("YA")], w=[T("Pb")])
            yield
            a = 0
            for rnd in range(1, 5):
                b = 1 - a
                do_sq = rnd <= 3
                do_pr = rnd >= 2
                if do_sq:
                    kY, pY, pYtok = P.bget()
                    for h in range(4):
                        P.mm(pY[:, h * 128:(h + 1) * 128], Xs[a][:, h, :], Ys[a][:, h, :], True, True, r=[xs_t[a], ys_t[a]], w=[pYtok])
                    kX, pX, pXtok = P.bget()
                    for h in range(4):
                        P.mm(pX[:, h * 128:(h + 1) * 128], Ys[a][:, h, :], Xs[a][:, h, :], True, True, r=[xs_t[a], ys_t[a]], w=[pXtok])
                if do_pr:
                    kP, pP, pPtok = P.bget()
                    for h in range(4):
                        P.mm(pP[:, h * 128:(h + 1) * 128], Qb[:, h, :], Ys[a][:, h, :], True, True, r=[T("Qb"), ys_t[a]], w=[pPtok])
                    kQ, pQ, pQtok = P.bget()
                    for h in range(4):
                        P.mm(pQ[:, h * 128:(h + 1) * 128], Pb[:, h, :], Xs[a][:, h, :], True, True, r=[T("Pb"), xs_t[a]], w=[pQtok])
                if do_sq:
                    P.cp("act", Ys[b][:], v3(pY), r=[pYtok], w=[ys_t[b]])
                    P.bput(kY)
                    P.cp("act", Xs[b][:], v3(pX), r=[pXtok], w=[xs_t[b]])
                    P.bput(kX)
                if do_pr:
                    P.tt("dve", Pb[:], Pb[:], v3(pP), ALU.add, r=[T("Pb"), pPtok], w=[T("Pb")])
                    P.bput(kP)
                    P.tt("dve", Qb[:], Qb[:], v3(pQ), ALU.add, r=[T("Qb"), pQtok], w=[T("Qb")])
                    P.bput(kQ)
                a = b
                yield
            for lvl, (Xo, Yo, xo_t, yo_t) in enumerate(((Xo1, Yo1, T("Xo1"), T("Yo1")), (Xo2, Yo2, T("Xo2"), T("Yo2")))):
                last = lvl == 1
                k2, p2, p2tok = P.bget()
                for h in range(4):
                    P.mm(p2[:, h * 128:(h + 1) * 128], Xo[:, h, :], Pb[:, h, :], True, True, r=[xo_t, T("Pb")], w=[p2tok])
                if not last:
                    k1, p1, p1tok = P.bget()
                    for h in range(4):
                        P.mm(p1[:, h * 128:(h + 1) * 128], Yo[:, h, :], Qb[:, h, :], True, True, r=[yo_t, T("Qb")], w=[p1tok])
                if last:
                    P.tt("dve", M2b[:], v3(p2), m3(mk64), ALU.mult, r=[p2tok, "mk"], w=[T("YA")])
                else:
                    P.cp("act", M2b[:], v3(p2), r=[p2tok], w=[T("YA")])
                P.bput(k2)
                if not last:
                    P.cp("act", M1b[:], v3(p1), r=[p1tok], w=[T("XA")])
                    P.bput(k1)
                yield
                kP, pP, pPtok = P.bget()
                for h in range(4):
                    P.mm(pP[:, h * 128:(h + 1) * 128], Qb[:, h, :], M2b[:, h, :], True, True, r=[T("Qb"), T("YA")], w=[pPtok])
                if not last:
                    kQ, pQ, pQtok = P.bget()
                    for h in range(4):
                        P.mm(pQ[:, h * 128:(h + 1) * 128], Pb[:, h, :], M1b[:, h, :], True, True, r=[T("Pb"), T("XA")], w=[pQtok])
                P.tt("dve", Pb[:], Pb[:], v3(pP), ALU.add, r=[T("Pb"), pPtok], w=[T("Pb")])
                P.bput(kP)
                if not last:
                    P.tt("dve", Qb[:], Qb[:], v3(pQ), ALU.add, r=[T("Qb"), pQtok], w=[T("Qb")])
                    P.bput(kQ)
                yield
            kw_, pw, pwtok = P.bget()
            ku_, pu, putok = P.bget()
            for h in range(4):
                P.mm(pw[:, h * 128:(h + 1) * 128], RHSw[:, h, :], Pb[:, h, :], True, True, r=[T("RHSw"), T("Pb")], w=[pwtok])
            for h in range(4):
                P.mm(pu[:, h * 128:(h + 1) * 128], Pb[:, h, :], RHSu[:, h, :], True, True, r=[T("RHSu"), T("Pb")], w=[putok])
            P.cp("act", wT[:], v3(pw), r=[pwtok], w=[T("H4")])
            P.cp("act", uu[:], v3(pu), r=[putok], w=[T("A3")])
            P.bput(kw_)
            P.bput(ku_)

        def scan_gen(j, bs):
            k_ = bs["k"]
            T = lambda n: (n, k_)
            qgT, kd, attT, wT, vnew, uu, egl = bs["qgT"], bs["kd"], bs["attT"], bs["H4"], bs["vnew"], bs["A3"], bs["egl"]
            ko, po, potok = P.bget()
            bs["ko"] = (ko, po, potok)
            for r_ in range(2):
                rows = slice(64 * r_, 64 * r_ + 64)
                c = scan_state["c"]
                So, Sn = Sst[c % 2], Sst[(c + 1) % 2]
                so_t, sn_t = ("S", c % 2), ("S", (c + 1) % 2)
                scan_state["c"] = c + 1
                ka, pa, patok = P.bget()
                for h in range(4):
                    P.mm(pa[rows, h * 128:(h + 1) * 128], wT[:, h, rows], Sbf[:, h, :], True, True, r=[T("H4"), "Sbf"], w=[patok])
                P.tt("dve", vnew[rows], uu[rows], v3(pa)[rows], ALU.subtract, r=[T("A3"), patok], w=[T("vnew")])
                P.bput(ka)
                yield
                for h in range(4):
                    P.mm(po[rows, h * 128:(h + 1) * 128], qgT[:, h, rows], Sbf[:, h, :], True, False, r=[T("qgT"), "Sbf"], w=[potok])
                    P.mm(po[rows, h * 128:(h + 1) * 128], attT[rows, h, rows], vnew[rows, h, :], False, True, r=[T("attT"), T("vnew")], w=[potok])
                ks, ps_, pstok = P.bget()
                for h in range(4):
                    P.mm(ps_[:, h * 128:(h + 1) * 128], kd[rows, h, :], vnew[rows, h, :], True, True, r=[T("kd"), T("vnew")], w=[pstok])
                for h in range(4):
                    egc_ = egl[:, 2 * h + r_:2 * h + r_ + 1]
                    P.stt("dve", Sbf[:, h, :], So[:, h, :], egc_, ps_[:, h * 128:(h + 1) * 128], ALU.mult, ALU.add,
                          r=[so_t, T("egl"), pstok], w=["Sbf"])
                for h in range(4):
                    egc_ = egl[:, 2 * h + r_:2 * h + r_ + 1]
                    P.stt("dve", Sn[:, h, :], So[:, h, :], egc_, ps_[:, h * 128:(h + 1) * 128], ALU.mult, ALU.add,
                          r=[so_t, T("egl"), pstok], w=[sn_t])
                P.bput(ks)
                yield

        def post_gen(j, bs):
            k_ = bs["k"]
            T = lambda n: (n, k_)
            ts_ = slice(j * 128, (j + 1) * 128)
            og, ogb, oss = bs["A4"], bs["H4"], bs["oss"]
            ko, po, potok = bs["ko"]
            for h in range(4):
                P.act(jk[:], po[:, h * 128:(h + 1) * 128], AF.Square, accum_out=oss[:, h:h + 1], r=[potok], w=["jk", T("oss")])
            P.act(oss[:, 4:8], oss[:, 0:4], AF.Ln, scale=1.0 / 128.0, bias=1e-6, r=[T("oss")], w=[T("oss")])
            P.act(oss[:, 4:8], oss[:, 4:8], AF.Exp, scale=-0.5, r=[T("oss")], w=[T("oss")])
            P.tt("dve", og[:], v3(po), col3(oss[:, 4:8]), ALU.mult, r=[potok, T("oss"), T("A4")], w=[T("A4")])
            P.bput(ko)
            yield
            P.tt("pool", ogb[:], og[:], sz2[cur["tb"] % 2][:, j, :].rearrange("p (h n) -> p h n", h=4), ALU.mult,
                 r=[T("A4"), ("sz", cur["tb"] % 2, j), T("H4")], w=[T("H4")])
            kb_, pb_, phtok = P.bget()
            ph = pb_[:, 0:256].bitcast(BF16)
            for h in range(4):
                P.tr(ph[:, h * 128:(h + 1) * 128], ogb[:, h, :], idb[:], r=[T("H4"), "idb"], w=[phtok])
            P.cp("act", mixd[:, :, ts_], ph.rearrange("p (h n) -> p h n", h=4), r=[phtok], w=["mixd"])
            P.bput(kb_)

        def l2norm_chunk(c, ci, bp):
            qkvs = qkvs2[bp]
            i = ci % 2
            k, pt, ptok = P.bget()
            P.mm(pt[:], ones_bf[:], sq8[:, ci, :], True, True, r=["ones_bf", ("sq8", ci)], w=[ptok])
            if c == 0:
                P.act(lnb[i][:], pt[:], AF.Ln, scale=128.0, bias=128.0e-6, r=[ptok], w=[("lnb", i)])
            else:
                P.act(lnb[i][:], pt[:], AF.Ln, bias=1e-6, r=[ptok], w=[("lnb", i)])
            P.bput(k)
            P.act(lnb[i][:], lnb[i][:], AF.Exp, scale=-0.5, r=[("lnb", i)], w=[("lnb", i)])
            P.tt("dve", qkvs[:, ci, :], qkvs[:, ci, :], lnb[i][:], ALU.mult, r=[("qkvs", bp, ci), ("lnb", i)], w=[("qkvs", bp, ci)])

        def block_stage0(tb):
            sz = sz2[tb % 2]
            szp = tb % 2
            bp = tb % 2
            qkvs, beta, gg = qkvs2[bp], beta2[bp], gg2[bp]
            for c in range(3):
                W, wtok = ring.load(WCH["in"][0] + c)
                for m in range(4):
                    ci = c * 4 + m
                    k, pt, ptok = P.bget()
                    for kc in range(8):
                        P.mm(pt[:], W[:, kc, m * 128:(m + 1) * 128], hb[:, kc, :], kc == 0, kc == 7, r=[wtok, "hT"], w=[ptok])
                    P.cp("act", pre4[:, m, 3:515], pt[:], r=[ptok], w=[("pre4", m)])
                    P.bput(k)
                    P.cp("pool", pre4[:, m, 0:3], prec[:, ci, :], r=[("prec", ci)], w=[("pre4", m)])
                    P.ts("dve", qkvs[:, ci, :], pre4[:, m, 0:512], cwt[:, ci, 0:1], ALU.mult, r=[("pre4", m), "cwt"], w=[("qkvs", bp, ci)])
                    for jj in range(1, 4):
                        P.stt("dve", qkvs[:, ci, :], pre4[:, m, jj:jj + 512], cwt[:, ci, jj:jj + 1], qkvs[:, ci, :], ALU.mult, ALU.add,
                              r=[("pre4", m), "cwt", ("qkvs", bp, ci)], w=[("qkvs", bp, ci)])
                    P.cp("pool", prec[:, ci, :], pre4[:, m, 512:515], r=[("pre4", m)], w=[("prec", ci)])
                    P.act(qkvs[:, ci, :], qkvs[:, ci, :], AF.Silu, r=[("qkvs", bp, ci)], w=[("qkvs", bp, ci)])
                    if c < 2:
                        P.act(sq8[:, ci, :], qkvs[:, ci, :], AF.Square, r=[("qkvs", bp, ci)], w=[("sq8", ci)])
                    yield
            W, wtok = ring.load(WCH["in"][0] + 3)
            for j in range(4):
                k, pt, ptok = P.bget()
                for kc in range(8):
                    P.mm(pt[:], hb[:, kc, j * 128:(j + 1) * 128], W[:, kc, :], kc == 0, kc == 7, r=[wtok, "hT"], w=[ptok])
                P.act(sz[:, j, :], pt[:], AF.Silu, r=[ptok], w=[("sz", szp, j)])
                P.bput(k)
                P.tt("pool", sz[:, j, :].rearrange("p (h n) -> p h n", h=4), sz[:, j, :].rearrange("p (h n) -> p h n", h=4), dng4[:],
                     ALU.mult, r=[("sz", szp, j), "dng4"], w=[("sz", szp, j)])
                yield
            W, wtok = ring.load(WCH["in"][0] + 5)
            k, pt, ptok = P.bget()
            for j in range(4):
                for kc in range(8):
                    P.mm(pt[:, j * 8:(j + 1) * 8], hb[:, kc, j * 128:(j + 1) * 128], W[:, kc, 384:392], kc == 0, kc == 7, r=[wtok, "hT"], w=[ptok])
            P.cp("dve", ba[:], pt[:, 0:32].rearrange("p (j n) -> p j n", j=4), r=[ptok], w=["ba"])
            P.bput(k)
            P.act(beta[:], ba[:, :, 0:4], AF.Sigmoid, r=["ba"], w=[("beta", bp)])
            P.tt("dve", gg[:], ba[:, :, 4:8], bc(dtb[:].unsqueeze(1), [128, 4, 4]), ALU.add, r=["ba", "dtb"], w=[("gg", bp)])
            P.act(gg[:], gg[:], AF.Exp, r=[("gg", bp)], w=[("gg", bp)])
            P.act(gg[:], gg[:], AF.Ln, bias=1.0, r=[("gg", bp)], w=[("gg", bp)])
            P.tt("dve", gg[:], gg[:], bc(negA[:].unsqueeze(1), [128, 4, 4]), ALU.mult, r=[("gg", bp), "negA"], w=[("gg", bp)])
            yield
            for ci in range(8):
                l2norm_chunk(ci // 4, ci, bp)
                if ci % 2 == 1:
                    yield

        def bg_ffn_prep():
            stg_ = [P.sb([128, 512], F32) for _ in range(2)]
            stb_ = [P.sb([128, 512], BF16) for _ in range(2)]
            pieces = []
            for nm, wt in (("gate", w_gate), ("up", w_up)):
                for c in range(6):
                    ncol = min(512, DFF - c * 512)
                    for kc in range(8):
                        pieces.append((WCH[nm][0] + c, kc, wt[kc * 128:(kc + 1) * 128, c * 512:c * 512 + ncol], ncol))
            for g in range(3):
                nk = 8 if g < 2 else 6
                for hf in range(2):
                    for kc in range(nk):
                        r0 = g * 1024 + kc * 128
                        pieces.append((WCH["down"][0] + g * 2 + hf, kc, w_down[r0:r0 + 128, hf * 512:(hf + 1) * 512], 512))

            def load(n):
                ch, kc, src, ncol = pieces[n]
                i = n % 2
                if ncol < 512:
                    P.memset("pool", stg_[i][:, ncol:512], 0.0, w=[("bgs", i)])
                P.dma("sp", stg_[i][:, 0:ncol], src, w=[("bgs", i)])

            load(0)
            for n in range(len(pieces)):
                ch, kc, src, ncol = pieces[n]
                i = n % 2
                if n + 1 < len(pieces):
                    load(n + 1)
                P.cp("act", stb_[i][:], stg_[i][:], r=[("bgs", i)], w=[("bgb", i)])
                yield
                P.dma("act", wsc[ch][:, kc * 512:(kc + 1) * 512], stb_[i][:], r=[("bgb", i)], w=[("wscbg", ch, kc)])
                yield

        def front_gen(tb):
            for j in range(4):
                t = tb * 4 + j
                i = t % 2
                P.dma("sp", xt[i][:], x[t * 128:(t + 1) * 128, :], w=[("xt", i)])
                make_hT(P, xt[i][:], ("xt", i), j, s1m, shm, "mod", idb, hb, "hT", tmpf, hbf, eng="pool")
                yield
            P.dma("act", hTs[tb].rearrange("p (k n) -> p k n", k=8), hb[:], r=["hT"], w=[("hTs", tb)])
            if GDN:
                yield from block_stage0(tb)

        def exhaust(g):
            if g is not None:
                for _ in g:
                    pass

        bg = bg_ffn_prep() if BG_FFN else None
        bgs = [bg]

        def bg_step():
            if bgs[0] is not None:
                try:
                    next(bgs[0])
                except StopIteration:
                    bgs[0] = None

        exhaust(front_gen(0))
        for tb in range(NB):
            if not GDN:
                P.dma("pool", odn[tb], zt[:], r=["mixd"], w=[("odn", tb)])
                if tb + 1 < NB:
                    exhaust(front_gen(tb + 1))
                continue
            cur["tb"] = tb
            state = {}
            gens = {}
            set_of = {}
            nyield = {}
            free_sets = list(range(NSET))
            nxt = 0
            scan_next = 0
            done = 0
            fgen = None
            fstarted = False
            while done < 4:
                while nxt < 4 and free_sets:
                    k = free_sets.pop(0)
                    set_of[nxt] = sets[k]
                    gens[nxt] = prep_gen(nxt, sets[k])
                    state[nxt] = "prep"
                    nyield[nxt] = 0
                    nxt += 1
                if scan_next < 4 and state.get(scan_next) == "ready" and not any(v == "scan" for v in state.values()):
                    gens[scan_next] = scan_gen(scan_next, set_of[scan_next])
                    state[scan_next] = "scan"
                if not fstarted and tb + 1 < NB:
                    fgen = front_gen(tb + 1)
                    fstarted = True
                order_ = sorted([j for j in gens if state[j] in ("prep", "scan", "post")], key=lambda j: (state[j] != "scan", j))
                for j in order_:
                    try:
                        next(gens[j])
                        nyield[j] = nyield.get(j, 0) + 1
                    except StopIteration:
                        if state[j] == "prep":
                            state[j] = "ready"
                        elif state[j] == "scan":
                            state[j] = "post"
                            gens[j] = post_gen(j, set_of[j])
                            scan_next += 1
                        elif state[j] == "post":
                            state[j] = "done"
                            free_sets.append(set_of[j]["k"])
                            done += 1
                if fgen is not None:
                    try:
                        next(fgen)
                    except StopIteration:
                        fgen = None
                bg_step()
            P.dma("pool", odn[tb], zt[:], r=["mixd"], w=[("odn", tb)])
            if tb + 1 < NB and not fstarted:
                fgen = front_gen(tb + 1)
            exhaust(fgen)
        exhaust(bgs[0])
        P.finish()

    def build_p1b():
        P = Pass(nc, "p1b")
        P.psum_banks(8)
        cst = P.sb([128, NCONST], F32)
        P.dma("sp", cst[:], consts, w=["cst"])
        ones_b = P.sb([128, 128], BF16)
        utb = P.sb([128, 128], BF16)
        P.cp("dve", ones_b[:], cst[:, C_ONE:C_ONE + 128], r=["cst"], w=["ones_b"])
        P.cp("dve", utb[:], cst[:, C_UT:C_UT + 128], r=["cst"], w=["utb"])
        invf = cst[0:64, C_MISC + 0:C_MISC + 1]
        sgn = cst[0:64, C_MISC + 1:C_MISC + 2]
        gtm = P.sb([128, D], F32)
        g1 = P.sb([128, D], F32)
        b1 = P.sb([128, D], F32)
        P.dma("sp", gtm[:], modsc[:, 2 * D:3 * D], w=["mod"])
        P.dma("act", g1[:], bc(ln1_g[0:1, :], [128, D]), w=["ln"])
        P.dma("act", b1[:], bc(ln1_b[0:1, :], [128, D]), w=["ln"])
        kTc = P.sb([128, 4, SEQ], BF16)
        krTc = P.sb([128, SEQ], BF16)
        Vc = P.sb([128, NT, 512], BF16)
        ring = Ring(P, 2)
        hT = P.sb([128, 8, 512], BF16)
        cqT = P.sb([128, 4, 512], BF16)
        cqsq = P.sb([128, 4, 512], BF16)
        ckvT = P.sb([128, 2, 512], BF16)
        ckvsq = P.sb([128, 2, 512], BF16)
        rq_b = P.sb([128, 512], F32)
        rkv_b = P.sb([128, 512], F32)
        rkv_c = P.sb([128, 4], F32)
        cosT = P.sb([64, 512], F32)
        sinT = P.sb([64, 512], F32)
        posi = P.sb([64, 512], I32)
        ta = P.sb([64, 512], F32)
        tb_ = P.sb([64, 512], F32)
        tc = P.sb([64, 512], F32)
        ki = P.sb([64, 512], I32)
        t1 = P.sb([128, 512], F32)
        t2 = P.sb([128, 512], F32)
        qTn2 = [P.sb([128, 4, 512], BF16) for _ in range(2)]
        qrT2 = [P.sb([128, 4, 512], BF16) for _ in range(2)]
        P.memset("pool", krTc[64:128, :], 0.0, w=["krpad"])
        for q_ in qrT2:
            P.memset("pool", q_[64:128, :, :], 0.0, w=["qrpad"])
        sqa = P.sb([128, 512], BF16)
        sqb = P.sb([64, 512], BF16)
        pT = [P.sb([128, 512], BF16) for _ in range(5)]
        mixT = P.sb([128, 8, 512], BF16)
        xt = [P.sb([128, D], F32) for _ in range(4)]
        pending_ln = []
        tmpy = P.sb([128, D], F32)
        junk = P.sb([128, D], F32)
        st4 = [P.sb([128, 8], F32) for _ in range(4)]
        ot = P.sb([128, D], F32)
        km2 = P.sb([128, 4], F32)
        sm = P.sb([128, 8], F32)
        bias2 = [P.sb([128, 4], F32) for _ in range(2)]
        P.memset("dve", km2[:], 0.0, w=["km2"])
        npt = [0]

        def sumsq_b(dst, pieces, r):
            n = len(pieces)
            for k, (ap, npart) in enumerate(pieces):
                P.mm(dst, ones_b[0:npart, :], ap, k == 0, k == n - 1, r=["ones_b"] + r, w=[dst_tok[0]])

        def mla_prep(tb, par):
            t0 = tb * 512
            qTn = qTn2[par]
            qrT = qrT2[par]
            bias_h = bias2[par]
            P.dma("sp", hT[:], hTs[tb].rearrange("p (k n) -> p k n", k=8), w=["hT"])
            P.dma("act", posi[:], bc(pos[0:1, t0:t0 + 512], [64, 512]), w=["posi"])
            P.cp("dve", ta[:], posi[:], r=["posi"], w=["ta"])
            P.ts("dve", ta[:], ta[:], invf, ALU.mult, r=["ta", "cst"], w=["ta"])
            P.ts("dve", tb_[:], ta[:], 1.0 / (2.0 * PI), ALU.mult, r=["ta"], w=["tb_"])
            P.cp("dve", ki[:], tb_[:], r=["tb_"], w=["ki"])
            P.cp("dve", tb_[:], ki[:], r=["ki"], w=["tb_"])
            P.stt("dve", ta[:], tb_[:], -C1, ta[:], ALU.mult, ALU.add, r=["tb_", "ta"], w=["ta"])
            P.stt("dve", ta[:], tb_[:], -C2, ta[:], ALU.mult, ALU.add, r=["tb_", "ta"], w=["ta"])
            P.ts("dve", tc[:], ta[:], PI / 2.0, ALU.add, r=["ta"], w=["tc"])
            P.ts("dve", tb_[:], tc[:], PI, ALU.is_gt, r=["tc"], w=["tb_"])
            P.stt("dve", tc[:], tb_[:], -2.0 * PI, tc[:], ALU.mult, ALU.add, r=["tb_", "tc"], w=["tc"])
            P.ts("dve", tc[:], tc[:], -PI, ALU.max, s2=PI, op1=ALU.min, r=["tc"], w=["tc"])
            P.ts("dve", ta[:], ta[:], -PI, ALU.max, s2=PI, op1=ALU.min, r=["ta"], w=["ta"])
            P.act(cosT[:], tc[:], AF.Sin, r=["tc"], w=["cosT"])
            P.act(sinT[:], ta[:], AF.Sin, r=["ta"], w=["sinT"])
            P.ts("dve", sinT[:], sinT[:], sgn, ALU.mult, r=["sinT", "cst"], w=["sinT"])
            yield

            def rope_out(dst, pa, patok, pbk, pbtok, extra=None, extok=None, dtok=None):
                P.tt("dve", t1[0:64, :], pa[0:64, :], cosT[:], ALU.mult, r=[patok, "cosT"], w=["t1"])
                P.tt("dve", t2[0:64, :], pbk[0:64, :], sinT[:], ALU.mult, r=[pbtok, "sinT"], w=["t2"])
                if extra is None:
                    P.tt("dve", dst, t1[0:64, :], t2[0:64, :], ALU.add, r=["t1", "t2"], w=[dtok])
                else:
                    P.tt("pool", t1[0:64, :], t1[0:64, :], t2[0:64, :], ALU.add, r=["t1", "t2"], w=["t1"])
                    P.tt("dve", dst, t1[0:64, :], extra, ALU.mult, r=["t1", extok], w=[dtok])

            w4, w4tok = ring.load(WCH["in"][0] + 4)
            for fc in range(4):
                k, pt, ptok = P.bget()
                for kc in range(8):
                    P.mm(pt[:], w4[:, kc, fc * 128:(fc + 1) * 128], hT[:, kc, :], kc == 0, kc == 7, r=[w4tok, "hT"], w=[ptok])
                P.cp("act", cqT[:, fc, :], pt[:], r=[ptok], w=["cqT"])
                P.act(cqsq[:, fc, :], pt[:], AF.Square, r=[ptok], w=["cqsq"])
                P.bput(k)
                yield
            k, pt, ptok = P.bget()
            for fc in range(4):
                P.mm(pt[:], ones_b[:], cqsq[:, fc, :], fc == 0, fc == 3, r=["ones_b", "cqsq"], w=[ptok])
            P.act(t1[:], pt[:], AF.Ln, scale=1.0 / 512.0, bias=1e-6, r=[ptok], w=["t1"])
            P.act(rq_b[:], t1[:], AF.Exp, scale=-0.5, r=["t1"], w=["rq_b"])
            P.bput(k)
            yield
            w5, w5tok = ring.load(WCH["in"][0] + 5)
            for fc in range(2):
                k, pt, ptok = P.bget()
                for kc in range(8):
                    P.mm(pt[:], w5[:, kc, fc * 128:(fc + 1) * 128], hT[:, kc, :], kc == 0, kc == 7, r=[w5tok, "hT"], w=[ptok])
                P.cp("act", ckvT[:, fc, :], pt[:], r=[ptok], w=["ckvT"])
                P.act(ckvsq[:, fc, :], pt[:], AF.Square, r=[ptok], w=["ckvsq"])
                P.bput(k)
                yield
            k, pt, ptok = P.bget()
            for fc in range(2):
                P.mm(pt[:], ones_b[:], ckvsq[:, fc, :], fc == 0, fc == 1, r=["ones_b", "ckvsq"], w=[ptok])
            P.act(t1[:], pt[:], AF.Ln, scale=1.0 / 256.0, bias=1e-6, r=[ptok], w=["t1"])
            P.act(rkv_b[:], t1[:], AF.Exp, scale=-0.5, r=["t1"], w=["rkv_b"])
            P.bput(k)
            yield
            k, pt, ptok = P.bget()
            for j in range(4):
                for fc in range(2):
                    P.mm(pt[:, j:j + 1], ckvsq[:, fc, j * 128:(j + 1) * 128], ones_b[:, 0:1], fc == 0, fc == 1,
                         r=["ones_b", "ckvsq"], w=[ptok])
            P.act(sm[:, 0:4], pt[:, 0:4], AF.Ln, scale=1.0 / 256.0, bias=1e-6, r=[ptok], w=["sm"])
            P.act(rkv_c[:], sm[:, 0:4], AF.Exp, scale=-0.5, r=["sm"], w=["rkv_c"])
            P.bput(k)
            yield
            ka, pa, patok = P.bget()
            kb, pbk, pbtok = P.bget()
            for kc in range(8):
                P.mm(pa[0:64, :], w5[:, kc, 256:320], hT[:, kc, :], kc == 0, kc == 7, r=[w5tok, "hT"], w=[patok])
            for kc in range(8):
                P.mm(pbk[0:64, :], w5[:, kc, 320:384], hT[:, kc, :], kc == 0, kc == 7, r=[w5tok, "hT"], w=[pbtok])
            rope_out(krTc[0:64, t0:t0 + 512], pa, patok, pbk, pbtok, dtok=("krT", tb))
            P.bput(ka)
            P.bput(kb)
            yield
            P.act(sqb[:], krTc[0:64, t0:t0 + 512], AF.Square, r=[("krT", tb)], w=["sqb"])
            wk, wktok = ring.load(WCH["ukv"][0] + 0, nk=2)
            for h in range(4):
                k, pt, ptok = P.bget()
                for fc in range(2):
                    P.mm(pt[:], wk[:, fc, h * 128:(h + 1) * 128], ckvT[:, fc, :], fc == 0, fc == 1, r=[wktok, "ckvT"], w=[ptok])
                P.tt("dve", kTc[:, h, t0:t0 + 512], pt[:], rkv_b[:], ALU.mult, r=[ptok, "rkv_b"], w=[("kT", h, tb)])
                P.bput(k)
                yield
                P.act(sqa[:], kTc[:, h, t0:t0 + 512], AF.Square, r=[("kT", h, tb)], w=["sqa"])
                k, pt, ptok = P.bget()
                P.mm(pt[:], ones_b[:], sqa[:], True, False, r=["ones_b", "sqa"], w=[ptok])
                P.mm(pt[:], ones_b[0:64, :], sqb[:], False, True, r=["ones_b", "sqb"], w=[ptok])
                P.S.op("dve", lambda e, o=sm[:, 5:6], i_=pt[:]: e.reduce_max(out=o, in_=i_, axis=mybir.AxisListType.X), r=[ptok], w=["sm"])
                P.tt("dve", km2[:, h:h + 1], km2[:, h:h + 1], sm[:, 5:6], ALU.max, r=["sm", "km2"], w=["km2"])
                P.bput(k)
                yield
            wv, wvtok = ring.load(WCH["ukv"][0] + 1, nk=2)
            for j in range(4):
                k, pt, ptok = P.bget()
                for fc in range(2):
                    P.mm(pt[:], ckvT[:, fc, j * 128:(j + 1) * 128], wv[:, fc, :], fc == 0, fc == 1, r=[wvtok, "ckvT"], w=[ptok])
                P.act(Vc[:, tb * 4 + j, :], pt[:], AF.Copy, scale=rkv_c[:, j:j + 1], r=[ptok, "rkv_c"], w=[("V", tb * 4 + j)])
                P.bput(k)
                yield
            wq0, wq0tok = ring.load(WCH["uq"][0] + 0, nk=4)
            for h in range(4):
                k, pt, ptok = P.bget()
                for fc in range(4):
                    P.mm(pt[:], wq0[:, fc, h * 128:(h + 1) * 128], cqT[:, fc, :], fc == 0, fc == 3, r=[wq0tok, "cqT"], w=[ptok])
                P.tt("dve", qTn[:, h, :], pt[:], rq_b[:], ALU.mult, r=[ptok, "rq_b"], w=[("qTn", par, h)])
                P.bput(k)
                yield
            wq1, wq1tok = ring.load(WCH["uq"][0] + 1, nk=4)
            for h in range(4):
                ka, pa, patok = P.bget()
                kb, pbk, pbtok = P.bget()
                for fc in range(4):
                    P.mm(pa[0:64, :], wq1[:, fc, h * 64:(h + 1) * 64], cqT[:, fc, :], fc == 0, fc == 3, r=[wq1tok, "cqT"], w=[patok])
                for fc in range(4):
                    P.mm(pbk[0:64, :], wq1[:, fc, 256 + h * 64:256 + (h + 1) * 64], cqT[:, fc, :], fc == 0, fc == 3, r=[wq1tok, "cqT"], w=[pbtok])
                rope_out(qrT[0:64, h, :], pa, patok, pbk, pbtok, extra=rq_b[0:64, :], extok="rq_b", dtok=("qrT", par, h))
                P.bput(ka)
                P.bput(kb)
                yield
            for h in range(4):
                P.act(sqa[:], qTn[:, h, :], AF.Square, r=[("qTn", par, h)], w=["sqa"])
                P.act(sqb[:], qrT[0:64, h, :], AF.Square, r=[("qrT", par, h)], w=["sqb"])
                k, pt, ptok = P.bget()
                P.mm(pt[:], ones_b[:], sqa[:], True, False, r=["ones_b", "sqa"], w=[ptok])
                P.mm(pt[:], ones_b[0:64, :], sqb[:], False, True, r=["ones_b", "sqb"], w=[ptok])
                P.S.op("dve", lambda e, o=sm[:, 6:7], i_=pt[:]: e.reduce_max(out=o, in_=i_, axis=mybir.AxisListType.X), r=[ptok], w=["sm"])
                P.bput(k)
                yield
                P.tt("dve", sm[:, 7:8], sm[:, 6:7], km2[:, h:h + 1], ALU.mult, r=["sm", "km2"], w=["sm"])
                P.act(sm[:, 7:8], sm[:, 7:8], AF.Ln, bias=1e-30, r=["sm"], w=["sm"])
                P.act(sm[:, 7:8], sm[:, 7:8], AF.Exp, scale=0.5, r=["sm"], w=["sm"])
                P.ts("dve", bias_h[:, h:h + 1], sm[:, 7:8], -SCALE, ALU.mult, r=["sm"], w=[("bias", par, h)])
        gprep = mla_prep(0, 0)
        for _ in gprep:
            pass
        for tb in range(NB):
            par = tb % 2
            qTn = qTn2[par]
            qrT = qrT2[par]
            bias_h = bias2[par]
            gnext = [mla_prep(tb + 1, 1 - par) if tb + 1 < NB else None]

            def step_next():
                if gnext[0] is not None:
                    try:
                        next(gnext[0])
                    except StopIteration:
                        gnext[0] = None
            P.dma("act", mixT[:, 0:4, :], odn[tb].rearrange("p (k n) -> p k n", k=4), w=[("mixT", k) for k in range(4)])
            for h in range(4):
                ko, po, potok = P.bget()
                kl, pl, pltok = P.bget()
                nkt = 4 * tb + 4
                order = list(range(4 * tb, nkt)) + list(range(0, 4 * tb))
                LAG = 2
                pend = []

                def pv(item):
                    n__, kt_, q0_, pi__ = item
                    p__ = pT[pi__]
                    first = n__ == 0
                    last = n__ == len(order) - 1
                    P.mm(po[:, q0_:512], Vc[:, kt_, h * 128:(h + 1) * 128], p__[:, q0_:512], first, last, r=[("V", kt_), ("pT", pi__)], w=[potok])
                    P.mm(pl[:, q0_:512], ones_b[:], p__[:, q0_:512], first, last, r=["ones_b", ("pT", pi__)], w=[pltok])

                for n_, kt in enumerate(order):
                    i = kt - 4 * tb
                    q0 = max(i, 0) * 128
                    k, ps_, pstok = P.bget()
                    P.mm(ps_[:, q0:512], kTc[:, h, kt * 128:(kt + 1) * 128], qTn[:, h, q0:512], True, False,
                         r=[("kT", h, kt // 4), ("qTn", par, h)], w=[pstok])
                    P.mm(ps_[:, q0:512], krTc[:, kt * 128:(kt + 1) * 128], qrT[:, h, q0:512], False, True,
                         r=[("krT", kt // 4), ("qrT", par, h), "krpad", "qrpad"], w=[pstok])
                    pi_ = npt[0] % len(pT)
                    npt[0] += 1
                    p_ = pT[pi_]
                    P.act(p_[:, q0:512], ps_[:, q0:512], AF.Exp, scale=SCALE, bias=bias_h[:, h:h + 1], r=[pstok, ("bias", par, h)], w=[("pT", pi_)])
                    P.bput(k)
                    if i >= 0:
                        P.tt("pool", p_[:, q0:q0 + 128], p_[:, q0:q0 + 128], utb[:], ALU.mult, r=[("pT", pi_), "utb"], w=[("pT", pi_)])
                    pend.append((n_, kt, q0, pi_))
                    if len(pend) > LAG:
                        pv(pend.pop(0))
                    step_next()
                while pend:
                    pv(pend.pop(0))
                P.cp("dve", tmpy[:, 0:512], pl[:], r=[pltok], w=["tmpy"])
                P.cp("dve", tmpy[:, 512:1024], po[:], r=[potok], w=["tmpy2"])
                P.bput(ko)
                P.bput(kl)
                P.recip(tmpy[:, 0:512], tmpy[:, 0:512], r=["tmpy"], w=["tmpy"])
                P.tt("dve", mixT[:, 4 + h, :], tmpy[:, 512:1024], tmpy[:, 0:512], ALU.mult, r=["tmpy2", "tmpy"], w=[("mixT", 4 + h)])
                for _ in range(1 if h == 0 else 2):
                    if pending_ln:
                        pending_ln.pop(0)()
            while pending_ln:
                pending_ln.pop(0)()
            while gnext[0] is not None:
                step_next()
            wo = [ring.load(WCH["o"][0] + hf) for hf in range(2)]
            for j in range(4):
                t = tb * 4 + j
                i = j
                P.dma("sp", xt[i][:], x[t * 128:(t + 1) * 128, :], w=[("xt", i)])
                for hf in range(2):
                    k, pt, ptok = P.bget()
                    for kc in range(8):
                        P.mm(pt[:], mixT[:, kc, j * 128:(j + 1) * 128], wo[hf][0][:, kc, :], kc == 0, kc == 7,
                             r=[("mixT", kc), wo[hf][1]], w=[ptok])
                    sl = slice(hf * 512, (hf + 1) * 512)
                    P.tt("dve", tmpy[:, sl], pt[:], gtm[:, sl], ALU.mult, r=[ptok, "mod"], w=["tmpy", "tmpy2"])
                    P.bput(k)
                P.stt("dve", xt[i][:], xt[i][:], ALPHA, tmpy[:], ALU.mult, ALU.add, r=[("xt", i), "tmpy", "tmpy2"], w=[("xt", i)])

                def ln1a(i=i):
                    ln_stats(P, xt[i][:], ("xt", i), junk, st4[i], ("st", i))

                def ln1b(t=t, i=i):
                    ln_apply(P, xt[i][:], ("xt", i), g1, b1, "ln", ot[:], "ot", junk, st4[i], ("st", i))
                    P.dma("pool", x1s[t * 128:(t + 1) * 128, :], ot[:], r=["ot"], w=[("x1s", t)])
                pending_ln += [ln1a, ln1b]
        while pending_ln:
            pending_ln.pop(0)()
        P.finish()

    if "mix" in parts:
        build_p1a()
        build_p1b()

    P = Pass(nc, "p2")
    P.psum_banks(8)
    cst, idb = load_consts(P)
    ring = Ring(P, 3)
    s1f = P.sb([128, D], F32)
    shf = P.sb([128, D], F32)
    gtf = P.sb([128, D], F32)
    g2 = P.sb([128, D], F32)
    b2 = P.sb([128, D], F32)
    P.dma("sp", shf[:], modsc[:, 3 * D:4 * D], w=["mod"])
    P.dma("sp", s1f[:], modsc[:, 4 * D:5 * D], w=["mod"])
    P.dma("sp", gtf[:], modsc[:, 5 * D:6 * D], w=["mod"])
    P.dma("act", g2[:], bc(ln2_g[0:1, :], [128, D]), w=["ln"])
    P.dma("act", b2[:], bc(ln2_b[0:1, :], [128, D]), w=["ln"])
    xblk2 = [P.sb([128, 4, D], F32) for _ in range(2)]
    hT2 = [P.sb([128, 8, 512], BF16) for _ in range(2)]
    aT = P.sb([128, NFC, 512], BF16)
    tmpf = P.sb([128, D], F32)
    tmpg = P.sb([128, 512], F32)
    hbf = P.sb([128, D], BF16)
    sg = [P.sb([128, 512], F32) for _ in range(2)]
    junk = P.sb([128, D], F32)
    st4 = [P.sb([128, 8], F32) for _ in range(4)]
    ot = [P.sb([128, D], F32) for _ in range(2)]

    def p2_prep(tb):
        par = tb % 2
        for j in range(4):
            t = tb * 4 + j
            P.dma("sp", xblk2[par][:, j, :], x1s[t * 128:(t + 1) * 128, :], w=[("xblk", par, j)])
            make_hT(P, xblk2[par][:, j, :], ("xblk", par, j), j, s1f, shf, "mod", idb, hT2[par], ("hT", par), tmpf, hbf)

    p2_prep(0)
    pending_ln = []
    for tb in range(NB):
        par = tb % 2
        xblk = xblk2[par]
        hT = hT2[par]
        hTtok = ("hT", par)
        for c in range(6):
            wg, wgtok = ring.load(WCH["gate"][0] + c, "sp")
            wu, wutok = ring.load(WCH["up"][0] + c, "sp")
            for m in range(4 if c < 5 else 2):
                fc = c * 4 + m
                kg, pg, pgtok = P.bget()
                ku, pu, putok = P.bget()
                for kc in range(8):
                    P.mm(pg[:], wg[:, kc, m * 128:(m + 1) * 128], hT[:, kc, :], kc == 0, kc == 7, r=[wgtok, hTtok], w=[pgtok])
                for kc in range(8):
                    P.mm(pu[:], wu[:, kc, m * 128:(m + 1) * 128], hT[:, kc, :], kc == 0, kc == 7, r=[wutok, hTtok], w=[putok])
                i = fc % 2
                P.act(sg[i][:], pg[:], AF.Silu, r=[pgtok], w=[("sg", i)])
                P.tt("dve", aT[:, fc, :], sg[i][:], pu[:], ALU.mult, r=[("sg", i), putok], w=[("aT", fc)])
                P.bput(kg)
                P.bput(ku)
            for _ in range(1 if c == 0 else 2):
                if pending_ln:
                    pending_ln.pop(0)()
            if c == 4 and tb + 1 < NB:
                assert not pending_ln
                p2_prep(tb + 1)
        for hf in range(2):
            acc = [P.bget() for _ in range(4)]
            for g in range(3):
                nk = 8 if g < 2 else 6
                wd, wdtok = ring.load(WCH["down"][0] + g * 2 + hf, "sp", nk=nk)
                for j in range(4):
                    k, pt, ptok = acc[j]
                    for kk in range(nk):
                        fc = g * 8 + kk
                        P.mm(pt[:], aT[:, fc, j * 128:(j + 1) * 128], wd[:, kk, :], fc == 0, fc == NFC - 1,
                             r=[("aT", fc), wdtok], w=[ptok])
            for j in range(4):
                k, pt, ptok = acc[j]
                sl = slice(hf * 512, (hf + 1) * 512)
                P.tt("dve", tmpg[:], pt[:], gtf[:, sl], ALU.mult, r=[ptok, "mod"], w=["tmpg"])
                P.stt("dve", xblk[:, j, sl], xblk[:, j, sl], ALPHA, tmpg[:], ALU.mult, ALU.add,
                      r=[("xblk", par, j), "tmpg"], w=[("xblk", par, j)])
                P.bput(k)
        def mk_ln(tb=tb, par=par, xblk=xblk):
            fs = []
            for j in range(4):
                t = tb * 4 + j

                def fa(j=j):
                    ln_stats(P, xblk[:, j, :], ("xblk", par, j), junk, st4[j], ("st", j))

                def fb(j=j, t=t):
                    i = t % 2
                    ln_apply(P, xblk[:, j, :], ("xblk", par, j), g2, b2, "ln", ot[i][:], ("ot", i), junk, st4[j], ("st", j))
                    P.dma("pool", out[t * 128:(t + 1) * 128, :], ot[i][:], r=[("ot", i)], w=[("out", t)])
                fs += [fa, fb]
            return fs
        pending_ln = mk_ln()
    while pending_ln:
        pending_ln.pop(0)()
    P.finish()
    return nc


_NC_CACHE = {}


def _prep_inputs(inp, b):
    f = lambda a: np.ascontiguousarray(a, dtype=np.float32)
    conv_w = np.asarray(inp["conv_w"])[0, :, 0, :]
    m = {
        "x": f(inp["x"][b]),
        "cT": f(np.asarray(inp["c"])[b].reshape(8, 128).T),
        "pos": np.ascontiguousarray(np.asarray(inp["positions"])[b][None, :].astype(np.int32)),
        "w_ada": f(inp["w_ada"][0]),
        "b_ada": f(inp["b_ada"][0][None, :]),
        "w_in": f(inp["w_in"][0]),
        "cw": f(conv_w.reshape(4, 12, 128).transpose(2, 1, 0)),
        "a_log": f(inp["a_log"][0][None, :]),
        "dt_bias": f(inp["dt_bias"][0][None, :]),
        "dn_g": f(inp["dn_norm_g"][0][None, :]),
        "qn_g": f(np.asarray(inp["q_norm_g"])[0].reshape(4, 128).T),
        "kvn_g": f(np.asarray(inp["kv_norm_g"])[0].reshape(2, 128).T),
        "w_uq": f(inp["w_uq"][0]),
        "w_ukv": f(inp["w_ukv"][0]),
        "w_o": f(inp["w_o"][0]),
        "ln1_g": f(inp["ln1_g"][0][None, :]),
        "ln1_b": f(inp["ln1_b"][0][None, :]),
        "w_gate": f(inp["w_gate"][0]),
        "w_up": f(inp["w_up"][0]),
        "w_down": f(inp["w_down"][0]),
        "ln2_g": f(inp["ln2_g"][0][None, :]),
        "ln2_b": f(inp["ln2_b"][0][None, :]),
        "consts": host_consts(),
    }
    return m


def kernel(**inputs):
    inp = {k: np.asarray(v) for k, v in inputs.items()}
    if "nc" not in _NC_CACHE:
        _NC_CACHE["nc"] = build_program()
    nc = _NC_CACHE["nc"]
    in_maps = [_prep_inputs(inp, b) for b in range(8)]
    res = run_bass_kernel_spmd(nc, in_maps, core_ids=list(range(8)))
    return np.stack([np.asarray(r["out"]) for r in res.results], axis=0).astype(np.float32)
```
